# Optimizing a Trainium2 kernel written in Bass

```python
import jax, jax.numpy as jnp
from jax import lax
import numpy as np

D_MODEL = 1024
BATCH = 16
SEQ = 256
DEPTH = 1
DEC_BATCH = 2
DEC_SEQ = 2048
PAST_LEN = 256

GRID_W = 64
MLA_HEADS = 8
Q_RANK = 384
KV_RANK = 256
NOPE_DIM = 64
ROPE_DIM = 32
MLA_V_DIM = 64
ROPE_BASE = 10000.0
Q_BLOCK = 128
MLA_SCALE = (NOPE_DIM + ROPE_DIM) ** -0.5
MLSTM_HEADS = 4
MLSTM_DK = 128
MLSTM_DV = 256
MLSTM_CHUNK = 64
FFN_HIDDEN = ((8 * D_MODEL // 3 + 255) // 256) * 256
EPS = 1e-6
IN_SIZES = (Q_RANK, KV_RANK, ROPE_DIM,
            MLSTM_HEADS * MLSTM_DK, MLSTM_HEADS * MLSTM_DK, MLSTM_HEADS * MLSTM_DV,
            4 * MLSTM_HEADS, MLSTM_HEADS * MLSTM_DV, 2 * D_MODEL)
N_IN = Q_RANK + KV_RANK + ROPE_DIM + 2 * MLSTM_HEADS * MLSTM_DK + 2 * MLSTM_HEADS * MLSTM_DV + 4 * MLSTM_HEADS + 2 * D_MODEL

kernel_name = "hybrid_mla_mlstm_diffusion_step"


def rmsnorm(x, g):
    xf = x.astype(jnp.float32)
    y = xf * lax.rsqrt(jnp.mean(xf * xf, axis=-1, keepdims=True) + EPS)
    return (y * g.astype(jnp.float32)).astype(x.dtype)


def adaln(cond, w_mod, b_mod):
    mod = jax.nn.silu(cond) @ w_mod + b_mod
    mod = mod.reshape(cond.shape[0], 1, 6, D_MODEL)
    return [mod[:, :, i] for i in range(6)]


def grid_angles(n_tokens):
    rows = n_tokens // GRID_W
    r, col = jnp.meshgrid(jnp.arange(rows, dtype=jnp.float32), jnp.arange(GRID_W, dtype=jnp.float32), indexing='ij')
    half = ROPE_DIM // 2
    inv = ROPE_BASE ** (-jnp.arange(0, half, 2, dtype=jnp.float32) / half)
    return r.reshape(-1, 1) * inv, col.reshape(-1, 1) * inv


def rope_2d(x, ang_r, ang_c):
    xf = x.astype(jnp.float32)

    def rot(xh, ang):
        x1, x2 = jnp.split(xh, 2, axis=-1)
        cs, sn = jnp.cos(ang), jnp.sin(ang)
        return jnp.concatenate([x1 * cs - x2 * sn, x1 * sn + x2 * cs], axis=-1)

    half = ROPE_DIM // 2
    out = jnp.concatenate([rot(xf[..., :half], ang_r), rot(xf[..., half:], ang_c)], axis=-1)
    return out.astype(x.dtype)


def mla_attend(q_nope, q_pe, k_nope, k_pe, v):
    B, Sq, H, _ = q_nope.shape
    nb = Sq // Q_BLOCK
    qn = jnp.moveaxis(q_nope.reshape(B, nb, Q_BLOCK, H, NOPE_DIM), 1, 0)
    qp = jnp.moveaxis(q_pe.reshape(B, nb, Q_BLOCK, H, ROPE_DIM), 1, 0)

    def block(args):
        qn_b, qp_b = args
        s = jnp.einsum('bqhd,bkhd->bhqk', qn_b, k_nope) + jnp.einsum('bqhr,bkr->bhqk', qp_b, k_pe)
        p = jax.nn.softmax(s.astype(jnp.float32) * MLA_SCALE, axis=-1)
        return jnp.einsum('bhqk,bkhv->bqhv', p.astype(v.dtype), v)

    out = lax.map(block, (qn, qp))
    return jnp.moveaxis(out, 0, 1).reshape(B, Sq, H * MLA_V_DIM)


def mlstm_scan(q, k, v, ig, lf, C0, n0, m0):
    B, S, H, _ = q.shape
    L = MLSTM_CHUNK
    nc = S // L
    f32 = jnp.float32

    def chunks(t):
        t = t.astype(f32).reshape((B, nc, L, H) + t.shape[3:])
        return jnp.moveaxis(jnp.moveaxis(t, 1, 0), 3, 2)

    mask = jnp.tril(jnp.ones((L, L), dtype=bool))

    def step(carry, xs):
        C, n, m = carry
        qc, kc, vc, ic, fc = xs
        b = jnp.cumsum(fc, axis=-1)
        log_d = jnp.where(mask, b[..., :, None] - b[..., None, :] + ic[..., None, :], -jnp.inf)
        inter = b + m[..., None]
        m_row = jnp.maximum(inter, jnp.max(log_d, axis=-1))
        d = jnp.exp(log_d - m_row[..., None])
        w_inter = jnp.exp(inter - m_row)
        s = jnp.einsum('bhjd,bhsd->bhjs', qc, kc) * d
        num = jnp.einsum('bhjs,bhsv->bhjv', s, vc) + w_inter[..., None] * jnp.einsum('bhvd,bhjd->bhjv', C, qc)
        den = jnp.sum(s, axis=-1) + w_inter * jnp.einsum('bhd,bhjd->bhj', n, qc)
        h = num / jnp.maximum(jnp.abs(den), jnp.exp(-m_row))[..., None]
        b_last = b[..., -1]
        log_w = b_last[..., None] - b + ic
        m_new = jnp.maximum(b_last + m, jnp.max(log_w, axis=-1))
        w = jnp.exp(log_w - m_new[..., None])
        decay = jnp.exp(b_last + m - m_new)
        C_new = decay[..., None, None] * C + jnp.einsum('bhs,bhsv,bhsd->bhvd', w, vc, kc)
        n_new = decay[..., None] * n + jnp.einsum('bhs,bhsd->bhd', w, kc)
        return (C_new, n_new, m_new), h

    (C, n, m), h = lax.scan(step, (C0.astype(f32), n0.astype(f32), m0.astype(f32)),
                            (chunks(q), chunks(k), chunks(v), chunks(ig), chunks(lf)))
    h = jnp.moveaxis(jnp.moveaxis(h, 0, 1), 2, 3).reshape(B, S, H, v.shape[-1])
    return h.astype(v.dtype), (C, n, m)


def bidirectional_mlstm(q, k, v, ig_f, lf_f, ig_b, lf_b, C0, n0, m0):
    h_f, (Cf, nf, mf) = mlstm_scan(q, k, v, ig_f, lf_f, C0[:, 0], n0[:, 0], m0[:, 0])
    flip = lambda t: jnp.flip(t, axis=1)
    h_b, (Cb, nb, mb) = mlstm_scan(flip(q), flip(k), flip(v), flip(ig_b), flip(lf_b), C0[:, 1], n0[:, 1], m0[:, 1])
    h = h_f + flip(h_b)
    return h, (jnp.stack([Cf, Cb], axis=1), jnp.stack([nf, nb], axis=1), jnp.stack([mf, mb], axis=1))


def mixer(h, p, ctx, angles, state0):
    B, S, _ = h.shape
    z = h @ p['w_in']
    offsets = np.cumsum(IN_SIZES)[:-1].tolist()
    zq, zkv, zkpe, zmq, zmk, zmv, zgate, zmo, zbr = jnp.split(z, offsets, axis=-1)
    q = (rmsnorm(zq, p['g_q_norm']) @ p['w_uq']).reshape(B, S, MLA_HEADS, NOPE_DIM + ROPE_DIM)
    q_nope, q_pe = q[..., :NOPE_DIM], q[..., NOPE_DIM:]
    ckv = rmsnorm(zkv, p['g_kv_norm'])
    kpe = zkpe
    if angles is None:
        ckv_all, kpe_all = ckv, kpe
    else:
        ang_r, ang_c = angles
        q_pe = rope_2d(q_pe, ang_r[:, None, :], ang_c[:, None, :])
        ckv_all = jnp.concatenate([ckv, ctx[0].astype(ckv.dtype)], axis=1)
        kpe_all = jnp.concatenate([rope_2d(kpe, ang_r, ang_c), ctx[1].astype(kpe.dtype)], axis=1)
    Sk = ckv_all.shape[1]
    kv = (ckv_all @ p['w_ukv']).reshape(B, Sk, MLA_HEADS, NOPE_DIM + MLA_V_DIM)
    att = mla_attend(q_nope, q_pe, kv[..., :NOPE_DIM], kpe_all, kv[..., NOPE_DIM:])
    mq = zmq.reshape(B, S, MLSTM_HEADS, MLSTM_DK)
    mk = zmk.reshape(B, S, MLSTM_HEADS, MLSTM_DK) * (MLSTM_DK ** -0.5)
    mv = zmv.reshape(B, S, MLSTM_HEADS, MLSTM_DV)
    gates = (zgate + p['b_gates']).astype(jnp.float32).reshape(B, S, 4, MLSTM_HEADS)
    ig_f, ig_b = gates[:, :, 0], gates[:, :, 1]
    lf_f, lf_b = jax.nn.log_sigmoid(gates[:, :, 2]), jax.nn.log_sigmoid(gates[:, :, 3])
    hm, new_state = bidirectional_mlstm(mq, mk, mv, ig_f, lf_f, ig_b, lf_b, *state0)
    hm = rmsnorm(hm, p['g_mlstm_norm'].reshape(MLSTM_HEADS, MLSTM_DV)).reshape(B, S, MLSTM_HEADS * MLSTM_DV)
    hm = hm * jax.nn.sigmoid(zmo)
    g_a, g_b = jnp.split(jax.nn.sigmoid(zbr), 2, axis=-1)
    out = (g_a * (att @ p['w_o_mla']) + g_b * (hm @ p['w_o_mlstm'])) @ p['w_out']
    return out, ckv, kpe, new_state


def block(x, cond, p, ctx, angles, state0):
    sh1, sc1, gt1, sh2, sc2, gt2 = adaln(cond, p['w_mod'], p['b_mod'])
    h = rmsnorm(x, p['g_norm_mix']) * (1.0 + sc1) + sh1
    out, ckv, kpe, st = mixer(h, p, ctx, angles, state0)
    x = x + gt1 * out
    h = rmsnorm(x, p['g_norm_ffn']) * (1.0 + sc2) + sh2
    a, u = jnp.split(h @ p['w_ffn_in'], 2, axis=-1)
    x = x + gt2 * ((jax.nn.silu(a) * u) @ p['w_ffn_out'])
    return x, ckv, kpe, st


def setup_inputs(seed: int = 0) -> dict:
    key = jax.random.key(seed)
    ks = jax.random.split(key, 32)
    f32 = jnp.float32
    D = D_MODEL

    def nrm(k, shape, scale=1.0):
        return scale * jax.random.normal(k, shape, f32)

    def gain(k, shape):
        return 1.0 + 0.1 * nrm(k, shape)

    i_bias = 0.1 * nrm(ks[13], (DEPTH, 2 * MLSTM_HEADS))
    f_bias = jnp.tile(jnp.linspace(3.0, 6.0, MLSTM_HEADS), 2)[None, :] + 0.1 * nrm(ks[14], (DEPTH, 2 * MLSTM_HEADS))
    return {
        'x_prompt': nrm(ks[0], (BATCH, SEQ, D)),
        'x_sample': nrm(ks[1], (DEC_BATCH, DEC_SEQ, D)),
        'cache_ckv': nrm(ks[2], (DEC_BATCH, DEPTH, PAST_LEN, KV_RANK)),
        'cache_krope': nrm(ks[3], (DEC_BATCH, DEPTH, PAST_LEN, ROPE_DIM)),
        'state_C': nrm(ks[4], (DEC_BATCH, DEPTH, 2, MLSTM_HEADS, MLSTM_DV, MLSTM_DK), 0.1),
        'state_n': nrm(ks[5], (DEC_BATCH, DEPTH, 2, MLSTM_HEADS, MLSTM_DK), 0.1),
        'state_m': nrm(ks[6], (DEC_BATCH, DEPTH, 2, MLSTM_HEADS), 0.5),
        'c': nrm(ks[7], (DEC_BATCH, D)),
        'c_ctx': nrm(ks[8], (D,)),
        'w_mod': nrm(ks[9], (DEPTH, D, 6 * D), 0.5 * D ** -0.5),
        'b_mod': nrm(ks[10], (DEPTH, 6 * D), 0.02),
        'g_norm_mix': gain(ks[11], (DEPTH, D)),
        'w_in': nrm(ks[12], (DEPTH, D, N_IN), D ** -0.5),
        'b_gates': jnp.concatenate([i_bias, f_bias], axis=-1),
        'g_q_norm': gain(ks[15], (DEPTH, Q_RANK)),
        'w_uq': nrm(ks[16], (DEPTH, Q_RANK, MLA_HEADS * (NOPE_DIM + ROPE_DIM)), Q_RANK ** -0.5),
        'g_kv_norm': gain(ks[17], (DEPTH, KV_RANK)),
        'w_ukv': nrm(ks[18], (DEPTH, KV_RANK, MLA_HEADS * (NOPE_DIM + MLA_V_DIM)), KV_RANK ** -0.5),
        'g_mlstm_norm': gain(ks[19], (DEPTH, MLSTM_HEADS * MLSTM_DV)),
        'w_o_mla': nrm(ks[20], (DEPTH, MLA_HEADS * MLA_V_DIM, D), (MLA_HEADS * MLA_V_DIM) ** -0.5),
        'w_o_mlstm': nrm(ks[21], (DEPTH, MLSTM_HEADS * MLSTM_DV, D), (MLSTM_HEADS * MLSTM_DV) ** -0.5),
        'w_out': nrm(ks[22], (DEPTH, D, D), D ** -0.5),
        'g_norm_ffn': gain(ks[23], (DEPTH, D)),
        'w_ffn_in': nrm(ks[24], (DEPTH, D, 2 * FFN_HIDDEN), D ** -0.5),
        'w_ffn_out': nrm(ks[25], (DEPTH, FFN_HIDDEN, D), FFN_HIDDEN ** -0.5),
        'g_final': gain(ks[26], (D,)),
    }


def reference(x_prompt, x_sample, cache_ckv, cache_krope, state_C, state_n, state_m, c, c_ctx,
              w_mod, b_mod, g_norm_mix, w_in, b_gates, g_q_norm, w_uq, g_kv_norm, w_ukv,
              g_mlstm_norm, w_o_mla, w_o_mlstm, w_out, g_norm_ffn, w_ffn_in, w_ffn_out, g_final):
    def layer_params(l):
        return {'w_mod': w_mod[l], 'b_mod': b_mod[l], 'g_norm_mix': g_norm_mix[l], 'w_in': w_in[l],
                'b_gates': b_gates[l], 'g_q_norm': g_q_norm[l], 'w_uq': w_uq[l], 'g_kv_norm': g_kv_norm[l],
                'w_ukv': w_ukv[l], 'g_mlstm_norm': g_mlstm_norm[l], 'w_o_mla': w_o_mla[l],
                'w_o_mlstm': w_o_mlstm[l], 'w_out': w_out[l], 'g_norm_ffn': g_norm_ffn[l],
                'w_ffn_in': w_ffn_in[l], 'w_ffn_out': w_ffn_out[l]}

    bp = x_prompt.shape[0]
    C0 = jnp.zeros((bp, 2, MLSTM_HEADS, MLSTM_DV, MLSTM_DK), jnp.float32)
    n0 = jnp.zeros((bp, 2, MLSTM_HEADS, MLSTM_DK), jnp.float32)
    m0 = jnp.zeros((bp, 2, MLSTM_HEADS), jnp.float32)
    xp = x_prompt
    ckv_l, kpe_l, C_l, n_l, m_l = [], [], [], [], []
    for l in range(DEPTH):
        xp, ckv, kpe, (Cc, nc_, mc) = block(xp, c_ctx[None, :], layer_params(l), None, None, (C0, n0, m0))
        ckv_l.append(ckv); kpe_l.append(kpe); C_l.append(Cc); n_l.append(nc_); m_l.append(mc)
    y_prompt = rmsnorm(xp, g_final)
    new_ckv = jnp.stack(ckv_l, axis=1)
    new_krope = jnp.stack(kpe_l, axis=1)
    new_C = jnp.stack(C_l, axis=1)
    new_n = jnp.stack(n_l, axis=1)
    new_m = jnp.stack(m_l, axis=1)

    angles = grid_angles(x_sample.shape[1])
    xs = x_sample
    for l in range(DEPTH):
        xs, _, _, _ = block(xs, c, layer_params(l), (cache_ckv[:, l], cache_krope[:, l]), angles,
                            (state_C[:, l], state_n[:, l], state_m[:, l]))
    y_sample = rmsnorm(xs, g_final)
    return (y_prompt, y_sample, new_ckv, new_krope, new_C, new_n, new_m)
```

```python
import contextlib
import numpy as np
import concourse.bass as bass
import concourse.mybir as mybir
from concourse.bass_utils import run_bass_kernel_spmd

F32 = mybir.dt.float32
BF16 = mybir.dt.bfloat16
ALU = mybir.AluOpType
AF = mybir.ActivationFunctionType
AX = mybir.AxisListType

N_DMA_SEMS = 8
COMPUTE = ('pe', 'act', 'dve', 'pool')

D = 1024
KC = 8
FFN = 2816
EPS = 1e-6
MLA_SCALE = 96 ** -0.5
O_Q, O_KV, O_KPE, O_MQ, O_MK, O_MV, O_G, O_MO, O_BR = 0, 384, 640, 672, 1184, 1696, 2720, 2736, 3760
NKEY = 2816


class _Op:
    __slots__ = ('eng', 'fn', 'reads', 'writes', 'dma', 'idx', 'deps', 'signal',
                 'sem', 'sigval', 'prev_same_sem')

    def __init__(self, eng, fn, reads, writes, dma):
        self.eng = eng
        self.fn = fn
        self.reads = reads
        self.writes = writes
        self.dma = dma
        self.deps = set()
        self.signal = False
        self.sem = None
        self.sigval = 0
        self.prev_same_sem = None


def _tile_of(k):
    return k[0] if isinstance(k, tuple) else k


class Prog:
    def __init__(self):
        self.ops = []
        self.last_write = {}
        self.readers = {}
        self.inherit = {}
        self.keys_of = {}
        self.phase = 'init'
        self.filler = None
        self.fill_every = 2
        self._fill_cnt = 0
        self._in_fill = False
        self.op_phase = []
        self.ins_phase = {}

    EXPAND = ('hTo', 'hTs')

    def _expand(self, keys):
        out = []
        for k in keys:
            if isinstance(k, tuple) and len(k) == 2 and k[0] in self.EXPAND:
                out.extend((k[0], k[1], ch) for ch in range(8))
            else:
                out.append(k)
        return tuple(out)

    def add(self, eng, fn, reads=(), writes=(), dma=False):
        op = _Op(eng, fn, self._expand(reads), self._expand(writes), dma)
        op.idx = len(self.ops)
        deps = set()
        for k in op.reads + op.writes:
            t = _tile_of(k)
            ks = self.keys_of.setdefault(t, set())
            if k not in ks:
                ks.add(k)
                inh = self.inherit.get(t)
                if inh:
                    deps |= inh
        for k in op.reads:
            w = self.last_write.get(k)
            if w is not None:
                deps.add(w)
        for k in op.writes:
            w = self.last_write.get(k)
            if w is not None:
                deps.add(w)
            for r in self.readers.get(k, {}).values():
                deps.add(r)
        deps.discard(op.idx)
        op.deps = deps
        rkey = ('dma', op.idx) if dma else eng
        for k in op.reads:
            self.readers.setdefault(k, {})[rkey] = op.idx
        for k in op.writes:
            self.last_write[k] = op.idx
            self.readers[k] = {}
        self.ops.append(op)
        self.op_phase.append(self.phase)
        if self.filler and not self._in_fill:
            self._fill_cnt += 1
            if self._fill_cnt % self.fill_every == 0:
                self._in_fill = True
                try:
                    self.filler.pop(0)()
                finally:
                    self._in_fill = False
        return op

    def accessors(self, tile):
        s = set()
        for k in self.keys_of.get(tile, ()):
            w = self.last_write.get(k)
            if w is not None:
                s.add(w)
            s.update(self.readers.get(k, {}).values())
        return s

    def pe(self, fn, reads=(), writes=()):
        return self.add('pe', fn, reads, writes)

    def act(self, fn, reads=(), writes=()):
        return self.add('act', fn, reads, writes)

    def dve(self, fn, reads=(), writes=()):
        return self.add('dve', fn, reads, writes)

    def pool(self, fn, reads=(), writes=()):
        return self.add('pool', fn, reads, writes)

    def dma(self, q, out, in_, reads=(), writes=(), **kw):
        return self.add(q, lambda e: e.dma_start(out=out, in_=in_, **kw), reads, writes, dma=True)

    def emit(self, nc, final_keys=()):
        ops = self.ops
        self.add('sp', None, reads=tuple(final_keys), writes=())
        for op in ops:
            for d in op.deps:
                p = ops[d]
                if p.eng == 'pe' and op.eng == 'pe' and not p.dma and not op.dma:
                    continue
                p.signal = True
        with contextlib.ExitStack() as st:
            csem = {e: st.enter_context(nc.semaphore('s_' + e)) for e in COMPUTE}
            dsem = {q: [st.enter_context(nc.semaphore('d_%s%d' % (q, i))) for i in range(N_DMA_SEMS)]
                    for q in ('sp', 'act', 'pool')}
            ccount = {e: 0 for e in COMPUTE}
            dcount = {q: [0] * N_DMA_SEMS for q in dsem}
            dlast = {q: [None] * N_DMA_SEMS for q in dsem}
            drr = {q: 0 for q in dsem}
            for op in ops:
                if op.dma:
                    q = op.eng
                    i = drr[q] % N_DMA_SEMS
                    drr[q] += 1
                    op.sem = dsem[q][i]
                    op.prev_same_sem = dlast[q][i]
                    dcount[q][i] += 16
                    op.sigval = dcount[q][i]
                    dlast[q][i] = op.idx
                    op.signal = True
                elif op.signal:
                    ccount[op.eng] += 1
                    op.sem = csem[op.eng]
                    op.sigval = ccount[op.eng]
            assert max(ccount.values()) < 60000, ccount
            by_eng = {e: [] for e in ('pe', 'act', 'dve', 'pool', 'sp')}
            for op in ops:
                by_eng[op.eng].append(op)

            def run(engname, e):
                known = {}
                for op in by_eng[engname]:
                    need = {}
                    deps = op.deps
                    if op.dma and op.prev_same_sem is not None:
                        deps = set(deps)
                        deps.add(op.prev_same_sem)
                    for d in deps:
                        p = ops[d]
                        if (not p.dma) and (not op.dma) and p.eng == 'pe' and op.eng == 'pe':
                            continue
                        key = id(p.sem)
                        if known.get(key, 0) >= p.sigval:
                            continue
                        if key not in need or need[key][1] < p.sigval:
                            need[key] = (p.sem, p.sigval)
                    for key, (sem, val) in need.items():
                        e.wait_ge(sem, val)
                        known[key] = val
                    if op.fn is None:
                        continue
                    ins = op.fn(e)
                    try:
                        self.ins_phase[ins.ins.name] = self.op_phase[op.idx]
                    except Exception:
                        pass
                    if op.signal:
                        ins.then_inc(op.sem, 16 if op.dma else 1)

            with nc.Block() as block:
                @block.tensor
                def _(e):
                    run('pe', e)

                @block.scalar
                def _(e):
                    run('act', e)

                @block.vector
                def _(e):
                    run('dve', e)

                @block.gpsimd
                def _(e):
                    run('pool', e)

                @block.sync
                def _(e):
                    run('sp', e)
        return ccount


class Arena:
    def __init__(self, P, ap_bf16, nbytes):
        self.P = P
        self.base = ap_bf16
        self.nbytes = nbytes
        self.off = 0
        self.live = []
        self.freed = []
        self.peak = 0
        self.names = set()
        self.fixed = {}

    def free_fixed(self, name):
        s, e = self.fixed.pop(name)
        deps = self.P.accessors(name) | self.P.inherit.get(name, set())
        self.freed.append((s, e, deps))

    def off_of(self, name):
        for (n, s, e) in self.live:
            if n == name:
                return s
        raise KeyError(name)

    def open_hole(self, start, end):
        keep = []
        for (name, s, e) in self.live:
            if s >= start and e <= end:
                deps = self.P.accessors(name) | self.P.inherit.get(name, set())
                self.freed.append((s, e, deps))
            else:
                keep.append((name, s, e))
        self.live = keep
        self.hole_off = start
        self.hole_end = end

    def alloc(self, name, shape, dtype, hole=False, at=None):
        assert name not in self.names, name
        self.names.add(name)
        esz = 4 if dtype == F32 else 2
        n = 1
        for s in shape[1:]:
            n *= s
        nb = (n * esz + 63) // 64 * 64
        if at is not None:
            start = at
            end = start + nb
            assert end <= self.nbytes and start >= self.off, (name, start, end, self.off)
            self.fixed[name] = (start, end)
        elif hole:
            start = self.hole_off
            end = start + nb
            assert end <= self.hole_end, (name, end, self.hole_end)
            self.hole_off = end
        else:
            start = self.off
            end = start + nb
            assert end <= self.nbytes, (name, end, self.nbytes)
            for fn_, (fs_, fe_) in self.fixed.items():
                assert end <= fs_ or start >= fe_, ('stack alloc overlaps fixed tile', name, fn_, start, end, fs_, fe_)
            self.off = end
            self.peak = max(self.peak, end)
        v = self.base[0:shape[0], start // 2:(start + n * esz) // 2]
        if dtype == F32:
            v = v.bitcast(F32)
        if len(shape) > 2:
            letters = 'abcdefg'[:len(shape) - 1]
            pat = 'p (' + ' '.join(letters) + ') -> p ' + ' '.join(letters)
            kw = {l: s for l, s in zip(letters, shape[1:])}
            v = v.rearrange(pat, **kw)
        inh = set()
        keep = []
        for (s, e, deps) in self.freed:
            if s < end and start < e:
                inh |= deps
            keep.append((s, e, deps))
        if inh:
            self.P.inherit[name] = inh
        if not hole and at is None:
            self.live.append((name, start, end))
        return v

    def mark(self):
        return (self.off, len(self.live))

    def report(self, tag):
        print('  arena', tag, 'off', self.off, 'peak', self.peak)

    def release(self, mark):
        off, nlive = mark
        for (name, s, e) in self.live[nlive:]:
            deps = self.P.accessors(name) | self.P.inherit.get(name, set())
            self.freed = [(a, b, d) for (a, b, d) in self.freed if not (a >= s and b <= e)]
            self.freed.append((s, e, deps))
        self.live = self.live[:nlive]
        self.off = off


IN_SPECS = [
    ('xo', [1024, 1024]), ('xs', [1536, 1024]), ('condT', [128, 16]), ('b_modT', [128, 48]),
    ('gmixT', [128, 8]), ('gffnT', [128, 8]), ('gqT', [128, 3]), ('gmlT', [128, 8]),
    ('gkv', [1, 256]), ('gfin', [1, 1024]), ('bgate', [16, 1]),
    ('cckv', [256, 256]), ('ckro', [256, 32]), ('stC', [8, 256, 128]), ('stnT', [128, 8]),
    ('stm4', [4, 2]), ('sel', [128, 4]),
    ('cos_s', [32, 1536]), ('sin_s', [32, 1536]), ('dmask', [4, 1536]), ('dsl', [4, 12]), ('cos_o', [32, 512]), ('sin_o', [32, 512]),
    ('w_mod', [1024, 6144]), ('w_in', [1024, 5808]), ('w_uq', [384, 768]), ('w_ukv', [256, 1024]),
    ('w_o_mla', [512, 1024]), ('w_o_mlstm', [1024, 1024]), ('w_out', [1024, 1024]),
    ('w_ffn_in', [1024, 5632]), ('w_ffn_out', [2816, 1024]),
]
OUT_SPECS = [
    ('y', [1024, 1024]), ('ockv', [512, 256]), ('okr', [512, 32]),
    ('oC', [16, 256, 128]), ('on', [16, 128]), ('om', [4, 4]),
]


def build_program(debug=None, stop_after=None):
    debug = debug or {}
    nc = bass.Bass("TRN2", target_bir_lowering=False)
    P = Prog()
    I = {n: nc.dram_tensor(n, s, F32, kind="ExternalInput").ap() for n, s in IN_SPECS}
    O = {n: nc.dram_tensor(n, s, F32, kind="ExternalOutput").ap() for n, s in OUT_SPECS}
    DBG = {n: nc.dram_tensor('dbg_' + n, list(s), F32, kind="ExternalOutput").ap() for n, s in debug.items()}
    final_keys = []

    with contextlib.ExitStack() as st:
        ARENA_BYTES = 207 * 1024
        arena_t = st.enter_context(nc.sbuf_tensor('arena', [128, ARENA_BYTES // 2], BF16))
        A = Arena(P, arena_t[:, :], ARENA_BYTES)
        banks = [st.enter_context(nc.psum_tensor('bank%d' % i, [128, 512], F32)) for i in range(8)]
        BK = [b[:, :] for b in banks]
        BKb = [b[:, :].bitcast(BF16) for b in banks]

        def bkey(i):
            return ('B', i)

        def mm(out, lhsT, rhs, start, stop, reads, writes):
            P.pe(lambda e: e.matmul(out, lhsT=lhsT, rhs=rhs, start=start, stop=stop), reads, writes)

        def tr(out, in_, ident, reads, writes):
            P.pe(lambda e: e.transpose(out=out, in_=in_, identity=ident), reads, writes)

        def act(out, in_, func, reads, writes, bias=None, scale=None, accum=None):
            kw = {}
            if bias is not None:
                kw['bias'] = bias
            if scale is not None:
                kw['scale'] = scale
            if accum is not None:
                kw['accum_out'] = accum
            P.act(lambda e: e.activation(out=out, in_=in_, func=func, **kw), reads, writes)

        def ts(eng, out, in0, s1, s2, op0, op1, reads, writes):
            if op1 is None:
                P.add(eng, lambda e: e.tensor_scalar(out=out, in0=in0, scalar1=s1, scalar2=None, op0=op0),
                      reads, writes)
            else:
                P.add(eng, lambda e: e.tensor_scalar(out=out, in0=in0, scalar1=s1, scalar2=s2, op0=op0, op1=op1),
                      reads, writes)

        def tt(eng, out, in0, in1, op, reads, writes):
            P.add(eng, lambda e: e.tensor_tensor(out=out, in0=in0, in1=in1, op=op), reads, writes)

        def stt(out, in0, scalar, in1, op0, op1, reads, writes):
            P.dve(lambda e: e.scalar_tensor_tensor(out=out, in0=in0, scalar=scalar, in1=in1, op0=op0, op1=op1),
                  reads, writes)

        def cp(eng, out, in_, reads, writes):
            if eng == 'act':
                P.act(lambda e: e.copy(out=out, in_=in_), reads, writes)
            else:
                P.add(eng, lambda e: e.tensor_copy(out=out, in_=in_), reads, writes)

        def memset(eng, ap, val, writes):
            P.add(eng, lambda e: e.memset(ap, val), (), writes)

        def recip(out, in_, reads, writes):
            P.dve(lambda e: e.reciprocal(out=out, in_=in_), reads, writes)

        def scan(out, d0, d1, initial, op0, op1, reads, writes):
            P.dve(lambda e: e.tensor_tensor_scan(out=out, data0=d0, data1=d1, initial=initial, op0=op0, op1=op1),
                  reads, writes)

        def dbg(name, ap, reads):
            if name in DBG:
                P.dma('sp', DBG[name], ap, reads=reads, writes=[('dbg', name)])
                final_keys.append(('dbg', name))

        def finish():
            cc = P.emit(nc, final_keys=final_keys)
            print('arena peak', A.peak, 'sig counts', cc, 'nops', len(P.ops))
            nc._phase_map = P.ins_phase
            return nc

        ones_f = A.alloc('ones_f', [128, 128], F32)
        ident_f = A.alloc('ident_f', [128, 128], F32)
        ident_b = A.alloc('ident_b', [128, 128], BF16)
        ones_b = A.alloc('ones_b', [128, 128], BF16)
        memset('pool', ones_f, 1.0, ['ones_f'])
        memset('pool', ones_b, 1.0, ['ones_b'])
        P.pool(lambda e: e.affine_select(out=ident_f, in_=ones_f, pattern=[[1, 128]], compare_op=ALU.is_equal,
                                         fill=0.0, base=0, channel_multiplier=-1), ['ones_f'], ['ident_f'])
        P.pool(lambda e: e.affine_select(out=ident_b, in_=ones_f, pattern=[[1, 128]], compare_op=ALU.is_equal,
                                         fill=0.0, base=0, channel_multiplier=-1), ['ones_f'], ['ident_b'])

        def load_small(name, shape, src, q='sp'):
            t = A.alloc(name, shape, F32)
            P.dma(q, t, src, writes=[name])
            return t

        condT = load_small('condT', [128, 16], I['condT'])
        b_modT = load_small('b_modT', [128, 48], I['b_modT'])
        gmixT = load_small('gmixT', [128, 8], I['gmixT'])
        gffnT = load_small('gffnT', [128, 8], I['gffnT'])
        gqT = load_small('gqT', [128, 3], I['gqT'])
        gmlT = load_small('gmlT', [128, 8], I['gmlT'])
        gkv_bc = load_small('gkv_bc', [128, 256], I['gkv'].partition_broadcast(128))
        bgate = load_small('bgate', [16, 1], I['bgate'])
        sel = load_small('sel', [128, 4], I['sel'])

        P.phase = 'P0_adaln'
        scT = A.alloc('scT', [128, 8, 2], BF16)
        act(scT, condT.rearrange('p (a b) -> p a b', b=2), AF.Silu, ['condT'], ['scT'])
        modT = A.alloc('modT', [128, 48, 2], F32)
        NWM = 6
        wm = [A.alloc('wm%d' % i, [128, 8, 512], BF16, at=154 * 1024 + i * 8192) for i in range(NWM)]
        w_mod_v = I['w_mod'].rearrange('(kc p) n -> p kc n', p=128)
        NG = 12

        def load_wm(g):
            P.dma('pool', wm[g % NWM], w_mod_v[:, :, g * 512:(g + 1) * 512], writes=[('wm%d' % (g % NWM),)])
        for g_ in range(NWM):
            load_wm(g_)
        pm = BK[0][:, 0:96].rearrange('p (a b) -> p a b', b=2)

        def mod_slab(g):
            for cc in range(4):
                i = g * 4 + cc
                for kc in range(KC):
                    mm(pm[:, i, :], wm[g % NWM][:, kc, cc * 128:(cc + 1) * 128], scT[:, kc, :], kc == 0, kc == KC - 1,
                       [('wm%d' % (g % NWM),), 'scT'], [bkey(0)])
            if g + NWM < NG:
                load_wm(g + NWM)
        for g in range(4):
            mod_slab(g)
        tt('dve', modT[:, 0:16, :], pm[:, 0:16, :], b_modT[:, 0:16].unsqueeze(2).to_broadcast([128, 16, 2]), ALU.add,
           [bkey(0), 'b_modT'], [('modT', 0)])
        A1 = A.alloc('A1', [128, 8, 2], F32)
        A2 = A.alloc('A2', [128, 8, 2], F32)
        ts('dve', A1, modT[:, 8:16, :], 1.0, None, ALU.add, None, [('modT', 0)], ['A1'])
        tt('dve', A1, A1, gmixT.unsqueeze(2).to_broadcast([128, 8, 2]), ALU.mult, ['A1', 'gmixT'], ['A1'])

        def finish_mod():
            tt('dve', modT[:, 16:48, :], pm[:, 16:48, :], b_modT[:, 16:48].unsqueeze(2).to_broadcast([128, 32, 2]), ALU.add,
               [bkey(0), 'b_modT'], [('modT', 1)])
            ts('dve', A2, modT[:, 32:40, :], 1.0, None, ALU.add, None, [('modT', 1)], ['A2'])
            tt('dve', A2, A2, gffnT.unsqueeze(2).to_broadcast([128, 8, 2]), ALU.mult, ['A2', 'gffnT'], ['A2'])
            for i_ in range(NWM):
                A.free_fixed('wm%d' % i_)
        B1 = modT[:, 0:8, :]
        B2 = modT[:, 24:32, :]
        def build_gt(gi, idx, banks2, hole=False):
            res = {}
            dg = [A.alloc('dg%d_%d' % (gi, i), [128, 128], F32, hole=hole) for i in range(2)]
            n_dg = 0
            bsel = 0
            for c in range(2):
                name = 'gt%d_%d' % (gi, c)
                t = A.alloc(name, [128, 1024], F32, hole=hole)
                res[c] = t
                for half in range(2):
                    bk = banks2[bsel % 2]
                    bsel += 1
                    for q4 in range(4):
                        ch = half * 4 + q4
                        d_ = dg[n_dg % 2]
                        dk = 'dg%d_%d' % (gi, n_dg % 2)
                        n_dg += 1
                        ts('dve', d_, ident_f, modT[:, idx * 8 + ch, c:c + 1], None, ALU.mult, None,
                           ['ident_f', ('modT', 1)], [dk])
                        mm(BK[bk][:, q4 * 128:(q4 + 1) * 128], ones_f, d_, True, True, ['ones_f', dk], [bkey(bk)])
                    cp('act', t[:, half * 512:(half + 1) * 512], BK[bk], [bkey(bk)], [(name, half)])
            return res

        nrm_xn = [A.alloc('nrm_xn%d' % i, [128, 1024], BF16) for i in range(2)]
        nrm_ss = [A.alloc('nrm_ss%d' % i, [128, 4], F32) for i in range(2)]
        nrm_cnt = [0]
        tr_bank = [0]

        def rstd_from_ss(ss, n, key):
            act(ss[:, 1:2], ss[:, 0:1], AF.Sqrt, [(key, 0)], [(key, 1)], bias=None, scale=1.0 / n)
            return

        eps_t = A.alloc('eps_t', [128, 1], F32)
        memset('pool', eps_t, EPS, ['eps_t'])
        zero_t = A.alloc('zero_t', [128, 1], F32)
        memset('pool', zero_t, 0.0, ['zero_t'])

        def norm_A(xt, xkeys):
            i = nrm_cnt[0] % 2
            nrm_cnt[0] += 1
            ss = nrm_ss[i]
            sk = 'nrm_ss%d' % i
            xn = nrm_xn[i]
            xk = 'nrm_xn%d' % i
            act(xn, xt, AF.Square, xkeys, [xk, (sk, 0)], accum=ss[:, 0:1])
            act(ss[:, 1:2], ss[:, 0:1], AF.Sqrt, [(sk, 0), 'eps_t'], [(sk, 1)], bias=eps_t, scale=1.0 / D)
            recip(ss[:, 2:3], ss[:, 1:2], [(sk, 1)], [(sk, 2)])
            ts('dve', xn, xt, ss[:, 2:3], None, ALU.mult, None, xkeys + [(sk, 2)], [xk])
            return xn, xk

        def norm_B(xn, xk, Aa, Bb, c, dst, dkey, AaKey, BbKey, banks_rr):
            bA, bB = banks_rr[tr_bank[0] % len(banks_rr)]
            tr_bank[0] += 1
            pA = BKb[bA].rearrange('p (a b) -> p a b', b=128)
            pB = BKb[bB].rearrange('p (a b) -> p a b', b=128)
            for ch in range(KC):
                pv, bk = (pA, bA) if ch < 4 else (pB, bB)
                tr(pv[:, ch % 4, :], xn[:, ch * 128:(ch + 1) * 128], ident_b, [xk, 'ident_b'], [bkey(bk)])
            for q in range(4):
                for ch in (q, 4 + q):
                    dk_ = (dkey[0], dkey[1], ch)
                    if ch < 4:
                        act(dst[:, ch, :], pA[:, ch, :], AF.Identity, [bkey(bA), AaKey, BbKey], [dk_],
                            bias=Bb[:, ch, c:c + 1], scale=Aa[:, ch, c:c + 1])
                    else:
                        ts('dve', dst[:, ch, :], pB[:, ch - 4, :], Aa[:, ch, c:c + 1], Bb[:, ch, c:c + 1], ALU.mult, ALU.add,
                           [bkey(bB), AaKey, BbKey], [dk_])

        def norm_pipeline(items, between=None):
            pend = None
            for i, (xf, argsB) in enumerate(items):
                xt, xkeys = xf()
                cur = norm_A(xt, xkeys)
                if pend is not None:
                    norm_B(*pend)
                    if between is not None:
                        between(i - 1)
                pend = cur + argsB
            norm_B(*pend)
            if between is not None:
                between(len(items) - 1)

        P.phase = 'P1_hTo'
        hTo = A.alloc('hTo', [128, 8, 1024], BF16)
        xin = [A.alloc('xin%d' % i, [128, 1024], F32) for i in range(3)]
        xin_cnt = [0]

        def load_x(src_rows):
            i = xin_cnt[0] % 3
            xin_cnt[0] += 1
            P.dma('sp', xin[i], src_rows, writes=['xin%d' % i])
            return xin[i], 'xin%d' % i

        pre_x = [load_x(I['xo'][t_ * 128:(t_ + 1) * 128, :]) for t_ in range(3)]
        if 'hTo' in DBG:
            hdbg = A.alloc('hdbg', [128, 8, 1024], F32)
            cp('dve', hdbg, hTo, [('hTo', t) for t in range(8)], ['hdbg'])
            dbg('hTo', hdbg.rearrange('p a b -> p (a b)'), ['hdbg'])

        w_in_v = I['w_in'].rearrange('(kc p) n -> p kc n', p=128)

        def load_w(name, shape, src, q='pool'):
            t = A.alloc(name, shape, BF16)
            P.dma(q, t, src, writes=[name])
            return t

        keepF = A.alloc('keepF', [4, 2048], BF16)
        keepB = A.alloc('keepB', [4, 2048], BF16)
        memset('pool', keepF, 1.0, ['keepF'])
        memset('pool', keepF.rearrange('p (c s) -> p c s', s=128)[:, :, 0:1], 0.0, ['keepF'])
        memset('pool', keepB, 1.0, ['keepB'])
        memset('pool', keepB.rearrange('p (c s) -> p c s', s=128)[:, :, 127:128], 0.0, ['keepB'])
        maskF = A.alloc('maskF', [128, 4, 128], BF16)
        maskB = A.alloc('maskB', [128, 4, 128], BF16)
        m_onesm = A.mark()
        onesm = A.alloc('onesm', [128, 4, 128], F32)
        memset('pool', onesm, 1.0, ['onesm'])
        P.pool(lambda e: e.affine_select(out=maskF, in_=onesm, pattern=[[0, 4], [1, 128]], compare_op=ALU.is_ge,
                                         fill=0.0, base=0, channel_multiplier=-1), ['onesm'], ['maskF'])
        P.pool(lambda e: e.affine_select(out=maskB, in_=onesm, pattern=[[0, 4], [-1, 128]], compare_op=ALU.is_ge,
                                         fill=0.0, base=0, channel_multiplier=1), ['onesm'], ['maskB'])
        A.release(m_onesm)
        I4 = ident_f[0:4, 0:4]
        stm4 = load_small('stm4', [4, 2], I['stm4'])
        stnT = load_small('stnT', [128, 8], I['stnT'])

        def m_chain(pref, d, amax, nbt, n, m0, rk, m0keys):
            rk = list(rk) + list(m0keys)
            mout = A.alloc(pref + 'mout', [4, n], F32)
            mprev = A.alloc(pref + 'mprev', [4, n], F32)
            Ml = A.alloc(pref + 'Ml', [4, n], F32)
            g = A.alloc(pref + 'g', [4, n], F32)
            k = lambda x: pref + x
            if d == 'f':
                scan(mout, amax, nbt, m0, ALU.max, ALU.subtract, rk, [k('mout')])
                first = mprev[:, 0:1]
                if n > 1:
                    cp('dve', mprev[:, 1:n], mout[:, 0:n - 1], [k('mout')], [(k('mprev'), 1)])
            else:
                scan(mout[:, ::-1], amax[:, ::-1], nbt[:, ::-1], m0, ALU.max, ALU.subtract, rk, [k('mout')])
                first = mprev[:, n - 1:n]
                if n > 1:
                    cp('dve', mprev[:, 0:n - 1], mout[:, 1:n], [k('mout')], [(k('mprev'), 1)])
            if isinstance(m0, float):
                memset('dve', first, m0, [(k('mprev'), 0)])
            else:
                cp('dve', first, m0, list(m0keys), [(k('mprev'), 0)])
            mpk = [(k('mprev'), 0), (k('mprev'), 1)] if n > 1 else [(k('mprev'), 0)]
            tt('dve', Ml, mprev, amax, ALU.max, mpk + rk, [k('Ml')])
            tt('dve', g, mprev, Ml, ALU.subtract, mpk + [k('Ml')], [k('g')])
            act(g, g, AF.Exp, [k('g')], [k('g')])
            return dict(mout=mout, mprev=mprev, Ml=Ml, g=g, pref=pref, mpk=mpk)

        def gate_dir(pref, di, ntok, G16, G16keys, chains, w_bank, thr_bank):
            d = 'fb'[di]
            nch = ntok // 128
            amax = A.alloc(pref + 'amax' + d, [4, nch], F32)
            nbt = A.alloc(pref + 'nbt' + d, [4, nch], F32)
            amk, nbk = pref + 'amax' + d, pref + 'nbt' + d
            out = {}
            chain_tiles = []
            mk = A.mark()
            for (name, c0, n, m0, m0keys) in chains:
                chain_tiles.append(None)
            A.release(mk)
            pre = pref + d + '_'
            ch_res = {}
            specs = []
            for (name, c0, n, m0, m0keys) in chains:
                specs.append((name, c0, n, m0, m0keys))
            pend = []
            for (name, c0, n, m0, m0keys) in specs:
                p_ = pre + name + '_'
                tiles = dict(mout=A.alloc(p_ + 'mout', [4, n], F32), mprev=A.alloc(p_ + 'mprev', [4, n], F32),
                             Ml=A.alloc(p_ + 'Ml', [4, n], F32), g=A.alloc(p_ + 'g', [4, n], F32))
                pend.append((p_, tiles))
            mrow = A.mark()
            ig = A.alloc(pre + 'ig', [4, ntok], F32)
            lf = A.alloc(pre + 'lf', [4, ntok], F32)
            bn = A.alloc(pre + 'bn', [4, ntok], F32)
            igk, lfk, bnk = pre + 'ig', pre + 'lf', pre + 'bn'
            P.dma('sp', ig, G16[di * 4:di * 4 + 4, :], reads=G16keys, writes=[igk])
            P.dma('sp', lf, G16[8 + di * 4:8 + di * 4 + 4, :], reads=G16keys, writes=[lfk])
            act(lf, lf, AF.Exp, [lfk], [lfk], scale=-1.0)
            act(lf, lf, AF.Ln, [lfk], [lfk], bias=1.0)
            if d == 'f':
                scan(bn, keepF[:, 0:ntok], lf, 0.0, ALU.mult, ALU.add, ['keepF', lfk], [bnk])
            else:
                scan(bn[:, ::-1], keepB[:, 0:ntok][:, ::-1], lf[:, ::-1], 0.0, ALU.mult, ALU.add, ['keepB', lfk], [bnk])
            tt('dve', ig, ig, bn, ALU.add, [igk, bnk], [igk])
            igv = ig.rearrange('p (c s) -> p c s', s=128)
            bnv = bn.rearrange('p (c s) -> p c s', s=128)
            P.dve(lambda e, o=amax, i=igv: e.tensor_reduce(out=o, in_=i, axis=AX.X, op=ALU.max), [igk], [amk])
            cp('dve', nbt, bnv[:, :, 127] if d == 'f' else bnv[:, :, 0], [bnk], [nbk])
            for (name, c0, n, m0, m0keys), (p_, tiles) in zip(specs, pend):
                k = lambda x, p_=p_: p_ + x
                mout, mprev, Ml, g = tiles['mout'], tiles['mprev'], tiles['Ml'], tiles['g']
                am, nb = amax[:, c0:c0 + n], nbt[:, c0:c0 + n]
                rk = [amk, nbk] + list(m0keys)
                if d == 'f':
                    scan(mout, am, nb, m0, ALU.max, ALU.subtract, rk, [k('mout')])
                    first = mprev[:, 0:1]
                    if n > 1:
                        cp('dve', mprev[:, 1:n], mout[:, 0:n - 1], [k('mout')], [(k('mprev'), 1)])
                else:
                    scan(mout[:, ::-1], am[:, ::-1], nb[:, ::-1], m0, ALU.max, ALU.subtract, rk, [k('mout')])
                    first = mprev[:, n - 1:n]
                    if n > 1:
                        cp('dve', mprev[:, 0:n - 1], mout[:, 1:n], [k('mout')], [(k('mprev'), 1)])
                if isinstance(m0, float):
                    memset('dve', first, m0, [(k('mprev'), 0)])
                else:
                    cp('dve', first, m0, list(m0keys), [(k('mprev'), 0)])
                mpk = [(k('mprev'), 0), (k('mprev'), 1)] if n > 1 else [(k('mprev'), 0)]
                tt('dve', Ml, mprev, am, ALU.max, mpk + [amk], [k('Ml')])
                tt('dve', g, mprev, Ml, ALU.subtract, mpk + [k('Ml')], [k('g')])
                act(g, g, AF.Exp, [k('g')], [k('g')])
                ch_res[name] = dict(mout=mout, mprev=mprev, Ml=Ml, g=g, pref=p_, mpk=mpk, c0=c0, n=n)
                i3 = ig[:, c0 * 128:(c0 + n) * 128].rearrange('p (c s) -> p c s', s=128)
                tt('dve', i3, i3, Ml.unsqueeze(2).to_broadcast([4, n, 128]), ALU.subtract, [igk, k('Ml')], [igk])
                if thr_bank is not None:
                    b3 = bn[:, c0 * 128:(c0 + n) * 128].rearrange('p (c s) -> p c s', s=128)
                    tt('dve', b3, b3, Ml.unsqueeze(2).to_broadcast([4, n, 128]), ALU.subtract, [bnk, k('Ml'), nbk], [bnk])
            act(ig, ig, AF.Exp, [igk], [igk])
            if thr_bank is not None:
                act(bn, bn, AF.Exp, [bnk], [bnk])
            for c in range(nch):
                col = (c * 2 + di) * 4
                tr(BK[w_bank][:, col:col + 4], ig[:, c * 128:(c + 1) * 128], I4, [igk, 'ident_f'], [bkey(w_bank)])
                if thr_bank is not None:
                    tr(BK[thr_bank][:, col:col + 4], bn[:, c * 128:(c + 1) * 128], I4, [bnk, 'ident_f'], [bkey(thr_bank)])
            A.release(mrow)
            return ch_res

        P.phase = 'P2a_hTs'
        Cinit = [A.alloc('Cinit%d' % d, [128, 4, 258], F32) for d in range(2)]
        minit = A.alloc('minit', [4, 2], F32)
        attT = A.alloc('attT', [128, 4, 1024], BF16)
        m_att = A.mark()
        ckvT = A.alloc('ckvT', [128, 2, NKEY], BF16)
        kpeR = A.alloc('kpeR', [96, NKEY], BF16)
        Vt = A.alloc('Vt', [128, 22, 512], BF16)
        wukv = load_w('wukv', [128, 2, 1024], I['w_ukv'].rearrange('(kc p) n -> p kc n', p=128))
        wkv_s = load_w('wkv_s', [128, 8, 256], w_in_v[:, :, O_KV:O_KV + 256])
        wkpe = A.alloc('wkpe', [128, 8, 96], BF16)
        wkpe2 = A.alloc('wkpe2', [128, 8, 96], BF16)
        memset('pool', wkpe, 0.0, ['wkpe'])
        memset('pool', wkpe2, 0.0, ['wkpe2'])
        P.dma('pool', wkpe[:, :, 64:96], w_in_v[:, :, O_KPE:O_KPE + 32], reads=['wkpe'], writes=['wkpe'])
        def build_wkpe2():
            for a in range(2):
                o = 64 + a * 16
                P.act(lambda e, d_=wkpe2[:, :, o:o + 8], s_=wkpe[:, :, o + 8:o + 16]: e.mul(out=d_, in_=s_, mul=-1.0),
                      ['wkpe', 'wkpe2'], ['wkpe2'])
                cp('dve', wkpe2[:, :, o + 8:o + 16], wkpe[:, :, o:o + 8], ['wkpe', 'wkpe2'], ['wkpe2'])
        hTs_keys = lambda t0, n: [('hTs', t) for t in range(t0, t0 + n)]
        cs_g = [A.alloc('cs_g%d' % i, [96, 2, 512], F32) for i in range(1)]
        kv_ss = [A.alloc('kv_ss%d' % i, [128, 4], F32) for i in range(2)]
        kv_junk = A.alloc('kv_junk', [128, 384], BF16)
        kv_bf = [A.alloc('kv_bf%d' % i, [128, 384], BF16) for i in range(2)]
        rp_t = [A.alloc('rp_t%d' % i, [96, 512], F32) for i in range(2)]
        kv_cnt = [0]
        eps_kv = eps_t

        def kside_p1(hT, hkey, tok0, out_rows=None, zb=(5, 6)):
            i = kv_cnt[0] % 2
            kv_cnt[0] += 1
            bk = zb[i]
            ss, sk = kv_ss[i], 'kv_ss%d' % i
            for kc in range(KC):
                mm(BK[bk][:, 0:256], hT[:, kc, tok0:tok0 + 128], wkv_s[:, kc, :], kc == 0, kc == KC - 1,
                   ['wkv_s', hkey], [bkey(bk)])
            act(kv_junk[:, 0:256], BK[bk][:, 0:256], AF.Square, [bkey(bk)], ['kv_junk', (sk, 0)], accum=ss[:, 0:1])
            act(ss[:, 1:2], ss[:, 0:1], AF.Sqrt, [(sk, 0), 'eps_t'], [(sk, 1)], bias=eps_kv, scale=1.0 / 256)
            recip(ss[:, 2:3], ss[:, 1:2], [(sk, 1)], [(sk, 2)])
            cb, cbk = kv_bf[i], 'kv_bf%d' % i
            if out_rows is not None:
                cf = A.alloc('ckv_f%d' % kv_cnt[0], [128, 256], F32)
                cfk = 'ckv_f%d' % kv_cnt[0]
                stt(cf, BK[bk][:, 0:256], ss[:, 2:3], gkv_bc, ALU.mult, ALU.mult, [bkey(bk), (sk, 2), 'gkv_bc'], [cfk])
                P.dma('sp', out_rows, cf, reads=[cfk], writes=[('ockv', kv_cnt[0])])
                final_keys.append(('ockv', kv_cnt[0]))
                cp('act', cb[:, 0:256], cf, [cfk], [cbk])
            else:
                stt(cb[:, 0:256], BK[bk][:, 0:256], ss[:, 2:3], gkv_bc, ALU.mult, ALU.mult,
                    [bkey(bk), (sk, 2), 'gkv_bc'], [cbk])
            return i

        def kside_p2(i, keycol0, tb=7):
            cb, cbk = kv_bf[i], 'kv_bf%d' % i
            pv = BKb[tb].rearrange('p (a b) -> p a b', b=128)
            for rc in range(2):
                tr(pv[:, rc, :], cb[:, rc * 128:(rc + 1) * 128], ident_b, [cbk, 'ident_b'], [bkey(tb)])
            cp('act', ckvT[:, :, keycol0:keycol0 + 128], pv[:, 0:2, :], [bkey(tb)], [('ckvT', keycol0 // 128)])

        def kside_tile(hT, hkey, tok0, keycol0, out_rows=None, zb=(5, 6), tb=7):
            i = kside_p1(hT, hkey, tok0, out_rows, zb)
            kside_p2(i, keycol0, tb)

        def kpe_group(grp):
            cg, cgk = cs_g[0], 'cs_g0'
            P.dma('sp', cg[64:96, 0, :], I['cos_s'][:, grp * 512:(grp + 1) * 512], writes=[(cgk, 0)])
            P.dma('sp', cg[64:96, 1, :], I['sin_s'][:, grp * 512:(grp + 1) * 512], writes=[(cgk, 1)])
            for w_, bk in ((wkpe, 5), (wkpe2, 6)):
                for kc in range(KC):
                    mm(BK[bk][0:96, :], w_[:, kc, :], hTs[:, kc, grp * 512:(grp + 1) * 512], kc == 0, kc == KC - 1,
                       ['wkpe', 'wkpe2'] + hTs_keys(grp * 4, 4), [bkey(bk)])
            tt('dve', rp_t[0][64:96, :], BK[5][64:96, :], cg[64:96, 0, :], ALU.mult, [bkey(5), (cgk, 0)], ['rp_t0'])
            tt('dve', rp_t[1][64:96, :], BK[6][64:96, :], cg[64:96, 1, :], ALU.mult, [bkey(6), (cgk, 1)], ['rp_t1'])
            tt('dve', kpeR[64:96, grp * 512:(grp + 1) * 512], rp_t[0][64:96, :], rp_t[1][64:96, :], ALU.add,
               ['rp_t0', 'rp_t1'], [('kpeR', grp)])

        wukv_v = wukv.rearrange('p k (h x) -> p k h x', x=128)

        def v_block(kb, vb=None):
            bk = (5 + kb % 2) if vb is None else vb
            for rc in range(2):
                mm(BK[bk].rearrange('p (h x) -> p h x', x=64), ckvT[:, rc, kb * 128:(kb + 1) * 128], wukv_v[:, rc, :, 64:128],
                   rc == 0, rc == 1, [('ckvT', kb), 'wukv'], [bkey(bk)])
            cp('dve' if kb % 2 == 0 else 'act', Vt[:, kb, :], BK[bk], [bkey(bk)], [('Vt', kb)])

        kv_pend = {}

        def kv_between(i):
            if i == 2:
                build_wkpe2()
            if i in (1, 3, 5, 6, 7):
                mod_slab({1: 7, 3: 8, 5: 9, 6: 10, 7: 11}[i])
                if i == 7:
                    finish_mod()
            if 0 <= i - 1 < 12:
                kv_pend[i - 1] = kside_p1(hTs, ('hTs', i - 1), (i - 1) * 128)
            if 0 <= i - 2 < 12:
                kside_p2(kv_pend.pop(i - 2), (i - 2) * 128)
            if 0 <= i - 3 < 12:
                v_block(i - 3)
            if 1 <= i <= 12 and (i - 1) % 4 == 3:
                kpe_group((i - 1) // 4)

        def xo_src(t):
            def f():
                xt, xk = pre_x[t] if t < 3 else load_x(I['xo'][t * 128:(t + 1) * 128, :])
                return xt, [xk]
            return f
        norm_pipeline([(xo_src(t), (A1, B1, 0 if t < 4 else 1, hTo[:, :, t * 128:(t + 1) * 128], ('hTo', t), 'A1', ('modT', 0),
                                    [(3, 1), (4, 2)])) for t in range(8)],
                      between=lambda i: (mod_slab({3: 4, 5: 5, 7: 6}[i]) if i in (3, 5, 7) else None))
        m_seq = A.mark()
        NS = 12
        hTs = A.alloc('hTs', [128, 8, NS * 128], BF16)
        wtok_s = A.alloc('wtok_s', [128, NS, 4], F32)
        gbc_s = A.alloc('gbc_s', [128, NS, 4], F32)
        def xs_src(t):
            def f():
                xt, xk = load_x(I['xs'][t * 128:(t + 1) * 128, :])
                return xt, [xk]
            return f
        norm_pipeline([(xs_src(t), (A1, B1, 1, hTs[:, :, t * 128:(t + 1) * 128], ('hTs', t), 'A1', ('modT', 0), [(3, 1), (4, 2)]))
                       for t in range(12)], between=kv_between)
        for i_ in (12, 13, 14):
            kv_between(i_)
        m_kv = A.mark()
        cck = A.alloc('cck', [128, 2, 256], BF16)
        P.dma('pool', cck, I['cckv'].rearrange('(t p) r -> p t r', p=128), writes=['cck'])
        ckr = A.alloc('ckr', [128, 2, 96], BF16)
        memset('pool', ckr, 0.0, ['ckr'])
        P.dma('pool', ckr[:, :, 64:96], I['ckro'].rearrange('(t p) r -> p t r', p=128), reads=['ckr'], writes=['ckr'])
        for t in range(2):
            pv = BKb[3 + t].rearrange('p (a b) -> p a b', b=128)
            for rc in range(2):
                tr(pv[:, rc, :], cck[:, t, rc * 128:(rc + 1) * 128], ident_b, ['cck', 'ident_b'], [bkey(3 + t)])
            tr(pv[0:96, 2, :], ckr[:, t, :], ident_b, ['ckr', 'ident_b'], [bkey(3 + t)])
            cp('act', ckvT[:, :, 2048 + t * 128:2048 + (t + 1) * 128], pv[:, 0:2, :], [bkey(3 + t)], [('ckvT', 16 + t)])
            cp('dve', kpeR[64:96, 2048 + t * 128:2048 + (t + 1) * 128], pv[64:96, 2, :], [bkey(3 + t)], [('kpeR', 4)])
        for kb in (16, 17):
            v_block(kb)
        A.release(m_kv)

        okr_sb = A.alloc('okr_sb', [128, 4, 32], F32)
        kfill = []

        def kf_prompt(t):
            kside_tile(hTo, ('hTo', t), t * 128, 2304 + t * 128, out_rows=O['ockv'][t * 128:(t + 1) * 128, :],
                       zb=(1, 2), tb=3)
            for kc in range(KC):
                mm(BK[0][:, 0:32], hTo[:, kc, t * 128:(t + 1) * 128], wkpe[:, kc, 64:96], kc == 0, kc == KC - 1,
                   ['wkpe', ('hTo', t)], [bkey(0)])
            cp('dve', okr_sb[:, t, :], BK[0][:, 0:32], [bkey(0)], [('okr_sb', t)])
            v_block(18 + t, vb=4)

        def kf_prompt_fin():
            P.dma('sp', O['okr'].rearrange('(t p) r -> p t r', p=128), okr_sb, reads=[('okr_sb', t) for t in range(4)],
                  writes=['okr'])
            final_keys.append('okr')
            for kc in range(KC):
                mm(BK[0][0:96, :], wkpe[:, kc, :], hTo[:, kc, 0:512], kc == 0, kc == KC - 1,
                   ['wkpe'] + [('hTo', t) for t in range(4)], [bkey(0)])
            cp('dve', kpeR[64:96, 2304:2816], BK[0][64:96, :], [bkey(0)], [('kpeR', 5)])

        def kf_own(t):
            kside_tile(hTo, ('hTo', t), t * 128, (12 + t - 4) * 128, zb=(1, 2), tb=3)
            v_block(12 + t - 4, vb=4)
        for t in range(4):
            kfill.append(lambda t=t: kf_prompt(t))
        kfill.append(kf_prompt_fin)
        for t in range(4, 8):
            kfill.append(lambda t=t: kf_own(t))

        P.phase = 'P2b_gates'
        A.report('P2b_gates')
        NT = NS * 128
        minit_f = A.alloc('minit_f', [4, 1], F32)
        Rm = [A.alloc('Rm%d' % i, [4, 1], F32) for i in range(5)]
        amax_s = A.alloc('amax_s', [4, NS], F32)
        nbt_s = A.alloc('nbt_s', [4, NS], F32)
        mout_s = A.alloc('mout_s', [4, NS], F32)
        mprev_s = A.alloc('mprev_s', [4, NS], F32)
        Ml_s = A.alloc('Ml_s', [4, NS], F32)
        g_s = A.alloc('g_s', [4, NS], F32)
        dsl = load_small('dsl', [4, NS], I['dsl'])
        mt1 = A.alloc('mt1', [4, NS], F32)
        Gd = A.alloc('Gd_s', [4, NS, 4], F32)
        m_rows = A.mark()
        wg = load_w('wg_s', [128, 8, 16], w_in_v[:, :, O_G:O_G + 16])
        wk_s = A.alloc('wk_s', [128, 8, 512], BF16, at=190 * 1024)
        wv_h = [A.alloc('wv_s0', [128, 8, 512], BF16, at=198 * 1024), None]
        P.dma('pool', wk_s, w_in_v[:, :, O_MK:O_MK + 512], writes=['wk_s'])
        P.dma('pool', wv_h[0], w_in_v[:, :, O_MV:O_MV + 512], writes=['wv_s0'])
        G16s = A.alloc('G16s', [16, NT], F32)
        for grp in range(3):
            bk = 5 + grp % 2
            for kc in range(KC):
                mm(BK[bk][0:16, :], wg[:, kc, :], hTs[:, kc, grp * 512:(grp + 1) * 512], kc == 0, kc == KC - 1,
                   ['wg_s'] + hTs_keys(grp * 4, 4), [bkey(bk)])
            act(G16s[:, grp * 512:(grp + 1) * 512], BK[bk][0:16, :], AF.Identity, [bkey(bk), 'bgate'],
                [('G16s', grp)], bias=bgate)
        G16s_keys = [('G16s', g_) for g_ in range(3)]
        P.filler = kfill
        P.fill_every = 5
        T = []
        for ri in range(4):
            t_ = A.alloc('s_T%d' % ri, [4, NT], F32)
            P.dma('sp', t_, G16s[ri * 4:(ri + 1) * 4, :], reads=G16s_keys, writes=['s_T%d' % ri])
            T.append(t_)
        dm = A.alloc('s_dm', [4, NT], F32)
        P.dma('sp', dm, I['dmask'], writes=['s_dm'])

        def blend_rows(x, xk, y, yk):
            tt('dve', x, x, y, ALU.subtract, [xk, yk], [xk])
            tt('dve', x, x, dm, ALU.mult, [xk, 's_dm'], [xk])
            tt('dve', x, x, y, ALU.add, [xk, yk], [xk])
        blend_rows(T[0], 's_T0', T[1], 's_T1')
        blend_rows(T[2], 's_T2', T[3], 's_T3')
        act(T[2], T[2], AF.Exp, ['s_T2'], ['s_T2'], scale=-1.0)
        act(T[2], T[2], AF.Ln, ['s_T2'], ['s_T2'], bias=1.0)
        scan(T[1], keepF[:, 0:NT], T[2], 0.0, ALU.mult, ALU.add, ['keepF', 's_T2'], ['s_T1'])
        scan(T[3][:, ::-1], keepB[:, 0:NT][:, ::-1], T[2][:, ::-1], 0.0, ALU.mult, ALU.add, ['keepB', 's_T2'], ['s_T3'])
        bF = T[1].rearrange('p (c s) -> p c s', s=128)
        bB = T[3].rearrange('p (c s) -> p c s', s=128)
        cp('dve', nbt_s, bB[:, :, 0], ['s_T3'], ['nbt_s'])
        tt('dve', mt1, bF[:, :, 127], nbt_s, ALU.subtract, ['s_T1', 'nbt_s'], ['mt1'])
        tt('dve', mt1, mt1, dsl, ALU.mult, ['mt1', 'dsl'], ['mt1'])
        tt('dve', nbt_s, nbt_s, mt1, ALU.add, ['nbt_s', 'mt1'], ['nbt_s'])
        blend_rows(T[1], 's_T1', T[3], 's_T3')
        tt('dve', T[0], T[0], T[1], ALU.add, ['s_T0', 's_T1'], ['s_T0'])
        P.dve(lambda e: e.tensor_reduce(out=amax_s, in_=T[0].rearrange('p (c s) -> p c s', s=128), axis=AX.X,
                                        op=ALU.max), ['s_T0'], ['amax_s'])
        sel4 = sel[0:4, :]
        cp('dve', Rm[0], stm4[:, 0:1], ['stm4'], ['Rm0'])
        for k in range(4):
            rk_, rn_ = 'Rm%d' % k, 'Rm%d' % (k + 1)
            if k == 0:
                ts('dve', minit_f, Rm[0], sel4[:, 0:1], None, ALU.mult, None, ['Rm0', 'sel'], ['minit_f'])
            else:
                stt(minit_f, Rm[k], sel4[:, k:k + 1], minit_f, ALU.mult, ALU.add, [rk_, 'sel', 'minit_f'], ['minit_f'])
            tt('dve', mt1[:, 0:1], stm4[:, 1:2], Rm[k], ALU.subtract, ['stm4', rk_], ['mt1'])
            stt(Rm[k], mt1[:, 0:1], sel4[:, k:k + 1], Rm[k], ALU.mult, ALU.add, ['mt1', 'sel', rk_], [rk_])
            if k < 3:
                sl_ = slice(4 * k, 4 * k + 4)
                scan(mout_s[:, sl_], amax_s[:, sl_], nbt_s[:, sl_], Rm[k], ALU.max, ALU.subtract,
                     ['amax_s', 'nbt_s', rk_], [('mout_s', k)])
                cp('dve', mprev_s[:, 4 * k:4 * k + 1], Rm[k], [rk_], [('mprev_s', k, 0)])
                cp('dve', mprev_s[:, 4 * k + 1:4 * k + 4], mout_s[:, 4 * k:4 * k + 3], [('mout_s', k)], [('mprev_s', k, 1)])
                cp('dve', Rm[k + 1], mout_s[:, 4 * k + 3:4 * k + 4], [('mout_s', k)], [rn_])
        cp('dve', minit[:, 0:1], minit_f, ['minit_f'], [('minit', 0)])
        cp('dve', minit[:, 1:2], Rm[3], ['Rm3'], [('minit', 1)])
        mpk_s = [('mprev_s', k, i_) for k in range(3) for i_ in range(2)]
        tt('dve', Ml_s, mprev_s, amax_s, ALU.max, mpk_s + ['amax_s'], ['Ml_s'])
        tt('dve', g_s, mprev_s, Ml_s, ALU.subtract, mpk_s + ['Ml_s'], ['g_s'])
        act(g_s, g_s, AF.Exp, ['g_s'], ['g_s'])
        a3 = T[0].rearrange('p (c s) -> p c s', s=128)
        tt('dve', a3, a3, Ml_s.unsqueeze(2).to_broadcast([4, NS, 128]), ALU.subtract, ['s_T0', 'Ml_s'], ['s_T0'])
        act(T[0], T[0], AF.Exp, ['s_T0'], ['s_T0'])
        for c in range(NS):
            tr(BK[7][:, c * 4:c * 4 + 4], T[0][:, c * 128:(c + 1) * 128], I4, ['s_T0', 'ident_f'], [bkey(7)])
        cp('dve', wtok_s.rearrange('p a b -> p (a b)'), BK[7][:, 0:NS * 4], [bkey(7)], ['wtok_s'])
        tt('dve', Gd, g_s.unsqueeze(2).to_broadcast([4, NS, 4]), I4.unsqueeze(1).to_broadcast([4, NS, 4]), ALU.mult,
           ['g_s', 'ident_f'], ['Gd_s'])
        mm(BK[6][:, 0:NS * 4], ones_f[0:4, :], Gd.rearrange('p a b -> p (a b)'), True, True, ['ones_f', 'Gd_s'], [bkey(6)])
        cp('dve', gbc_s.rearrange('p a b -> p (a b)'), BK[6][:, 0:NS * 4], [bkey(6)], ['gbc_s'])
        dbg('minit', minit, [('minit', 0), ('minit', 1)])
        P.filler = None
        while kfill:
            kfill.pop(0)()
        A.release(m_rows)

        P.phase = 'P2c_scan'
        A.report('P2c_scan')
        m_scan = A.mark()
        Rst = A.alloc('Rst', [128, 4, 258], F32)
        C0b = A.alloc('C0b', [128, 4, 258], F32)
        ctmp = A.alloc('ctmp', [128, 4, 258], F32)
        RK = [('Rst', h_) for h_ in range(4)]
        m_stc = A.mark()
        stc_sb = A.alloc('stc_sb', [128, 8, 2, 128], F32)
        P.dma('sp', stc_sb, I['stC'].rearrange('a (vh p) d -> p a vh d', p=128), writes=['stc_sb'])
        for d, (dst_, dkeys) in enumerate(((Rst, RK), (C0b, ['C0b']))):
            memset('dve', dst_, 0.0, dkeys)
            for h in range(4):
                bk = 5 + (d * 4 + h) % 2
                for vh in range(2):
                    tr(BK[bk][:, vh * 128:(vh + 1) * 128], stc_sb[:, d * 4 + h, vh, :], ident_f, ['stc_sb', 'ident_f'],
                       [bkey(bk)])
                cp('act', dst_[:, h, 0:256], BK[bk][:, 0:256], [bkey(bk)], [dkeys[h]] if d == 0 else dkeys)
            cp('dve', dst_[:, :, 256:257], stnT[:, d * 4:(d + 1) * 4].unsqueeze(2), ['stnT'], dkeys)
        A.release(m_stc)
        wv_h[1] = load_w('wv_s1', [128, 8, 512], w_in_v[:, :, O_MV + 512:O_MV + 1024])
        NPB = 3
        mks = [A.alloc('mks%d' % i, [128, 512], BF16) for i in range(NPB)]
        mvs = [A.alloc('mvs%d' % i, [128, 4, 258], BF16) for i in range(NPB)]
        for i in range(NPB):
            memset('pool', mvs[i][:, :, 256:258], 0.0, [('mvs%d' % i, 'x')])
            memset('pool', mvs[i][:, :, 256:257], 1.0, [('mvs%d' % i, 'x')])
        pj_rr = [0]

        def proj_kv_seq(t):
            i = pj_rr[0] % NPB
            pj_rr[0] += 1
            bk = 4 + i % 2
            for kc in range(KC):
                mm(BK[bk], hTs[:, kc, t * 128:(t + 1) * 128], wk_s[:, kc, :], kc == 0, kc == KC - 1,
                   ['wk_s', ('hTs', t)], [bkey(bk)])
            act(mks[i], BK[bk], AF.Identity, [bkey(bk)], ['mks%d' % i], scale=128 ** -0.5)
            for half in range(2):
                bk2 = 6 + half
                for kc in range(KC):
                    mm(BK[bk2], hTs[:, kc, t * 128:(t + 1) * 128], wv_h[half][:, kc, :], kc == 0,
                       kc == KC - 1, ['wv_s%d' % half, ('hTs', t)], [bkey(bk2)])
                cp('dve' if half == 0 else 'act', mvs[i][:, half * 2:half * 2 + 2, 0:256],
                   BK[bk2].rearrange('p (h v) -> p h v', v=256), [bkey(bk2)], [('mvs%d' % i, half)])
            return i

        vw_s = [A.alloc('vw_s%d' % i, [128, 4, 258], BF16) for i in range(3)]
        sc_rr = [0]
        Rf = Rst.rearrange('p a b -> p (a b)')

        def boundary(k):
            cf = Cinit[0].rearrange('p a b -> p (a b)')
            if k == 0:
                ts('dve', cf, Rf, sel[:, 0:1], None, ALU.mult, None, RK + ['sel'], ['Cinit0'])
            else:
                stt(cf, Rf, sel[:, k:k + 1], cf, ALU.mult, ALU.add, RK + ['sel', 'Cinit0'], ['Cinit0'])
            ct = ctmp.rearrange('p a b -> p (a b)')
            tt('pool', ct, C0b.rearrange('p a b -> p (a b)'), Rf, ALU.subtract, ['C0b'] + RK, ['ctmp'])
            stt(Rf, ct, sel[:, k:k + 1], Rf, ALU.mult, ALU.add, ['ctmp', 'sel'] + RK, RK)

        def scan_prep(c, pb):
            i = sc_rr[0] % 3
            sc_rr[0] += 1
            tt('pool', vw_s[i], mvs[pb], wtok_s[:, c, :].unsqueeze(2).to_broadcast([128, 4, 258]), ALU.mult,
               [('mvs%d' % pb, 'x'), ('mvs%d' % pb, 0), ('mvs%d' % pb, 1), 'wtok_s'], ['vw_s%d' % i])
            return i

        def scan_step(i, c, pb):
            vw, vk = vw_s[i], 'vw_s%d' % i
            for h in range(4):
                bk = (i % 2) * 2 + h % 2
                mm(BK[bk][:, 0:257], mks[pb][:, h * 128:(h + 1) * 128], vw[:, h, 0:257], True, True,
                   ['mks%d' % pb, vk], [bkey(bk)])
                stt(Rst[:, h, 0:257], Rst[:, h, 0:257], gbc_s[:, c, h:h + 1], BK[bk][:, 0:257], ALU.mult, ALU.add,
                    [('Rst', h), 'gbc_s', bkey(bk)], [('Rst', h)])

        pend = None
        for c in range(NS):
            pb_ = proj_kv_seq(c)
            vi_ = scan_prep(c, pb_)
            if pend is not None:
                if pend[1] % 4 == 0:
                    boundary(pend[1] // 4)
                scan_step(*pend)
            pend = (vi_, c, pb_)
        if pend[1] % 4 == 0:
            boundary(pend[1] // 4)
        scan_step(*pend)
        boundary(3)
        cp('dve', Cinit[1].rearrange('p a b -> p (a b)'), Rf, RK, ['Cinit1'])
        dbg('Cinit0', Cinit[0].rearrange('p a b -> p (a b)'), ['Cinit0'])
        dbg('Cinit1', Cinit[1].rearrange('p a b -> p (a b)'), ['Cinit1'])
        A.release(m_scan)
        A.free_fixed('wk_s')
        A.free_fixed('wv_s0')

        A.release(m_seq)

        P.phase = 'P3a_qk'
        A.report('P3a_qk')
        m_mla = A.mark()
        qj = [A.alloc('qj%d' % i, [128, 384], BF16) for i in range(2)]
        rp_o = [A.alloc('rp_o%d' % i, [96, 512], F32) for i in range(2)]
        wq_s = load_w('wq_s', [128, 8, 384], w_in_v[:, :, O_Q:O_Q + 384])
        wuq = load_w('wuq', [128, 3, 768], I['w_uq'].rearrange('(kc p) n -> p kc n', p=128))
        for kc in range(3):
            ts('dve', wuq[:, kc, :], wuq[:, kc, :], gqT[:, kc:kc + 1], None, ALU.mult, None, ['wuq', 'gqT'], ['wuq'])
        wuqp = A.alloc('wuqp', [128, 3, 768], BF16)
        memset('pool', wuqp, 0.0, ['wuqp'])
        for kc in range(3):
            sv = wuq[:, kc, :].rearrange('p (h x) -> p h x', x=96)
            dv = wuqp[:, kc, :].rearrange('p (h x) -> p h x', x=96)
            for a in range(2):
                o = 64 + a * 16
                P.act(lambda e, d_=dv[:, :, o:o + 8], s_=sv[:, :, o + 8:o + 16]: e.mul(out=d_, in_=s_, mul=-1.0),
                      ['wuq', 'wuqp'], ['wuqp'])
                cp('dve', dv[:, :, o + 8:o + 16], sv[:, :, o:o + 8], ['wuq', 'wuqp'], ['wuqp'])
        cs_o = [A.alloc('cs_o%d' % i, [96, 512], F32) for i in range(2)]
        P.dma('sp', cs_o[0][64:96, :], I['cos_o'], writes=['cs_o0'])
        P.dma('sp', cs_o[1][64:96, :], I['sin_o'], writes=['cs_o1'])
        qnT = A.alloc('qnT', [128, 3, 1024], BF16)
        q_ss = [A.alloc('q_ss%d' % i, [128, 4], F32) for i in range(2)]
        def q_p1(t):
            i = t % 2
            bk = 5 + i
            ss, sk = q_ss[i], 'q_ss%d' % i
            for kc in range(KC):
                mm(BK[bk][:, 0:384], hTo[:, kc, t * 128:(t + 1) * 128], wq_s[:, kc, :], kc == 0, kc == KC - 1,
                   ['wq_s', ('hTo', t)], [bkey(bk)])
            act(qj[i], BK[bk][:, 0:384], AF.Square, [bkey(bk)], ['qj%d' % i, (sk, 0)], accum=ss[:, 0:1])
            act(ss[:, 1:2], ss[:, 0:1], AF.Sqrt, [(sk, 0), 'eps_t'], [(sk, 1)], bias=eps_t, scale=1.0 / 384)
            recip(ss[:, 2:3], ss[:, 1:2], [(sk, 1)], [(sk, 2)])
            ts('dve', qj[i], BK[bk][:, 0:384], ss[:, 2:3], None, ALU.mult, None, [bkey(bk), (sk, 2)], ['qj%d' % i])

        def q_p2(t):
            i = t % 2
            bk2 = 3 + i
            pv = BKb[bk2].rearrange('p (a b) -> p a b', b=128)
            for rc in range(3):
                tr(pv[:, rc, :], qj[i][:, rc * 128:(rc + 1) * 128], ident_b, ['qj%d' % i, 'ident_b'], [bkey(bk2)])
            cp('act', qnT[:, :, t * 128:(t + 1) * 128], pv[:, 0:3, :], [bkey(bk2)], [('qnT', t)])

        for t in range(8):
            q_p1(t)
            if t >= 1:
                q_p2(t - 1)
        q_p2(7)
        for w_, bk in ((wkpe, 1), (wkpe2, 2)):
            for kc in range(KC):
                mm(BK[bk][0:96, :], w_[:, kc, :], hTo[:, kc, 512:1024], kc == 0, kc == KC - 1,
                   ['wkpe', 'wkpe2'] + [('hTo', t) for t in range(4, 8)], [bkey(bk)])
        tt('dve', rp_t[0][64:96, :], BK[1][64:96, :], cs_o[0][64:96, :], ALU.mult, [bkey(1), 'cs_o0'], ['rp_t0'])
        tt('dve', rp_t[1][64:96, :], BK[2][64:96, :], cs_o[1][64:96, :], ALU.mult, [bkey(2), 'cs_o1'], ['rp_t1'])
        tt('dve', kpeR[64:96, 1536:2048], rp_t[0][64:96, :], rp_t[1][64:96, :], ALU.add, ['rp_t0', 'rp_t1'], [('kpeR', 3)])

        P.phase = 'P3b_attn'
        A.report('P3b_attn')
        w_mq = A.alloc('w_mq', [128, 8, 512], BF16, at=190 * 1024)
        w_mk = A.alloc('w_mk', [128, 8, 512], BF16, at=198 * 1024)
        P.dma('pool', w_mq, w_in_v[:, :, O_MQ:O_MQ + 512], writes=['w_mq'])
        P.dma('pool', w_mk, w_in_v[:, :, O_MK:O_MK + 512], writes=['w_mk'])
        KT = [A.alloc('KT%d' % i, [96, NKEY], BF16) for i in range(2)]
        QT = [A.alloc('QT%d' % i, [96, 1024], BF16) for i in range(2)]
        pT = [A.alloc('pT%d' % i, [128, 512], BF16) for i in range(3)]
        rdn = [A.alloc('rdn%d' % i, [128, 512], F32) for i in range(2)]
        pt_rr = [0]
        sb_rr = [0]
        od_rr = [0]
        units = [(0, 256, [18, 19]), (256, 256, [20, 21]), (512, 512, list(range(18)))]
        kpeR_keys = [('kpeR', i) for i in range(6)]
        def head_prep(h):
            KTh, ktk = KT[h % 2], 'KT%d' % (h % 2)
            QTh, qtk = QT[h % 2], 'QT%d' % (h % 2)
            th = []

            def kgrp(grp):
                k0 = grp * 512
                n = min(512, NKEY - k0)
                bk = 6 + grp % 2
                for rc in range(2):
                    mm(BK[bk][:, 0:n], wukv[:, rc, h * 128:(h + 1) * 128], ckvT[:, rc, k0:k0 + n], rc == 0, rc == 1,
                       ['wukv'] + [('ckvT', k0 // 128 + i) for i in range(n // 128)], [bkey(bk)])
                cp('dve', KTh[0:64, k0:k0 + n], BK[bk][0:64, 0:n], [bkey(bk)], [(ktk, grp)])
            for grp in range(6):
                th.append(lambda grp=grp: kgrp(grp))
            th.append(lambda: cp('dve', KTh[64:96, :], kpeR[64:96, :], kpeR_keys, [(ktk, 'r')]))

            def qhalf(half):
                bk = 6 + half
                tk = [('qnT', half * 4 + i) for i in range(4)]
                for kc in range(3):
                    mm(BK[bk][0:96, :], wuq[:, kc, h * 96:(h + 1) * 96], qnT[:, kc, half * 512:(half + 1) * 512], kc == 0,
                       kc == 2, ['wuq'] + tk, [bkey(bk)])
                if half == 0:
                    cp('dve', QTh[:, 0:512], BK[bk][0:96, :], [bkey(bk)], [(qtk, 0)])
                else:
                    for kc in range(3):
                        mm(BK[6][0:96, :], wuqp[:, kc, h * 96:(h + 1) * 96], qnT[:, kc, 512:1024], kc == 0, kc == 2,
                           ['wuqp'] + tk, [bkey(6)])
                    cp('dve', QTh[0:64, 512:1024], BK[bk][0:64, :], [bkey(bk)], [(qtk, 1)])
                    tt('dve', rp_o[0][64:96, :], BK[bk][64:96, :], cs_o[0][64:96, :], ALU.mult, [bkey(bk), 'cs_o0'], ['rp_o0'])
                    tt('dve', rp_o[1][64:96, :], BK[6][64:96, :], cs_o[1][64:96, :], ALU.mult, [bkey(6), 'cs_o1'], ['rp_o1'])
                    tt('dve', QTh[64:96, 512:1024], rp_o[0][64:96, :], rp_o[1][64:96, :], ALU.add, ['rp_o0', 'rp_o1'],
                       [(qtk, 2)])
            th.append(lambda: qhalf(0))
            th.append(lambda: qhalf(1))
            return th

        for t_ in head_prep(0):
            t_()
        for h in range(8):
            KTh, ktk = KT[h % 2], 'KT%d' % (h % 2)
            QTh, qtk = QT[h % 2], 'QT%d' % (h % 2)
            side = head_prep(h + 1) if h + 1 < 8 else []
            ktkeys = [(ktk, g_) for g_ in range(6)] + [(ktk, 'r')]
            qtkeys = [(qtk, 0), (qtk, 1), (qtk, 2)]
            prow = slice((h % 2) * 64, (h % 2) * 64 + 64)
            vcol = (h - h % 2) * 64
            for (q0, nq, blocks) in units:
                ob = 2 + (od_rr[0] % 2) * 2
                db = ob + 1
                od_rr[0] += 1
                nb = len(blocks)
                sbank = {}

                def score(bi):
                    sbk = sb_rr[0] % 2
                    sb_rr[0] += 1
                    sbank[bi] = sbk
                    kb = blocks[bi]
                    mm(BK[sbk][:, 0:nq], KTh[0:96, kb * 128:(kb + 1) * 128], QTh[0:96, q0:q0 + nq], True, True,
                       ktkeys + qtkeys, [bkey(sbk)])
                score(0)
                if nb > 1:
                    score(1)
                for bi, kb in enumerate(blocks):
                    sbk = sbank[bi]
                    pi = pt_rr[0] % 3
                    pt_rr[0] += 1
                    act(pT[pi][:, 0:nq], BK[sbk][:, 0:nq], AF.Exp, [bkey(sbk)], ['pT%d' % pi], scale=MLA_SCALE)
                    first, last = bi == 0, bi == nb - 1
                    mm(BK[ob][:, 0:nq], Vt[:, kb, vcol:vcol + 128], pT[pi][:, 0:nq], first, last,
                       [('Vt', kb), 'pT%d' % pi], [bkey(ob)])
                    mm(BK[db][:, 0:nq], ones_b, pT[pi][:, 0:nq], first, last, ['ones_b', 'pT%d' % pi], [bkey(db)])
                    if bi + 2 < nb:
                        score(bi + 2)
                    if nb > 4 and side and bi % 2 == 1:
                        side.pop(0)()
                ri = od_rr[0] % 2
                act(rdn[ri][prow, 0:nq], BK[db][prow, 0:nq], AF.Ln, [bkey(db)], ['rdn%d' % ri])
                act(rdn[ri][prow, 0:nq], rdn[ri][prow, 0:nq], AF.Exp, ['rdn%d' % ri], ['rdn%d' % ri], scale=-1.0)
                tt('dve', attT[prow, h // 2, q0:q0 + nq], BK[ob][prow, 0:nq], rdn[ri][prow, 0:nq], ALU.mult,
                   [bkey(ob), 'rdn%d' % ri], [('attT', h, q0)])
            while side:
                side.pop(0)()
        attT_keys = [('attT', h, q0) for h in range(8) for q0 in (0, 256, 512)]
        if 'attT' in DBG:
            adbg = A.alloc('adbg', [128, 4, 1024], F32)
            cp('dve', adbg, attT, attT_keys, ['adbg'])
            dbg('attT', adbg.rearrange('p a b -> p (a b)'), ['adbg'])
        A.release(m_att)
        if stop_after == 'mla':
            return finish()

        P.phase = 'P4a_proj'
        A.report('P4a_proj')
        hmT = A.alloc('hmT', [128, 8, 1024], BF16)
        m_ml = A.mark()
        mqT = A.alloc('mqT', [128, 4, 1024], BF16)
        mk_tok = A.alloc('mk_tok', [128, 8, 512], BF16)
        Vx = A.alloc('Vx', [128, 8, 4, 258], BF16)
        memset('pool', Vx[:, :, :, 256:258], 0.0, [('Vx', 'x')])
        memset('pool', Vx[:, :, :, 256:257], 1.0, [('Vx', 'x')])
        Sm = [A.alloc('Sm%d' % d, [128, 8, 4, 128], BF16) for d in range(2)]
        sgT = A.alloc('sgT', [128, 8, 1024], BF16)
        wtok_o = A.alloc('wtok_o', [128, 8, 2, 4], F32)
        thr_o = A.alloc('thr_o', [128, 8, 2, 4], F32)
        gbc_o = A.alloc('gbc_o', [128, 2, 8, 4], F32)
        om_sb = A.alloc('om_sb', [4, 4], F32)
        G16o = A.alloc('G16o', [16, 1024], F32)
        m_mlp = A.mark()
        hTo_half = lambda half: [('hTo', half * 4 + i) for i in range(4)]
        w_mv = load_w('w_mv', [128, 8, 1024], w_in_v[:, :, O_MV:O_MV + 1024])
        w_mo = load_w('w_mo', [128, 8, 1024], w_in_v[:, :, O_MO:O_MO + 1024])
        wg_o = load_w('wg_o', [128, 8, 16], w_in_v[:, :, O_G:O_G + 16])
        m_mkT = A.mark()
        mkT = A.alloc('mkT', [128, 4, 1024], BF16)
        n_ev = 0
        for hh in range(4):
            for half in range(2):
                for (w_, wn, dst, dn, scl) in ((w_mq, 'w_mq', mqT, 'mqT', None), (w_mk, 'w_mk', mkT, 'mkT', 128 ** -0.5)):
                    bk = 6 + n_ev % 2
                    n_ev += 1
                    for kc in range(KC):
                        mm(BK[bk], w_[:, kc, hh * 128:(hh + 1) * 128], hTo[:, kc, half * 512:(half + 1) * 512], kc == 0,
                           kc == KC - 1, [wn] + hTo_half(half), [bkey(bk)])
                    if scl is None:
                        cp('dve', dst[:, hh, half * 512:(half + 1) * 512], BK[bk], [bkey(bk)], [(dn, hh, half)])
                    else:
                        act(dst[:, hh, half * 512:(half + 1) * 512], BK[bk], AF.Identity, [bkey(bk)], [(dn, hh, half)], scale=scl)
        for t in range(8):
            bk = 4 + t % 2
            pv = BKb[bk].rearrange('p (a b) -> p a b', b=128)
            for hh in range(4):
                tr(pv[:, hh, :], mkT[:, hh, t * 128:(t + 1) * 128], ident_b, [('mkT', hh, t // 4), 'ident_b'], [bkey(bk)])
            cp('act' if t % 2 == 0 else 'dve', mk_tok[:, t, :].rearrange('p (a b) -> p a b', b=128), pv[:, 0:4, :],
               [bkey(bk)], [('mk_tok', t)])
        for c in range(8):
            bk = 2 + c % 2
            for hh in range(4):
                mm(BK[bk][:, hh * 128:(hh + 1) * 128], mkT[:, hh, c * 128:(c + 1) * 128], mqT[:, hh, c * 128:(c + 1) * 128],
                   True, True, [('mkT', hh, c // 4), ('mqT', hh, c // 4)], [bkey(bk)])
            sv = BK[bk].rearrange('p (h j) -> p h j', j=128)
            tt('dve', Sm[0][:, c], sv, maskF, ALU.mult, [bkey(bk), 'maskF'], [('Sm0', c)])
            tt('dve', Sm[1][:, c], sv, maskB, ALU.mult, [bkey(bk), 'maskB'], [('Sm1', c)])
        A.release(m_mkT)
        A.free_fixed('w_mq')
        A.free_fixed('w_mk')
        def v_unit(t, half):
            bk = 4 + half
            for kc in range(KC):
                mm(BK[bk], hTo[:, kc, t * 128:(t + 1) * 128], w_mv[:, kc, half * 512:(half + 1) * 512], kc == 0,
                   kc == KC - 1, ['w_mv', ('hTo', t)], [bkey(bk)])
            cp('dve' if half == 0 else 'act', Vx[:, t, half * 2:half * 2 + 2, 0:256],
               BK[bk].rearrange('p (h v) -> p h v', v=256), [bkey(bk)], [('Vx', t, half)])

        def sg_unit(fc, half):
            bk = 2 + half
            for kc in range(KC):
                mm(BK[bk], w_mo[:, kc, fc * 128:(fc + 1) * 128], hTo[:, kc, half * 512:(half + 1) * 512], kc == 0,
                   kc == KC - 1, ['w_mo'] + hTo_half(half), [bkey(bk)])
            act(sgT[:, fc, half * 512:(half + 1) * 512], BK[bk], AF.Sigmoid, [bkey(bk)], [('sgT', fc, half)])
        fill_units = [(lambda t=t, half=half: v_unit(t, half)) for t in range(8) for half in range(2)]
        fill_units += [(lambda fc=fc, half=half: sg_unit(fc, half)) for fc in range(8) for half in range(2)]
        for grp in range(2):
            bk = 6 + grp
            for kc in range(KC):
                mm(BK[bk][0:16, :], wg_o[:, kc, :], hTo[:, kc, grp * 512:(grp + 1) * 512], kc == 0, kc == KC - 1,
                   ['wg_o'] + hTo_half(grp), [bkey(bk)])
            act(G16o[:, grp * 512:(grp + 1) * 512], BK[bk][0:16, :], AF.Identity, [bkey(bk), 'bgate'], [('G16o', grp)],
                bias=bgate)
        ch_o = {}
        Gd_o = A.alloc('Gd_o', [4, 2, 8, 4], F32)
        P.filler = fill_units
        P.fill_every = 2
        for di, d in enumerate(('f', 'b')):
            chains = [('A', 0, 2, 0.0, []), ('B', 2, 2, 0.0, []), ('S', 4, 4, minit[:, di:di + 1], [('minit', di)])]
            r_ = gate_dir('o_', di, 1024, G16o, [('G16o', 0), ('G16o', 1)], chains, 0, 1)
            ch_o[d] = r_
            for nm in ('A', 'B', 'S'):
                c_ = r_[nm]
                tt('dve', Gd_o[:, di, c_['c0']:c_['c0'] + c_['n'], :], c_['g'].unsqueeze(2).to_broadcast([4, c_['n'], 4]),
                   I4.unsqueeze(1).to_broadcast([4, c_['n'], 4]), ALU.mult, [c_['pref'] + 'g', 'ident_f'], [('Gd_o', di, nm)])
            for ui, nm in enumerate(('A', 'B')):
                c_ = r_[nm]
                src = c_['mout'][:, 1:2] if d == 'f' else c_['mout'][:, 0:1]
                cp('dve', om_sb[:, ui * 2 + di:ui * 2 + di + 1], src, [c_['pref'] + 'mout'], [('om_sb', ui, di)])
        P.dma('sp', O['om'], om_sb, reads=[('om_sb', u_, d_) for u_ in range(2) for d_ in range(2)], writes=['om'])
        final_keys.append('om')
        P.filler = None
        while fill_units:
            fill_units.pop(0)()
        cp('dve', wtok_o.rearrange('p a b c -> p (a b c)'), BK[0][:, 0:64], [bkey(0)], ['wtok_o'])
        cp('dve', thr_o.rearrange('p a b c -> p (a b c)'), BK[1][:, 0:64], [bkey(1)], ['thr_o'])
        mm(BK[6][:, 0:64], ones_f[0:4, :], Gd_o.rearrange('p a b c -> p (a b c)'), True, True,
           ['ones_f'] + [('Gd_o', d_, n_) for d_ in range(2) for n_ in ('A', 'B', 'S')], [bkey(6)])
        cp('dve', gbc_o.rearrange('p a b c -> p (a b c)'), BK[6][:, 0:64], [bkey(6)], ['gbc_o'])
        A.release(m_mlp)

        P.phase = 'P4b_chunks'
        A.report('P4b_chunks')
        vw_o = [A.alloc('vw_o%d' % i, [128, 4, 258], BF16) for i in range(3)]
        qg_o = [A.alloc('qg_o%d' % i, [128, 4, 128], BF16) for i in range(3)]
        ddt = [A.alloc('ddt%d' % i, [128, 8], F32) for i in range(3)]
        hn_b = [A.alloc('hn_b%d' % i, [128, 4, 256], BF16) for i in range(2)]
        h_ss = [A.alloc('h_ss%d' % i, [128, 12], F32) for i in range(2)]
        stg = [A.alloc('stg%d' % i, [128, 2, 128], F32) for i in range(2)]
        stg_all = [(stg[0], 'stg0'), (stg[1], 'stg1')]
        for i_ in range(2):
            flat = hn_b[i_].rearrange('p a b -> p (a b)')
            for half_ in range(2):
                v_ = flat[:, half_ * 512:(half_ + 1) * 512].bitcast(F32).rearrange('p (a b) -> p a b', b=128)
                stg_all.append((v_, 'hn_b%d' % i_))
        step_rr = [0]
        nb_rr = [0]
        cb_rr = [0]
        stg_rr = [0]

        def chunk_prep(di, c):
            i = step_rr[0] % 3
            step_rr[0] += 1
            vw, vk = vw_o[i], 'vw_o%d' % i
            qg, qk = qg_o[i], 'qg_o%d' % i
            tt('pool', vw, Vx[:, c], wtok_o[:, c, di, :].unsqueeze(2).to_broadcast([128, 4, 258]), ALU.mult,
               [('Vx', 'x'), ('Vx', c, 0), ('Vx', c, 1), 'wtok_o'], [vk])
            tt('pool', qg, mqT[:, :, c * 128:(c + 1) * 128], gbc_o[:, di, c, :].unsqueeze(2).to_broadcast([128, 4, 128]),
               ALU.mult, [('mqT', h_, c // 4) for h_ in range(4)] + ['gbc_o'], [qk])
            return i

        def chunk_step(i, Cst_, Cb_, sk, di, c, hacc, hk_, tl, first):
            vw, vk = vw_o[i], 'vw_o%d' % i
            qg, qk = qg_o[i], 'qg_o%d' % i
            dd, dk = ddt[i], 'ddt%d' % i
            db = 6 + (i % 2)
            for hh in range(4):
                mm(BK[db][:, hh:hh + 1], Sm[di][:, c, hh, :], vw[:, hh, 256:257], True, False, [('Sm%d' % di, c), vk], [bkey(db)])
                mm(BK[db][:, hh:hh + 1], qg[:, hh, :], Cb_[:, hh, 256:257], False, True, [qk, (sk + 'b', hh)], [bkey(db)])
            act(dd[:, 0:4], BK[db][:, 0:4], AF.Abs, [bkey(db)], [(dk, 0)])
            tt('dve', dd[:, 0:4], dd[:, 0:4], thr_o[:, c, di, :], ALU.max, [(dk, 0), 'thr_o'], [(dk, 0)])
            recip(dd[:, 4:8], dd[:, 0:4], [(dk, 0)], [(dk, 1)])
            nbanks = []
            npar = nb_rr[0] % 2
            nb_rr[0] += 1
            for hh in range(4):
                bk = npar * 2 + hh // 2
                co = (hh % 2) * 256
                nbanks.append((bk, co))
                mm(BK[bk][:, co:co + 256], Sm[di][:, c, hh, :], vw[:, hh, 0:256], True, False, [('Sm%d' % di, c), vk],
                   [bkey(bk)])
                mm(BK[bk][:, co:co + 256], qg[:, hh, :], Cb_[:, hh, 0:256], False, True, [qk, (sk + 'b', hh)], [bkey(bk)])
            for hh in range(4):
                bk, co = nbanks[hh]
                if first:
                    act(hacc[:, tl, hh, :], BK[bk][:, co:co + 256], AF.Identity, [bkey(bk), (dk, 1)], [(hk_, tl, hh)],
                        scale=dd[:, 4 + hh:5 + hh])
                else:
                    stt(hacc[:, tl, hh, :], BK[bk][:, co:co + 256], dd[:, 4 + hh:5 + hh], hacc[:, tl, hh, :], ALU.mult,
                        ALU.add, [bkey(bk), (dk, 1), (hk_, tl, hh)], [(hk_, tl, hh)])
            for hh in range(4):
                bk = 4 + cb_rr[0] % 2
                cb_rr[0] += 1
                mm(BK[bk][:, 0:257], mk_tok[:, c, hh * 128:(hh + 1) * 128], vw[:, hh, 0:257], True, True,
                   [('mk_tok', c), vk], [bkey(bk)])
                stt(Cst_[:, hh, 0:257], Cst_[:, hh, 0:257], gbc_o[:, di, c, hh:hh + 1], BK[bk][:, 0:257], ALU.mult, ALU.add,
                    [(sk, hh), 'gbc_o', bkey(bk)], [(sk, hh)])
                cp('act', Cb_[:, hh, 0:257], Cst_[:, hh, 0:257], [(sk, hh)], [(sk + 'b', hh)])

        big_stg = [(xin[i_].rearrange('p (h a b) -> p h a b', h=4, a=2), ['xin%d' % i_]) for i_ in range(2)]

        def out_state(Cst_, sk, u, di):
            idx = u * 2 + di
            sg_, sgk = big_stg[stg_rr[0] % 2]
            stg_rr[0] += 1
            for hh in range(4):
                bk = 6 + hh % 2
                for vh in range(2):
                    tr(BK[bk][:, vh * 128:(vh + 1) * 128], Cst_[:, hh, vh * 128:(vh + 1) * 128], ident_f,
                       [(sk, hh), 'ident_f'], [bkey(bk)])
                cp('act' if hh % 2 == 0 else 'dve', sg_[:, hh], BK[bk][:, 0:256].rearrange('p (a b) -> p a b', b=128),
                   [bkey(bk)] + sgk, sgk)
            P.dma('sp', O['oC'][idx * 4:(idx + 1) * 4].rearrange('h (vh p) d -> p h vh d', p=128), sg_, reads=sgk,
                  writes=[('oC', idx)])
            final_keys.append(('oC', idx))
            P.dma('sp', O['on'][idx * 4:(idx + 1) * 4, :].rearrange('h d -> d h'), Cst_[:, :, 256],
                  reads=[(sk, h_) for h_ in range(4)], writes=[('on', idx)], allow_slow_non_contiguous=True)
            final_keys.append(('on', idx))

        def post_tile(hacc, hk_, tl, t):
            i = t % 2
            ss, sk_ = h_ss[i], 'h_ss%d' % i
            for hh in range(4):
                act(hn_b[i][:, hh, :], hacc[:, tl, hh, :], AF.Square, [(hk_, tl, hh)], ['hn_b%d' % i, (sk_, hh)],
                    accum=ss[:, hh:hh + 1])
            act(ss[:, 4:8], ss[:, 0:4], AF.Sqrt, [(sk_, h_) for h_ in range(4)] + ['eps_t'], [(sk_, 'sd')], bias=eps_t,
                scale=1.0 / 256)
            recip(ss[:, 8:12], ss[:, 4:8], [(sk_, 'sd')], [(sk_, 'r')])
            hn, hk = hn_b[i], 'hn_b%d' % i
            tt('dve', hn, hacc[:, tl], ss[:, 8:12].unsqueeze(2).to_broadcast([128, 4, 256]), ALU.mult,
               [(hk_, tl, h_) for h_ in range(4)] + [(sk_, 'r')], [hk])
            bk = i
            pv = BKb[bk].rearrange('p (a b) -> p a b', b=128)
            hn2 = hn.rearrange('p a b -> p (a b)')
            for ch in range(KC):
                tr(pv[:, ch, :], hn2[:, ch * 128:(ch + 1) * 128], ident_b, [hk, 'ident_b'], [bkey(bk)])
            tt('dve', hmT[:, :, t * 128:(t + 1) * 128], pv, sgT[:, :, t * 128:(t + 1) * 128], ALU.mult,
               [bkey(bk)] + [('sgT', fc, t // 4) for fc in range(8)], [('hmT', t)])

        for grp in range(2):
            m_grp = A.mark()
            if grp == 1:
                womls = A.alloc('womls', [128, 8, 1024], BF16, at=ARENA_BYTES - 16384)
                P.dma('pool', womls, I['w_o_mlstm'].rearrange('(kc p) n -> p kc n', p=128), writes=['womls'])
            hacc = A.alloc('hacc%d' % grp, [128, 4, 4, 256], F32)
            if grp == 0:
                chains = [('A', 0, [0, 1], 0), ('A', 0, [1, 0], 1), ('B', 1, [2, 3], 0), ('B', 1, [3, 2], 1)]
            else:
                chains = [('S', 2, [4, 5, 6, 7], 0), ('S', 2, [7, 6, 5, 4], 1)]
            sts = []
            for ci, (nm, u, order, di) in enumerate(chains):
                sk = 'Cst_o%d_%d' % (grp, ci)
                Cst_ = A.alloc(sk, [128, 4, 258], F32)
                Cb_ = A.alloc(sk + 'b', [128, 4, 258], BF16)
                if nm == 'S':
                    cp('dve', Cst_.rearrange('p a b -> p (a b)'), Cinit[di].rearrange('p a b -> p (a b)'), ['Cinit%d' % di],
                       [(sk, h_) for h_ in range(4)])
                    cp('act', Cb_.rearrange('p a b -> p (a b)'), Cinit[di].rearrange('p a b -> p (a b)'), ['Cinit%d' % di],
                       [(sk + 'b', h_) for h_ in range(4)])
                else:
                    memset('dve', Cst_, 0.0, [(sk, h_) for h_ in range(4)])
                    memset('pool', Cb_, 0.0, [(sk + 'b', h_) for h_ in range(4)])
                sts.append((Cst_, Cb_, sk))
            seen = set()
            nstep = len(chains[0][2])
            steps = []
            for s_ in range(nstep):
                for ci, (nm, u, order, di) in enumerate(chains):
                    c = order[s_]
                    first = c not in seen
                    seen.add(c)
                    steps.append((ci, di, c, c - grp * 4, first))
            P.phase = 'P4b_g%d_steps' % grp
            nxt = chunk_prep(steps[0][1], steps[0][2])
            for k_, (ci, di, c, tl, first) in enumerate(steps):
                cur = nxt
                if k_ + 1 < len(steps):
                    nxt = chunk_prep(steps[k_ + 1][1], steps[k_ + 1][2])
                chunk_step(cur, sts[ci][0], sts[ci][1], sts[ci][2], di, c, hacc, 'hacc%d' % grp, tl, first)
            P.phase = 'P4b_g%d_post' % grp
            for tl in range(4):
                if grp == 0:
                    nm, u, order, di = chains[tl]
                    out_state(sts[tl][0], sts[tl][2], u, di)
                post_tile(hacc, 'hacc%d' % grp, tl, grp * 4 + tl)
            A.release(m_grp)
        hmT_keys = [('hmT', t) for t in range(8)]
        if 'hmT' in DBG:
            A.release(m_ml)
            hdbg2 = A.alloc('hdbg2', [128, 8, 1024], F32)
            cp('dve', hdbg2, hmT, hmT_keys, ['hdbg2'])
            dbg('hmT', hdbg2.rearrange('p a b -> p (a b)'), ['hdbg2'])
        else:
            A.release(m_ml)
        if stop_after == 'mlstm':
            return finish()

        P.phase = 'P5_merge'
        A.report('P5_merge')
        hole0 = A.off_of('Cinit0')
        x1 = A.alloc('x1', [128, 8, 1024], F32)
        m_mg = A.mark()
        mixT = A.alloc('mixT', [128, 8, 1024], BF16)
        wout = A.alloc('wout', [128, 8, 1024], BF16)
        m_mg2 = A.mark()
        for kc in range(KC):
            ts('dve', womls[:, kc, :], womls[:, kc, :], gmlT[:, kc:kc + 1], None, ALU.mult, None, ['womls', 'gmlT'], ['womls'])
        womla = load_w('womla', [128, 4, 1024], I['w_o_mla'].rearrange('(kc p) n -> p kc n', p=128))
        wbr = [A.alloc('wbr%d' % i, [128, 8, 2, 512], BF16) for i in range(2)]

        def load_wbr(sl):
            for ab in range(2):
                P.dma('pool', wbr[sl % 2][:, :, ab, :], w_in_v[:, :, O_BR + ab * 1024 + sl * 512:O_BR + ab * 1024 + (sl + 1) * 512],
                      writes=[('wbr%d' % (sl % 2), ab)])
        load_wbr(0)
        load_wbr(1)
        P.dma('pool', wout, I['w_out'].rearrange('(kc p) n -> p kc n', p=128), writes=['wout'])
        sgt = [A.alloc('sgt%d' % i, [128, 512], F32) for i in range(4)]
        n_u = 0
        for fc in range(8):
            sl = fc // 4
            wb, wbk = wbr[sl % 2], 'wbr%d' % (sl % 2)
            for half in range(2):
                pbase = (n_u % 2) * 4
                n_u += 1
                ts_ = slice(half * 512, (half + 1) * 512)
                for kc in range(4):
                    mm(BK[pbase], womla[:, kc, fc * 128:(fc + 1) * 128], attT[:, kc, ts_], kc == 0, kc == 3,
                       ['womla'] + attT_keys, [bkey(pbase)])
                for kc in range(KC):
                    mm(BK[pbase + 1], womls[:, kc, fc * 128:(fc + 1) * 128], hmT[:, kc, ts_], kc == 0, kc == KC - 1,
                       ['womls'] + hmT_keys, [bkey(pbase + 1)])
                for ab in range(2):
                    for kc in range(KC):
                        mm(BK[pbase + 2 + ab], wb[:, kc, ab, (fc % 4) * 128:(fc % 4 + 1) * 128], hTo[:, kc, ts_], kc == 0,
                           kc == KC - 1, [(wbk, ab)] + hTo_half(half), [bkey(pbase + 2 + ab)])
                sa, sak = sgt[(n_u % 2) * 2], 'sgt%d' % ((n_u % 2) * 2)
                sb_, sbk = sgt[(n_u % 2) * 2 + 1], 'sgt%d' % ((n_u % 2) * 2 + 1)
                act(sa, BK[pbase + 2], AF.Sigmoid, [bkey(pbase + 2)], [sak])
                act(sb_, BK[pbase + 3], AF.Sigmoid, [bkey(pbase + 3)], [sbk])
                tt('dve', sa, sa, BK[pbase], ALU.mult, [sak, bkey(pbase)], [sak])
                tt('dve', sb_, sb_, BK[pbase + 1], ALU.mult, [sbk, bkey(pbase + 1)], [sbk])
                tt('dve', mixT[:, fc, ts_], sa, sb_, ALU.add, [sak, sbk], [('mixT', fc, half)])
        A.release(m_mg2)
        A.free_fixed('womls')
        gt1 = build_gt(1, 2, [6, 7])
        mixT_keys = [('mixT', fc, half) for fc in range(8) for half in range(2)]
        rtmp = [A.alloc('rtmp%d' % i, [128, 512], F32) for i in range(2)]
        n_r = 0
        for t in range(8):
            c = 0 if t < 4 else 1
            xt, xk = load_x(I['xo'][t * 128:(t + 1) * 128, :])
            for half in range(2):
                bk = n_r % 4
                cs_ = slice(half * 512, (half + 1) * 512)
                for kc in range(KC):
                    mm(BK[bk], mixT[:, kc, t * 128:(t + 1) * 128], wout[:, kc, cs_], kc == 0, kc == KC - 1,
                       ['wout', ('mixT', kc, t // 4)], [bkey(bk)])
                rt, rk = rtmp[n_r % 2], 'rtmp%d' % (n_r % 2)
                n_r += 1
                tt('dve', rt, BK[bk], gt1[c][:, cs_], ALU.mult, [bkey(bk), ('gt1_%d' % c, half)], [rk])
                tt('dve', x1[:, t, cs_], rt, xt[:, cs_], ALU.add, [rk, xk], [('x1', t, half)])
        A.release(m_mg)
        if 'x1' in DBG:
            dbg('x1', x1.rearrange('p a b -> p (a b)'), [('x1', t, h_) for t in range(8) for h_ in range(2)])
        if stop_after == 'merge':
            return finish()

        P.phase = 'P6_ffn'
        A.report('P6_ffn')
        A.open_hole(hole0, A.off_of('x1'))
        wfi = [A.alloc('wfi%d' % i, [128, 8, 2, 256], BF16, hole=True) for i in range(2)]
        gt2 = build_gt(2, 5, [6, 7], hole=True)
        ftmp = [A.alloc('ftmp%d' % i, [128, 512], BF16, hole=True) for i in range(2)]
        rtmp2 = [A.alloc('rtmp2_%d' % i, [128, 512], F32, hole=True) for i in range(2)]
        gT = A.alloc('gT', [128, 22, 1024], BF16)
        wfo = A.alloc('wfo', [128, 22, 1024], BF16)
        w_fi_v = I['w_ffn_in'].rearrange('(kc p) n -> p kc n', p=128)

        def load_wfi(sl):
            for au in range(2):
                P.dma('pool', wfi[sl % 2][:, :, au, :], w_fi_v[:, :, au * FFN + sl * 256:au * FFN + (sl + 1) * 256],
                      writes=[('wfi%d' % (sl % 2), au)])
        load_wfi(0)
        load_wfi(1)
        wfo_v = I['w_ffn_out'].rearrange('(kc p) n -> p kc n', p=128)
        for q_ in range(2):
            P.dma('pool', wfo[:, q_ * 11:(q_ + 1) * 11, :], wfo_v[:, q_ * 11:(q_ + 1) * 11, :], writes=[('wfo', q_)])
        norm_pipeline([((lambda t=t: (x1[:, t, :], [('x1', t, 0), ('x1', t, 1)])),
                        (A2, B2, 0 if t < 4 else 1, hTo[:, :, t * 128:(t + 1) * 128], ('hTo', t), 'A2', ('modT', 1), [(4, 6), (5, 7)]))
                       for t in range(8)])
        n_f = 0
        for sl in range(11):
            wf, wfk = wfi[sl % 2], 'wfi%d' % (sl % 2)
            for f2 in range(2):
                fc = sl * 2 + f2
                for half in range(2):
                    ba, bu = (n_f % 2) * 2, (n_f % 2) * 2 + 1
                    ts_ = slice(half * 512, (half + 1) * 512)
                    for au, bk in ((0, ba), (1, bu)):
                        for kc in range(KC):
                            mm(BK[bk], wf[:, kc, au, f2 * 128:(f2 + 1) * 128], hTo[:, kc, ts_], kc == 0, kc == KC - 1,
                               [(wfk, au)] + hTo_half(half), [bkey(bk)])
                    ft, fk = ftmp[n_f % 2], 'ftmp%d' % (n_f % 2)
                    n_f += 1
                    act(ft, BK[ba], AF.Silu, [bkey(ba)], [fk])
                    tt('dve', gT[:, fc, ts_], ft, BK[bu], ALU.mult, [fk, bkey(bu)], [('gT', fc, half)])
            if sl + 2 < 11:
                load_wfi(sl + 2)
        gfin_bc = A.alloc('gfin_bc', [128, 1024], F32)
        P.dma('sp', gfin_bc, I['gfin'].partition_broadcast(128), writes=['gfin_bc'])
        ybuf = xin
        f_ss = [A.alloc('f_ss%d' % i, [128, 4], F32, hole=True) for i in range(2)]
        fjunk = nrm_xn[0]
        n_r = 0
        for t in range(8):
            c = 0 if t < 4 else 1
            for half in range(2):
                bk = 4 + n_r % 4
                cs_ = slice(half * 512, (half + 1) * 512)
                for fc in range(22):
                    mm(BK[bk], gT[:, fc, t * 128:(t + 1) * 128], wfo[:, fc, cs_], fc == 0, fc == 21,
                       [('wfo', fc // 11), ('gT', fc, t // 4)], [bkey(bk)])
                rt, rk = rtmp2[n_r % 2], 'rtmp2_%d' % (n_r % 2)
                n_r += 1
                tt('dve', rt, BK[bk], gt2[c][:, cs_], ALU.mult, [bkey(bk), ('gt2_%d' % c, half)], [rk])
                tt('dve', x1[:, t, cs_], rt, x1[:, t, cs_], ALU.add, [rk, ('x1', t, half)], [('x1', t, half)])
            i = t % 2
            ss, sk_ = f_ss[i], 'f_ss%d' % i
            xk2 = [('x1', t, 0), ('x1', t, 1)]
            act(fjunk, x1[:, t, :], AF.Square, xk2, ['nrm_xn0', (sk_, 0)], accum=ss[:, 0:1])
            act(ss[:, 1:2], ss[:, 0:1], AF.Sqrt, [(sk_, 0), 'eps_t'], [(sk_, 1)], bias=eps_t, scale=1.0 / D)
            recip(ss[:, 2:3], ss[:, 1:2], [(sk_, 1)], [(sk_, 2)])
            yb, yk = ybuf[i], 'xin%d' % i
            stt(yb, x1[:, t, :], ss[:, 2:3], gfin_bc, ALU.mult, ALU.mult, xk2 + [(sk_, 2), 'gfin_bc'], [yk])
            P.dma('sp', O['y'][t * 128:(t + 1) * 128, :], yb, reads=[yk], writes=[('y', t)])
            final_keys.append(('y', t))


        return finish()


def _rope_tables():
    half = 16
    inv = 10000.0 ** (-np.arange(0, half, 2, dtype=np.float64) / half)
    t = np.arange(2048)
    r = (t // 64).astype(np.float64)
    col = (t % 64).astype(np.float64)
    ang_r = r[None, :] * inv[:, None]
    ang_c = col[None, :] * inv[:, None]
    ang = np.concatenate([ang_r, ang_r, ang_c, ang_c], 0)
    return np.cos(ang).astype(np.float32), np.sin(ang).astype(np.float32)


def make_in_maps(inp):
    f = lambda a: np.ascontiguousarray(a, dtype=np.float32)
    cosT, sinT = _rope_tables()
    shared = {
        'b_modT': f(inp['b_mod'][0].reshape(48, 128).T),
        'gmixT': f(inp['g_norm_mix'][0].reshape(8, 128).T),
        'gffnT': f(inp['g_norm_ffn'][0].reshape(8, 128).T),
        'gqT': f(inp['g_q_norm'][0].reshape(3, 128).T),
        'gmlT': f(inp['g_mlstm_norm'][0].reshape(8, 128).T),
        'gkv': f(inp['g_kv_norm'][0].reshape(1, 256)),
        'gfin': f(inp['g_final'].reshape(1, 1024)),
        'bgate': f(inp['b_gates'][0].reshape(16, 1)),
        'w_mod': f(inp['w_mod'][0]), 'w_in': f(inp['w_in'][0]), 'w_uq': f(inp['w_uq'][0]),
        'w_ukv': f(inp['w_ukv'][0]), 'w_o_mla': f(inp['w_o_mla'][0]), 'w_o_mlstm': f(inp['w_o_mlstm'][0]),
        'w_out': f(inp['w_out'][0]), 'w_ffn_in': f(inp['w_ffn_in'][0]), 'w_ffn_out': f(inp['w_ffn_out'][0]),
    }
    maps = []
    for core in range(8):
        b, j = core // 4, core % 4
        m = dict(shared)
        xo = np.concatenate([inp['x_prompt'][2 * core], inp['x_prompt'][2 * core + 1],
                             inp['x_sample'][b, j * 512:(j + 1) * 512]], 0)
        m['xo'] = f(xo)
        chunks = list(range(0, 4 * j)) + list(range(15, 4 * j + 3, -1))
        assert len(chunks) == 12
        tok = np.concatenate([np.arange(c * 128, (c + 1) * 128) for c in chunks])
        m['xs'] = f(inp['x_sample'][b][tok])
        m['cos_s'] = f(cosT[:, tok])
        m['sin_s'] = f(sinT[:, tok])
        dsl = np.zeros((4, 12), np.float32)
        dsl[:, :4 * j] = 1.0
        m['dsl'] = dsl
        m['dmask'] = f(np.repeat(dsl, 128, axis=1))
        cond = np.stack([inp['c_ctx'], inp['c'][b]], 0)
        m['condT'] = f(cond.reshape(2, 8, 128).transpose(2, 1, 0).reshape(128, 16))
        m['cckv'] = f(inp['cache_ckv'][b, 0])
        m['ckro'] = f(inp['cache_krope'][b, 0])
        m['stC'] = f(inp['state_C'][b, 0].reshape(8, 256, 128))
        m['stnT'] = f(inp['state_n'][b, 0].reshape(8, 128).T)
        m['stm4'] = f(inp['state_m'][b, 0].T)
        s = np.zeros((128, 4), np.float32)
        s[:, j] = 1.0
        m['sel'] = s
        m['cos_o'] = f(cosT[:, j * 512:(j + 1) * 512])
        m['sin_o'] = f(sinT[:, j * 512:(j + 1) * 512])
        maps.append(m)
    return maps


_NC_CACHE = {}


def kernel(**inp):
    inp = {k: np.asarray(v) for k, v in inp.items()}
    if 'nc' not in _NC_CACHE:
        _NC_CACHE['nc'] = build_program()
    nc = _NC_CACHE['nc']
    maps = make_in_maps(inp)
    res = run_bass_kernel_spmd(nc, maps, core_ids=list(range(8)))
    R = res.results
    y_prompt = np.zeros((16, 256, 1024), np.float32)
    y_sample = np.zeros((2, 2048, 1024), np.float32)
    new_ckv = np.zeros((16, 1, 256, 256), np.float32)
    new_krope = np.zeros((16, 1, 256, 32), np.float32)
    new_C = np.zeros((16, 1, 2, 4, 256, 128), np.float32)
    new_n = np.zeros((16, 1, 2, 4, 128), np.float32)
    new_m = np.zeros((16, 1, 2, 4), np.float32)
    for core in range(8):
        r = R[core]
        b, j = core // 4, core % 4
        y = r['y']
        y_prompt[2 * core] = y[0:256]
        y_prompt[2 * core + 1] = y[256:512]
        y_sample[b, j * 512:(j + 1) * 512] = y[512:1024]
        new_ckv[2 * core, 0] = r['ockv'][0:256]
        new_ckv[2 * core + 1, 0] = r['ockv'][256:512]
        new_krope[2 * core, 0] = r['okr'][0:256]
        new_krope[2 * core + 1, 0] = r['okr'][256:512]
        oC = r['oC'].reshape(2, 2, 4, 256, 128)
        on = r['on'].reshape(2, 2, 4, 128)
        om = r['om'].T.reshape(2, 2, 4)
        for u in range(2):
            new_C[2 * core + u, 0] = oC[u]
            new_n[2 * core + u, 0] = on[u]
            new_m[2 * core + u, 0] = om[u]
    return (y_prompt, y_sample, new_ckv, new_krope, new_C, new_n, new_m)
```

```python
import contextlib
import numpy as np
import concourse.bass as bass
import concourse.mybir as mybir
from concourse.bass_utils import run_bass_kernel_spmd

F32 = mybir.dt.float32
BF16 = mybir.dt.bfloat16
ALU = mybir.AluOpType
AF = mybir.ActivationFunctionType
AX = mybir.AxisListType

N_DMA_SEMS = 8
COMPUTE = ('pe', 'act', 'dve', 'pool')

D = 1024
KC = 8
FFN = 2816
EPS = 1e-6
MLA_SCALE = 96 ** -0.5
O_Q, O_KV, O_KPE, O_MQ, O_MK, O_MV, O_G, O_MO, O_BR = 0, 384, 640, 672, 1184, 1696, 2720, 2736, 3760
NKEY = 2816


class _Op:
    __slots__ = ('eng', 'fn', 'reads', 'writes', 'dma', 'idx', 'deps', 'signal',
                 'sem', 'sigval', 'prev_same_sem')

    def __init__(self, eng, fn, reads, writes, dma):
        self.eng = eng
        self.fn = fn
        self.reads = reads
        self.writes = writes
        self.dma = dma
        self.deps = set()
        self.signal = False
        self.sem = None
        self.sigval = 0
        self.prev_same_sem = None


def _tile_of(k):
    return k[0] if isinstance(k, tuple) else k


class Prog:
    def __init__(self):
        self.ops = []
        self.last_write = {}
        self.readers = {}
        self.inherit = {}
        self.keys_of = {}
        self.phase = 'init'
        self.filler = None
        self.fill_every = 2
        self._fill_cnt = 0
        self._in_fill = False
        self.op_phase = []
        self.ins_phase = {}

    EXPAND = ('hTo', 'hTs')

    def _expand(self, keys):
        out = []
        for k in keys:
            if isinstance(k, tuple) and len(k) == 2 and k[0] in self.EXPAND:
                out.extend((k[0], k[1], ch) for ch in range(8))
            else:
                out.append(k)
        return tuple(out)

    def add(self, eng, fn, reads=(), writes=(), dma=False):
        op = _Op(eng, fn, self._expand(reads), self._expand(writes), dma)
        op.idx = len(self.ops)
        deps = set()
        for k in op.reads + op.writes:
            t = _tile_of(k)
            ks = self.keys_of.setdefault(t, set())
            if k not in ks:
                ks.add(k)
                inh = self.inherit.get(t)
                if inh:
                    deps |= inh
        for k in op.reads:
            w = self.last_write.get(k)
            if w is not None:
                deps.add(w)
        for k in op.writes:
            w = self.last_write.get(k)
            if w is not None:
                deps.add(w)
            for r in self.readers.get(k, {}).values():
                deps.add(r)
        deps.discard(op.idx)
        op.deps = deps
        rkey = ('dma', op.idx) if dma else eng
        for k in op.reads:
            self.readers.setdefault(k, {})[rkey] = op.idx
        for k in op.writes:
            self.last_write[k] = op.idx
            self.readers[k] = {}
        self.ops.append(op)
        self.op_phase.append(self.phase)
        if self.filler and not self._in_fill:
            self._fill_cnt += 1
            if self._fill_cnt % self.fill_every == 0:
                self._in_fill = True
                try:
                    self.filler.pop(0)()
                finally:
                    self._in_fill = False
        return op

    def accessors(self, tile):
        s = set()
        for k in self.keys_of.get(tile, ()):
            w = self.last_write.get(k)
            if w is not None:
                s.add(w)
            s.update(self.readers.get(k, {}).values())
        return s

    def pe(self, fn, reads=(), writes=()):
        return self.add('pe', fn, reads, writes)

    def act(self, fn, reads=(), writes=()):
        return self.add('act', fn, reads, writes)

    def dve(self, fn, reads=(), writes=()):
        return self.add('dve', fn, reads, writes)

    def pool(self, fn, reads=(), writes=()):
        return self.add('pool', fn, reads, writes)

    def dma(self, q, out, in_, reads=(), writes=(), **kw):
        return self.add(q, lambda e: e.dma_start(out=out, in_=in_, **kw), reads, writes, dma=True)

    def emit(self, nc, final_keys=()):
        ops = self.ops
        self.add('sp', None, reads=tuple(final_keys), writes=())
        for op in ops:
            for d in op.deps:
                p = ops[d]
                if p.eng == 'pe' and op.eng == 'pe' and not p.dma and not op.dma:
                    continue
                p.signal = True
        with contextlib.ExitStack() as st:
            csem = {e: st.enter_context(nc.semaphore('s_' + e)) for e in COMPUTE}
            dsem = {q: [st.enter_context(nc.semaphore('d_%s%d' % (q, i))) for i in range(N_DMA_SEMS)]
                    for q in ('sp', 'act', 'pool')}
            ccount = {e: 0 for e in COMPUTE}
            dcount = {q: [0] * N_DMA_SEMS for q in dsem}
            dlast = {q: [None] * N_DMA_SEMS for q in dsem}
            drr = {q: 0 for q in dsem}
            for op in ops:
                if op.dma:
                    q = op.eng
                    i = drr[q] % N_DMA_SEMS
                    drr[q] += 1
                    op.sem = dsem[q][i]
                    op.prev_same_sem = dlast[q][i]
                    dcount[q][i] += 16
                    op.sigval = dcount[q][i]
                    dlast[q][i] = op.idx
                    op.signal = True
                elif op.signal:
                    ccount[op.eng] += 1
                    op.sem = csem[op.eng]
                    op.sigval = ccount[op.eng]
            assert max(ccount.values()) < 60000, ccount
            by_eng = {e: [] for e in ('pe', 'act', 'dve', 'pool', 'sp')}
            for op in ops:
                by_eng[op.eng].append(op)

            def run(engname, e):
                known = {}
                for op in by_eng[engname]:
                    need = {}
                    deps = op.deps
                    if op.dma and op.prev_same_sem is not None:
                        deps = set(deps)
                        deps.add(op.prev_same_sem)
                    for d in deps:
                        p = ops[d]
                        if (not p.dma) and (not op.dma) and p.eng == 'pe' and op.eng == 'pe':
                            continue
                        key = id(p.sem)
                        if known.get(key, 0) >= p.sigval:
                            continue
                        if key not in need or need[key][1] < p.sigval:
                            need[key] = (p.sem, p.sigval)
                    for key, (sem, val) in need.items():
                        e.wait_ge(sem, val)
                        known[key] = val
                    if op.fn is None:
                        continue
                    ins = op.fn(e)
                    try:
                        self.ins_phase[ins.ins.name] = self.op_phase[op.idx]
                    except Exception:
                        pass
                    if op.signal:
                        ins.then_inc(op.sem, 16 if op.dma else 1)

            with nc.Block() as block:
                @block.tensor
                def _(e):
                    run('pe', e)

                @block.scalar
                def _(e):
                    run('act', e)

                @block.vector
                def _(e):
                    run('dve', e)

                @block.gpsimd
                def _(e):
                    run('pool', e)

                @block.sync
                def _(e):
                    run('sp', e)
        return ccount


class Arena:
    def __init__(self, P, ap_bf16, nbytes):
        self.P = P
        self.base = ap_bf16
        self.nbytes = nbytes
        self.off = 0
        self.live = []
        self.freed = []
        self.peak = 0
        self.names = set()
        self.fixed = {}

    def free_fixed(self, name):
        s, e = self.fixed.pop(name)
        deps = self.P.accessors(name) | self.P.inherit.get(name, set())
        self.freed.append((s, e, deps))

    def off_of(self, name):
        for (n, s, e) in self.live:
            if n == name:
                return s
        raise KeyError(name)

    def open_hole(self, start, end):
        keep = []
        for (name, s, e) in self.live:
            if s >= start and e <= end:
                deps = self.P.accessors(name) | self.P.inherit.get(name, set())
                self.freed.append((s, e, deps))
            else:
                keep.append((name, s, e))
        self.live = keep
        self.hole_off = start
        self.hole_end = end

    def alloc(self, name, shape, dtype, hole=False, at=None):
        assert name not in self.names, name
        self.names.add(name)
        esz = 4 if dtype == F32 else 2
        n = 1
        for s in shape[1:]:
            n *= s
        nb = (n * esz + 63) // 64 * 64
        if at is not None:
            start = at
            end = start + nb
            assert end <= self.nbytes and start >= self.off, (name, start, end, self.off)
            self.fixed[name] = (start, end)
        elif hole:
            start = self.hole_off
            end = start + nb
            assert end <= self.hole_end, (name, end, self.hole_end)
            self.hole_off = end
        else:
            start = self.off
            end = start + nb
            assert end <= self.nbytes, (name, end, self.nbytes)
            for fn_, (fs_, fe_) in self.fixed.items():
                assert end <= fs_ or start >= fe_, ('stack alloc overlaps fixed tile', name, fn_, start, end, fs_, fe_)
            self.off = end
            self.peak = max(self.peak, end)
        v = self.base[0:shape[0], start // 2:(start + n * esz) // 2]
        if dtype == F32:
            v = v.bitcast(F32)
        if len(shape) > 2:
            letters = 'abcdefg'[:len(shape) - 1]
            pat = 'p (' + ' '.join(letters) + ') -> p ' + ' '.join(letters)
            kw = {l: s for l, s in zip(letters, shape[1:])}
            v = v.rearrange(pat, **kw)
        inh = set()
        keep = []
        for (s, e, deps) in self.freed:
            if s < end and start < e:
                inh |= deps
            keep.append((s, e, deps))
        if inh:
            self.P.inherit[name] = inh
        if not hole and at is None:
            self.live.append((name, start, end))
        return v

    def mark(self):
        return (self.off, len(self.live))

    def report(self, tag):
        print('  arena', tag, 'off', self.off, 'peak', self.peak)

    def release(self, mark):
        off, nlive = mark
        for (name, s, e) in self.live[nlive:]:
            deps = self.P.accessors(name) | self.P.inherit.get(name, set())
            self.freed = [(a, b, d) for (a, b, d) in self.freed if not (a >= s and b <= e)]
            self.freed.append((s, e, deps))
        self.live = self.live[:nlive]
        self.off = off


IN_SPECS = [
    ('xo', [1024, 1024]), ('xs', [1536, 1024]), ('condT', [128, 16]), ('b_modT', [128, 48]),
    ('gmixT', [128, 8]), ('gffnT', [128, 8]), ('gqT', [128, 3]), ('gmlT', [128, 8]),
    ('gkv', [1, 256]), ('gfin', [1, 1024]), ('bgate', [16, 1]),
    ('cckv', [256, 256]), ('ckro', [256, 32]), ('stC', [8, 256, 128]), ('stnT', [128, 8]),
    ('stm4', [4, 2]), ('sel', [128, 4]),
    ('cos_s', [32, 1536]), ('sin_s', [32, 1536]), ('dmask', [4, 1536]), ('dsl', [4, 12]), ('cos_o', [32, 512]), ('sin_o', [32, 512]),
    ('w_mod', [1024, 6144]), ('w_in', [1024, 5808]), ('w_uq', [384, 768]), ('w_ukv', [256, 1024]),
    ('w_o_mla', [512, 1024]), ('w_o_mlstm', [1024, 1024]), ('w_out', [1024, 1024]),
    ('w_ffn_in', [1024, 5632]), ('w_ffn_out', [2816, 1024]),
]
OUT_SPECS = [
    ('y', [1024, 1024]), ('ockv', [512, 256]), ('okr', [512, 32]),
    ('oC', [16, 256, 128]), ('on', [16, 128]), ('om', [4, 4]),
]


def build_program(debug=None, stop_after=None):
    debug = debug or {}
    nc = bass.Bass("TRN2", target_bir_lowering=False)
    P = Prog()
    I = {n: nc.dram_tensor(n, s, F32, kind="ExternalInput").ap() for n, s in IN_SPECS}
    O = {n: nc.dram_tensor(n, s, F32, kind="ExternalOutput").ap() for n, s in OUT_SPECS}
    DBG = {n: nc.dram_tensor('dbg_' + n, list(s), F32, kind="ExternalOutput").ap() for n, s in debug.items()}
    final_keys = []

    with contextlib.ExitStack() as st:
        ARENA_BYTES = 207 * 1024
        arena_t = st.enter_context(nc.sbuf_tensor('arena', [128, ARENA_BYTES // 2], BF16))
        A = Arena(P, arena_t[:, :], ARENA_BYTES)
        banks = [st.enter_context(nc.psum_tensor('bank%d' % i, [128, 512], F32)) for i in range(8)]
        BK = [b[:, :] for b in banks]
        BKb = [b[:, :].bitcast(BF16) for b in banks]

        def bkey(i):
            return ('B', i)

        def mm(out, lhsT, rhs, start, stop, reads, writes):
            P.pe(lambda e: e.matmul(out, lhsT=lhsT, rhs=rhs, start=start, stop=stop), reads, writes)

        def tr(out, in_, ident, reads, writes):
            P.pe(lambda e: e.transpose(out=out, in_=in_, identity=ident), reads, writes)

        def act(out, in_, func, reads, writes, bias=None, scale=None, accum=None):
            kw = {}
            if bias is not None:
                kw['bias'] = bias
            if scale is not None:
                kw['scale'] = scale
            if accum is not None:
                kw['accum_out'] = accum
            P.act(lambda e: e.activation(out=out, in_=in_, func=func, **kw), reads, writes)

        def ts(eng, out, in0, s1, s2, op0, op1, reads, writes):
            if op1 is None:
                P.add(eng, lambda e: e.tensor_scalar(out=out, in0=in0, scalar1=s1, scalar2=None, op0=op0),
                      reads, writes)
            else:
                P.add(eng, lambda e: e.tensor_scalar(out=out, in0=in0, scalar1=s1, scalar2=s2, op0=op0, op1=op1),
                      reads, writes)

        def tt(eng, out, in0, in1, op, reads, writes):
            P.add(eng, lambda e: e.tensor_tensor(out=out, in0=in0, in1=in1, op=op), reads, writes)

        def stt(out, in0, scalar, in1, op0, op1, reads, writes):
            P.dve(lambda e: e.scalar_tensor_tensor(out=out, in0=in0, scalar=scalar, in1=in1, op0=op0, op1=op1),
                  reads, writes)

        def cp(eng, out, in_, reads, writes):
            if eng == 'act':
                P.act(lambda e: e.copy(out=out, in_=in_), reads, writes)
            else:
                P.add(eng, lambda e: e.tensor_copy(out=out, in_=in_), reads, writes)

        def memset(eng, ap, val, writes):
            P.add(eng, lambda e: e.memset(ap, val), (), writes)

        def recip(out, in_, reads, writes):
            P.dve(lambda e: e.reciprocal(out=out, in_=in_), reads, writes)

        def scan(out, d0, d1, initial, op0, op1, reads, writes):
            P.dve(lambda e: e.tensor_tensor_scan(out=out, data0=d0, data1=d1, initial=initial, op0=op0, op1=op1),
                  reads, writes)

        def dbg(name, ap, reads):
            if name in DBG:
                P.dma('sp', DBG[name], ap, reads=reads, writes=[('dbg', name)])
                final_keys.append(('dbg', name))

        def finish():
            cc = P.emit(nc, final_keys=final_keys)
            print('arena peak', A.peak, 'sig counts', cc, 'nops', len(P.ops))
            nc._phase_map = P.ins_phase
            return nc

        ones_f = A.alloc('ones_f', [128, 128], F32)
        ident_f = A.alloc('ident_f', [128, 128], F32)
        ident_b = A.alloc('ident_b', [128, 128], BF16)
        ones_b = A.alloc('ones_b', [128, 128], BF16)
        memset('pool', ones_f, 1.0, ['ones_f'])
        memset('pool', ones_b, 1.0, ['ones_b'])
        P.pool(lambda e: e.affine_select(out=ident_f, in_=ones_f, pattern=[[1, 128]], compare_op=ALU.is_equal,
                                         fill=0.0, base=0, channel_multiplier=-1), ['ones_f'], ['ident_f'])
        P.pool(lambda e: e.affine_select(out=ident_b, in_=ones_f, pattern=[[1, 128]], compare_op=ALU.is_equal,
                                         fill=0.0, base=0, channel_multiplier=-1), ['ones_f'], ['ident_b'])

        def load_small(name, shape, src, q='sp'):
            t = A.alloc(name, shape, F32)
            P.dma(q, t, src, writes=[name])
            return t

        condT = load_small('condT', [128, 16], I['condT'])
        b_modT = load_small('b_modT', [128, 48], I['b_modT'])
        gmixT = load_small('gmixT', [128, 8], I['gmixT'])
        gffnT = load_small('gffnT', [128, 8], I['gffnT'])
        gqT = load_small('gqT', [128, 3], I['gqT'])
        gmlT = load_small('gmlT', [128, 8], I['gmlT'])
        gkv_bc = load_small('gkv_bc', [128, 256], I['gkv'].partition_broadcast(128))
        bgate = load_small('bgate', [16, 1], I['bgate'])
        sel = load_small('sel', [128, 4], I['sel'])

        P.phase = 'P0_adaln'
        scT = A.alloc('scT', [128, 8, 2], BF16)
        act(scT, condT.rearrange('p (a b) -> p a b', b=2), AF.Silu, ['condT'], ['scT'])
        modT = A.alloc('modT', [128, 48, 2], F32)
        NWM = 6
        wm = [A.alloc('wm%d' % i, [128, 8, 512], BF16, at=154 * 1024 + i * 8192) for i in range(NWM)]
        w_mod_v = I['w_mod'].rearrange('(kc p) n -> p kc n', p=128)
        NG = 12

        def load_wm(g):
            P.dma('pool', wm[g % NWM], w_mod_v[:, :, g * 512:(g + 1) * 512], writes=[('wm%d' % (g % NWM),)])
        for g_ in range(NWM):
            load_wm(g_)
        pm = BK[0][:, 0:96].rearrange('p (a b) -> p a b', b=2)

        def mod_slab(g):
            for cc in range(4):
                i = g * 4 + cc
                for kc in range(KC):
                    mm(pm[:, i, :], wm[g % NWM][:, kc, cc * 128:(cc + 1) * 128], scT[:, kc, :], kc == 0, kc == KC - 1,
                       [('wm%d' % (g % NWM),), 'scT'], [bkey(0)])
            if g + NWM < NG and g >= 4:
                load_wm(g + NWM)
        for g in range(4):
            mod_slab(g)
        tt('dve', modT[:, 0:16, :], pm[:, 0:16, :], b_modT[:, 0:16].unsqueeze(2).to_broadcast([128, 16, 2]), ALU.add,
           [bkey(0), 'b_modT'], [('modT', 0)])
        A1 = A.alloc('A1', [128, 8, 2], F32)
        A2 = A.alloc('A2', [128, 8, 2], F32)
        ts('dve', A1, modT[:, 8:16, :], 1.0, None, ALU.add, None, [('modT', 0)], ['A1'])
        tt('dve', A1, A1, gmixT.unsqueeze(2).to_broadcast([128, 8, 2]), ALU.mult, ['A1', 'gmixT'], ['A1'])

        def finish_mod():
            tt('dve', modT[:, 16:48, :], pm[:, 16:48, :], b_modT[:, 16:48].unsqueeze(2).to_broadcast([128, 32, 2]), ALU.add,
               [bkey(0), 'b_modT'], [('modT', 1)])
            ts('dve', A2, modT[:, 32:40, :], 1.0, None, ALU.add, None, [('modT', 1)], ['A2'])
            tt('dve', A2, A2, gffnT.unsqueeze(2).to_broadcast([128, 8, 2]), ALU.mult, ['A2', 'gffnT'], ['A2'])
            for i_ in range(NWM):
                A.free_fixed('wm%d' % i_)
        B1 = modT[:, 0:8, :]
        B2 = modT[:, 24:32, :]
        def build_gt(gi, idx, banks2, hole=False):
            res = {}
            dg = [A.alloc('dg%d_%d' % (gi, i), [128, 128], F32, hole=hole) for i in range(2)]
            n_dg = 0
            bsel = 0
            for c in range(2):
                name = 'gt%d_%d' % (gi, c)
                t = A.alloc(name, [128, 1024], F32, hole=hole)
                res[c] = t
                for half in range(2):
                    bk = banks2[bsel % 2]
                    bsel += 1
                    for q4 in range(4):
                        ch = half * 4 + q4
                        d_ = dg[n_dg % 2]
                        dk = 'dg%d_%d' % (gi, n_dg % 2)
                        n_dg += 1
                        ts('dve', d_, ident_f, modT[:, idx * 8 + ch, c:c + 1], None, ALU.mult, None,
                           ['ident_f', ('modT', 1)], [dk])
                        mm(BK[bk][:, q4 * 128:(q4 + 1) * 128], ones_f, d_, True, True, ['ones_f', dk], [bkey(bk)])
                    cp('act', t[:, half * 512:(half + 1) * 512], BK[bk], [bkey(bk)], [(name, half)])
            return res

        nrm_xn = [A.alloc('nrm_xn%d' % i, [128, 1024], BF16) for i in range(2)]
        nrm_ss = [A.alloc('nrm_ss%d' % i, [128, 4], F32) for i in range(2)]
        nrm_cnt = [0]
        tr_bank = [0]

        def rstd_from_ss(ss, n, key):
            act(ss[:, 1:2], ss[:, 0:1], AF.Sqrt, [(key, 0)], [(key, 1)], bias=None, scale=1.0 / n)
            return

        eps_t = A.alloc('eps_t', [128, 1], F32)
        memset('pool', eps_t, EPS, ['eps_t'])
        zero_t = A.alloc('zero_t', [128, 1], F32)
        memset('pool', zero_t, 0.0, ['zero_t'])

        def norm_A(xt, xkeys):
            i = nrm_cnt[0] % 2
            nrm_cnt[0] += 1
            ss = nrm_ss[i]
            sk = 'nrm_ss%d' % i
            xn = nrm_xn[i]
            xk = 'nrm_xn%d' % i
            act(xn, xt, AF.Square, xkeys, [xk, (sk, 0)], accum=ss[:, 0:1])
            act(ss[:, 1:2], ss[:, 0:1], AF.Sqrt, [(sk, 0), 'eps_t'], [(sk, 1)], bias=eps_t, scale=1.0 / D)
            recip(ss[:, 2:3], ss[:, 1:2], [(sk, 1)], [(sk, 2)])
            ts('dve', xn, xt, ss[:, 2:3], None, ALU.mult, None, xkeys + [(sk, 2)], [xk])
            return xn, xk

        def norm_B(xn, xk, Aa, Bb, c, dst, dkey, AaKey, BbKey, banks_rr):
            bA, bB = banks_rr[tr_bank[0] % len(banks_rr)]
            tr_bank[0] += 1
            pA = BKb[bA].rearrange('p (a b) -> p a b', b=128)
            pB = BKb[bB].rearrange('p (a b) -> p a b', b=128)
            for ch in range(KC):
                pv, bk = (pA, bA) if ch < 4 else (pB, bB)
                tr(pv[:, ch % 4, :], xn[:, ch * 128:(ch + 1) * 128], ident_b, [xk, 'ident_b'], [bkey(bk)])
            for q in range(4):
                for ch in (q, 4 + q):
                    dk_ = (dkey[0], dkey[1], ch)
                    if ch < 4:
                        act(dst[:, ch, :], pA[:, ch, :], AF.Identity, [bkey(bA), AaKey, BbKey], [dk_],
                            bias=Bb[:, ch, c:c + 1], scale=Aa[:, ch, c:c + 1])
                    else:
                        ts('dve', dst[:, ch, :], pB[:, ch - 4, :], Aa[:, ch, c:c + 1], Bb[:, ch, c:c + 1], ALU.mult, ALU.add,
                           [bkey(bB), AaKey, BbKey], [dk_])

        def norm_pipeline(items, between=None):
            pend = None
            for i, (xf, argsB) in enumerate(items):
                xt, xkeys = xf()
                cur = norm_A(xt, xkeys)
                if pend is not None:
                    norm_B(*pend)
                    if between is not None:
                        between(i - 1)
                pend = cur + argsB
            norm_B(*pend)
            if between is not None:
                between(len(items) - 1)

        P.phase = 'P1_hTo'
        hTo = A.alloc('hTo', [128, 8, 1024], BF16)
        xin = [A.alloc('xin%d' % i, [128, 1024], F32) for i in range(3)]
        xin_cnt = [0]

        def load_x(src_rows):
            i = xin_cnt[0] % 3
            xin_cnt[0] += 1
            P.dma('sp', xin[i], src_rows, writes=['xin%d' % i])
            return xin[i], 'xin%d' % i

        pre_x = [load_x(I['xo'][t_ * 128:(t_ + 1) * 128, :]) for t_ in range(3)]
        if 'hTo' in DBG:
            hdbg = A.alloc('hdbg', [128, 8, 1024], F32)
            cp('dve', hdbg, hTo, [('hTo', t) for t in range(8)], ['hdbg'])
            dbg('hTo', hdbg.rearrange('p a b -> p (a b)'), ['hdbg'])

        w_in_v = I['w_in'].rearrange('(kc p) n -> p kc n', p=128)

        def load_w(name, shape, src, q='pool'):
            t = A.alloc(name, shape, BF16)
            P.dma(q, t, src, writes=[name])
            return t

        keepF = A.alloc('keepF', [4, 2048], BF16)
        keepB = A.alloc('keepB', [4, 2048], BF16)
        memset('pool', keepF, 1.0, ['keepF'])
        memset('pool', keepF.rearrange('p (c s) -> p c s', s=128)[:, :, 0:1], 0.0, ['keepF'])
        memset('pool', keepB, 1.0, ['keepB'])
        memset('pool', keepB.rearrange('p (c s) -> p c s', s=128)[:, :, 127:128], 0.0, ['keepB'])
        maskF = A.alloc('maskF', [128, 4, 128], BF16)
        maskB = A.alloc('maskB', [128, 4, 128], BF16)
        m_onesm = A.mark()
        onesm = A.alloc('onesm', [128, 4, 128], F32)
        memset('pool', onesm, 1.0, ['onesm'])
        P.pool(lambda e: e.affine_select(out=maskF, in_=onesm, pattern=[[0, 4], [1, 128]], compare_op=ALU.is_ge,
                                         fill=0.0, base=0, channel_multiplier=-1), ['onesm'], ['maskF'])
        P.pool(lambda e: e.affine_select(out=maskB, in_=onesm, pattern=[[0, 4], [-1, 128]], compare_op=ALU.is_ge,
                                         fill=0.0, base=0, channel_multiplier=1), ['onesm'], ['maskB'])
        A.release(m_onesm)
        I4 = ident_f[0:4, 0:4]
        stm4 = load_small('stm4', [4, 2], I['stm4'])
        stnT = load_small('stnT', [128, 8], I['stnT'])

        def m_chain(pref, d, amax, nbt, n, m0, rk, m0keys):
            rk = list(rk) + list(m0keys)
            mout = A.alloc(pref + 'mout', [4, n], F32)
            mprev = A.alloc(pref + 'mprev', [4, n], F32)
            Ml = A.alloc(pref + 'Ml', [4, n], F32)
            g = A.alloc(pref + 'g', [4, n], F32)
            k = lambda x: pref + x
            if d == 'f':
                scan(mout, amax, nbt, m0, ALU.max, ALU.subtract, rk, [k('mout')])
                first = mprev[:, 0:1]
                if n > 1:
                    cp('dve', mprev[:, 1:n], mout[:, 0:n - 1], [k('mout')], [(k('mprev'), 1)])
            else:
                scan(mout[:, ::-1], amax[:, ::-1], nbt[:, ::-1], m0, ALU.max, ALU.subtract, rk, [k('mout')])
                first = mprev[:, n - 1:n]
                if n > 1:
                    cp('dve', mprev[:, 0:n - 1], mout[:, 1:n], [k('mout')], [(k('mprev'), 1)])
            if isinstance(m0, float):
                memset('dve', first, m0, [(k('mprev'), 0)])
            else:
                cp('dve', first, m0, list(m0keys), [(k('mprev'), 0)])
            mpk = [(k('mprev'), 0), (k('mprev'), 1)] if n > 1 else [(k('mprev'), 0)]
            tt('dve', Ml, mprev, amax, ALU.max, mpk + rk, [k('Ml')])
            tt('dve', g, mprev, Ml, ALU.subtract, mpk + [k('Ml')], [k('g')])
            act(g, g, AF.Exp, [k('g')], [k('g')])
            return dict(mout=mout, mprev=mprev, Ml=Ml, g=g, pref=pref, mpk=mpk)

        def gate_dir(pref, di, ntok, G16, G16keys, chains, w_bank, thr_bank):
            d = 'fb'[di]
            nch = ntok // 128
            amax = A.alloc(pref + 'amax' + d, [4, nch], F32)
            nbt = A.alloc(pref + 'nbt' + d, [4, nch], F32)
            amk, nbk = pref + 'amax' + d, pref + 'nbt' + d
            out = {}
            chain_tiles = []
            mk = A.mark()
            for (name, c0, n, m0, m0keys) in chains:
                chain_tiles.append(None)
            A.release(mk)
            pre = pref + d + '_'
            ch_res = {}
            specs = []
            for (name, c0, n, m0, m0keys) in chains:
                specs.append((name, c0, n, m0, m0keys))
            pend = []
            for (name, c0, n, m0, m0keys) in specs:
                p_ = pre + name + '_'
                tiles = dict(mout=A.alloc(p_ + 'mout', [4, n], F32), mprev=A.alloc(p_ + 'mprev', [4, n], F32),
                             Ml=A.alloc(p_ + 'Ml', [4, n], F32), g=A.alloc(p_ + 'g', [4, n], F32))
                pend.append((p_, tiles))
            mrow = A.mark()
            ig = A.alloc(pre + 'ig', [4, ntok], F32)
            lf = A.alloc(pre + 'lf', [4, ntok], F32)
            bn = A.alloc(pre + 'bn', [4, ntok], F32)
            igk, lfk, bnk = pre + 'ig', pre + 'lf', pre + 'bn'
            P.dma('sp', ig, G16[di * 4:di * 4 + 4, :], reads=G16keys, writes=[igk])
            P.dma('sp', lf, G16[8 + di * 4:8 + di * 4 + 4, :], reads=G16keys, writes=[lfk])
            act(lf, lf, AF.Exp, [lfk], [lfk], scale=-1.0)
            act(lf, lf, AF.Ln, [lfk], [lfk], bias=1.0)
            if d == 'f':
                scan(bn, keepF[:, 0:ntok], lf, 0.0, ALU.mult, ALU.add, ['keepF', lfk], [bnk])
            else:
                scan(bn[:, ::-1], keepB[:, 0:ntok][:, ::-1], lf[:, ::-1], 0.0, ALU.mult, ALU.add, ['keepB', lfk], [bnk])
            tt('dve', ig, ig, bn, ALU.add, [igk, bnk], [igk])
            igv = ig.rearrange('p (c s) -> p c s', s=128)
            bnv = bn.rearrange('p (c s) -> p c s', s=128)
            P.dve(lambda e, o=amax, i=igv: e.tensor_reduce(out=o, in_=i, axis=AX.X, op=ALU.max), [igk], [amk])
            cp('dve', nbt, bnv[:, :, 127] if d == 'f' else bnv[:, :, 0], [bnk], [nbk])
            for (name, c0, n, m0, m0keys), (p_, tiles) in zip(specs, pend):
                k = lambda x, p_=p_: p_ + x
                mout, mprev, Ml, g = tiles['mout'], tiles['mprev'], tiles['Ml'], tiles['g']
                am, nb = amax[:, c0:c0 + n], nbt[:, c0:c0 + n]
                rk = [amk, nbk] + list(m0keys)
                if d == 'f':
                    scan(mout, am, nb, m0, ALU.max, ALU.subtract, rk, [k('mout')])
                    first = mprev[:, 0:1]
                    if n > 1:
                        cp('dve', mprev[:, 1:n], mout[:, 0:n - 1], [k('mout')], [(k('mprev'), 1)])
                else:
                    scan(mout[:, ::-1], am[:, ::-1], nb[:, ::-1], m0, ALU.max, ALU.subtract, rk, [k('mout')])
                    first = mprev[:, n - 1:n]
                    if n > 1:
                        cp('dve', mprev[:, 0:n - 1], mout[:, 1:n], [k('mout')], [(k('mprev'), 1)])
                if isinstance(m0, float):
                    memset('dve', first, m0, [(k('mprev'), 0)])
                else:
                    cp('dve', first, m0, list(m0keys), [(k('mprev'), 0)])
                mpk = [(k('mprev'), 0), (k('mprev'), 1)] if n > 1 else [(k('mprev'), 0)]
                tt('dve', Ml, mprev, am, ALU.max, mpk + [amk], [k('Ml')])
                tt('dve', g, mprev, Ml, ALU.subtract, mpk + [k('Ml')], [k('g')])
                act(g, g, AF.Exp, [k('g')], [k('g')])
                ch_res[name] = dict(mout=mout, mprev=mprev, Ml=Ml, g=g, pref=p_, mpk=mpk, c0=c0, n=n)
                i3 = ig[:, c0 * 128:(c0 + n) * 128].rearrange('p (c s) -> p c s', s=128)
                tt('dve', i3, i3, Ml.unsqueeze(2).to_broadcast([4, n, 128]), ALU.subtract, [igk, k('Ml')], [igk])
                if thr_bank is not None:
                    b3 = bn[:, c0 * 128:(c0 + n) * 128].rearrange('p (c s) -> p c s', s=128)
                    tt('dve', b3, b3, Ml.unsqueeze(2).to_broadcast([4, n, 128]), ALU.subtract, [bnk, k('Ml'), nbk], [bnk])
            act(ig, ig, AF.Exp, [igk], [igk])
            if thr_bank is not None:
                act(bn, bn, AF.Exp, [bnk], [bnk])
            for c in range(nch):
                col = (c * 2 + di) * 4
                tr(BK[w_bank][:, col:col + 4], ig[:, c * 128:(c + 1) * 128], I4, [igk, 'ident_f'], [bkey(w_bank)])
                if thr_bank is not None:
                    tr(BK[thr_bank][:, col:col + 4], bn[:, c * 128:(c + 1) * 128], I4, [bnk, 'ident_f'], [bkey(thr_bank)])
            A.release(mrow)
            return ch_res

        P.phase = 'P2a_hTs'
        Cinit = [A.alloc('Cinit%d' % d, [128, 4, 258], F32) for d in range(2)]
        minit = A.alloc('minit', [4, 2], F32)
        attT = A.alloc('attT', [128, 4, 1024], BF16)
        m_att = A.mark()
        ckvT = A.alloc('ckvT', [128, 2, NKEY], BF16)
        kpeR = A.alloc('kpeR', [96, NKEY], BF16)
        Vt = A.alloc('Vt', [128, 22, 512], BF16)
        wukv = load_w('wukv', [128, 2, 1024], I['w_ukv'].rearrange('(kc p) n -> p kc n', p=128))
        wkv_s = load_w('wkv_s', [128, 8, 256], w_in_v[:, :, O_KV:O_KV + 256])
        wkpe = A.alloc('wkpe', [128, 8, 96], BF16)
        wkpe2 = A.alloc('wkpe2', [128, 8, 96], BF16)
        memset('pool', wkpe, 0.0, ['wkpe'])
        memset('pool', wkpe2, 0.0, ['wkpe2'])
        P.dma('pool', wkpe[:, :, 64:96], w_in_v[:, :, O_KPE:O_KPE + 32], reads=['wkpe'], writes=['wkpe'])
        def build_wkpe2():
            for a in range(2):
                o = 64 + a * 16
                P.act(lambda e, d_=wkpe2[:, :, o:o + 8], s_=wkpe[:, :, o + 8:o + 16]: e.mul(out=d_, in_=s_, mul=-1.0),
                      ['wkpe', 'wkpe2'], ['wkpe2'])
                cp('dve', wkpe2[:, :, o + 8:o + 16], wkpe[:, :, o:o + 8], ['wkpe', 'wkpe2'], ['wkpe2'])
        hTs_keys = lambda t0, n: [('hTs', t) for t in range(t0, t0 + n)]
        cs_g = [A.alloc('cs_g%d' % i, [96, 2, 512], F32) for i in range(1)]
        kv_ss = [A.alloc('kv_ss%d' % i, [128, 4], F32) for i in range(2)]
        kv_junk = A.alloc('kv_junk', [128, 384], BF16)
        kv_bf = [A.alloc('kv_bf%d' % i, [128, 384], BF16) for i in range(2)]
        rp_t = [A.alloc('rp_t%d' % i, [96, 512], F32) for i in range(2)]
        kv_cnt = [0]
        eps_kv = eps_t

        def kside_p1(hT, hkey, tok0, out_rows=None, zb=(5, 6)):
            i = kv_cnt[0] % 2
            kv_cnt[0] += 1
            bk = zb[i]
            ss, sk = kv_ss[i], 'kv_ss%d' % i
            for kc in range(KC):
                mm(BK[bk][:, 0:256], hT[:, kc, tok0:tok0 + 128], wkv_s[:, kc, :], kc == 0, kc == KC - 1,
                   ['wkv_s', hkey], [bkey(bk)])
            act(kv_junk[:, 0:256], BK[bk][:, 0:256], AF.Square, [bkey(bk)], ['kv_junk', (sk, 0)], accum=ss[:, 0:1])
            act(ss[:, 1:2], ss[:, 0:1], AF.Sqrt, [(sk, 0), 'eps_t'], [(sk, 1)], bias=eps_kv, scale=1.0 / 256)
            recip(ss[:, 2:3], ss[:, 1:2], [(sk, 1)], [(sk, 2)])
            cb, cbk = kv_bf[i], 'kv_bf%d' % i
            if out_rows is not None:
                cf = A.alloc('ckv_f%d' % kv_cnt[0], [128, 256], F32)
                cfk = 'ckv_f%d' % kv_cnt[0]
                stt(cf, BK[bk][:, 0:256], ss[:, 2:3], gkv_bc, ALU.mult, ALU.mult, [bkey(bk), (sk, 2), 'gkv_bc'], [cfk])
                P.dma('sp', out_rows, cf, reads=[cfk], writes=[('ockv', kv_cnt[0])])
                final_keys.append(('ockv', kv_cnt[0]))
                cp('act', cb[:, 0:256], cf, [cfk], [cbk])
            else:
                stt(cb[:, 0:256], BK[bk][:, 0:256], ss[:, 2:3], gkv_bc, ALU.mult, ALU.mult,
                    [bkey(bk), (sk, 2), 'gkv_bc'], [cbk])
            return i

        def kside_p2(i, keycol0, tb=7):
            cb, cbk = kv_bf[i], 'kv_bf%d' % i
            pv = BKb[tb].rearrange('p (a b) -> p a b', b=128)
            for rc in range(2):
                tr(pv[:, rc, :], cb[:, rc * 128:(rc + 1) * 128], ident_b, [cbk, 'ident_b'], [bkey(tb)])
            cp('act', ckvT[:, :, keycol0:keycol0 + 128], pv[:, 0:2, :], [bkey(tb)], [('ckvT', keycol0 // 128)])

        def kside_tile(hT, hkey, tok0, keycol0, out_rows=None, zb=(5, 6), tb=7):
            i = kside_p1(hT, hkey, tok0, out_rows, zb)
            kside_p2(i, keycol0, tb)

        def kpe_group(grp):
            cg, cgk = cs_g[0], 'cs_g0'
            P.dma('sp', cg[64:96, 0, :], I['cos_s'][:, grp * 512:(grp + 1) * 512], writes=[(cgk, 0)])
            P.dma('sp', cg[64:96, 1, :], I['sin_s'][:, grp * 512:(grp + 1) * 512], writes=[(cgk, 1)])
            for w_, bk in ((wkpe, 5), (wkpe2, 6)):
                for kc in range(KC):
                    mm(BK[bk][0:96, :], w_[:, kc, :], hTs[:, kc, grp * 512:(grp + 1) * 512], kc == 0, kc == KC - 1,
                       ['wkpe', 'wkpe2'] + hTs_keys(grp * 4, 4), [bkey(bk)])
            tt('dve', rp_t[0][64:96, :], BK[5][64:96, :], cg[64:96, 0, :], ALU.mult, [bkey(5), (cgk, 0)], ['rp_t0'])
            tt('dve', rp_t[1][64:96, :], BK[6][64:96, :], cg[64:96, 1, :], ALU.mult, [bkey(6), (cgk, 1)], ['rp_t1'])
            tt('dve', kpeR[64:96, grp * 512:(grp + 1) * 512], rp_t[0][64:96, :], rp_t[1][64:96, :], ALU.add,
               ['rp_t0', 'rp_t1'], [('kpeR', grp)])

        wukv_v = wukv.rearrange('p k (h x) -> p k h x', x=128)

        def v_block(kb, vb=None):
            bk = (5 + kb % 2) if vb is None else vb
            for rc in range(2):
                mm(BK[bk].rearrange('p (h x) -> p h x', x=64), ckvT[:, rc, kb * 128:(kb + 1) * 128], wukv_v[:, rc, :, 64:128],
                   rc == 0, rc == 1, [('ckvT', kb), 'wukv'], [bkey(bk)])
            cp('dve' if kb % 2 == 0 else 'act', Vt[:, kb, :], BK[bk], [bkey(bk)], [('Vt', kb)])

        kv_pend = {}

        def kv_between(i):
            if i == 2:
                build_wkpe2()
            if i in (1, 3, 5, 6, 7):
                mod_slab({1: 7, 3: 8, 5: 9, 6: 10, 7: 11}[i])
                if i == 7:
                    finish_mod()
            if 0 <= i - 1 < 12:
                kv_pend[i - 1] = kside_p1(hTs, ('hTs', i - 1), (i - 1) * 128)
            if 0 <= i - 2 < 12:
                kside_p2(kv_pend.pop(i - 2), (i - 2) * 128)
            if 0 <= i - 3 < 12:
                v_block(i - 3)
            if 1 <= i <= 12 and (i - 1) % 4 == 3:
                kpe_group((i - 1) // 4)

        for g_ in range(NWM, NWM + 4):
            load_wm(g_)
        def xo_src(t):
            def f():
                xt, xk = pre_x[t] if t < 3 else load_x(I['xo'][t * 128:(t + 1) * 128, :])
                return xt, [xk]
            return f
        norm_pipeline([(xo_src(t), (A1, B1, 0 if t < 4 else 1, hTo[:, :, t * 128:(t + 1) * 128], ('hTo', t), 'A1', ('modT', 0),
                                    [(3, 1), (4, 2)])) for t in range(8)],
                      between=lambda i: (mod_slab({3: 4, 5: 5, 7: 6}[i]) if i in (3, 5, 7) else None))
        m_seq = A.mark()
        NS = 12
        hTs = A.alloc('hTs', [128, 8, NS * 128], BF16)
        wtok_s = A.alloc('wtok_s', [128, NS, 4], F32)
        gbc_s = A.alloc('gbc_s', [128, NS, 4], F32)
        def xs_src(t):
            def f():
                xt, xk = load_x(I['xs'][t * 128:(t + 1) * 128, :])
                return xt, [xk]
            return f
        norm_pipeline([(xs_src(t), (A1, B1, 1, hTs[:, :, t * 128:(t + 1) * 128], ('hTs', t), 'A1', ('modT', 0), [(3, 1), (4, 2)]))
                       for t in range(12)], between=kv_between)
        for i_ in (12, 13, 14):
            kv_between(i_)
        m_kv = A.mark()
        cck = A.alloc('cck', [128, 2, 256], BF16)
        P.dma('pool', cck, I['cckv'].rearrange('(t p) r -> p t r', p=128), writes=['cck'])
        ckr = A.alloc('ckr', [128, 2, 96], BF16)
        memset('pool', ckr, 0.0, ['ckr'])
        P.dma('pool', ckr[:, :, 64:96], I['ckro'].rearrange('(t p) r -> p t r', p=128), reads=['ckr'], writes=['ckr'])
        for t in range(2):
            pv = BKb[3 + t].rearrange('p (a b) -> p a b', b=128)
            for rc in range(2):
                tr(pv[:, rc, :], cck[:, t, rc * 128:(rc + 1) * 128], ident_b, ['cck', 'ident_b'], [bkey(3 + t)])
            tr(pv[0:96, 2, :], ckr[:, t, :], ident_b, ['ckr', 'ident_b'], [bkey(3 + t)])
            cp('act', ckvT[:, :, 2048 + t * 128:2048 + (t + 1) * 128], pv[:, 0:2, :], [bkey(3 + t)], [('ckvT', 16 + t)])
            cp('dve', kpeR[64:96, 2048 + t * 128:2048 + (t + 1) * 128], pv[64:96, 2, :], [bkey(3 + t)], [('kpeR', 4)])
        for kb in (16, 17):
            v_block(kb)
        A.release(m_kv)

        okr_sb = A.alloc('okr_sb', [128, 4, 32], F32)
        kfill = []

        def kf_prompt(t):
            kside_tile(hTo, ('hTo', t), t * 128, 2304 + t * 128, out_rows=O['ockv'][t * 128:(t + 1) * 128, :],
                       zb=(1, 2), tb=3)
            for kc in range(KC):
                mm(BK[0][:, 0:32], hTo[:, kc, t * 128:(t + 1) * 128], wkpe[:, kc, 64:96], kc == 0, kc == KC - 1,
                   ['wkpe', ('hTo', t)], [bkey(0)])
            cp('dve', okr_sb[:, t, :], BK[0][:, 0:32], [bkey(0)], [('okr_sb', t)])
            v_block(18 + t, vb=4)

        def kf_prompt_fin():
            P.dma('sp', O['okr'].rearrange('(t p) r -> p t r', p=128), okr_sb, reads=[('okr_sb', t) for t in range(4)],
                  writes=['okr'])
            final_keys.append('okr')
            for kc in range(KC):
                mm(BK[0][0:96, :], wkpe[:, kc, :], hTo[:, kc, 0:512], kc == 0, kc == KC - 1,
                   ['wkpe'] + [('hTo', t) for t in range(4)], [bkey(0)])
            cp('dve', kpeR[64:96, 2304:2816], BK[0][64:96, :], [bkey(0)], [('kpeR', 5)])

        def kf_own(t):
            kside_tile(hTo, ('hTo', t), t * 128, (12 + t - 4) * 128, zb=(1, 2), tb=3)
            v_block(12 + t - 4, vb=4)
        for t in range(4):
            kfill.append(lambda t=t: kf_prompt(t))
        kfill.append(kf_prompt_fin)
        for t in range(4, 8):
            kfill.append(lambda t=t: kf_own(t))

        P.phase = 'P2b_gates'
        A.report('P2b_gates')
        NT = NS * 128
        minit_f = A.alloc('minit_f', [4, 1], F32)
        Rm = [A.alloc('Rm%d' % i, [4, 1], F32) for i in range(5)]
        amax_s = A.alloc('amax_s', [4, NS], F32)
        nbt_s = A.alloc('nbt_s', [4, NS], F32)
        mout_s = A.alloc('mout_s', [4, NS], F32)
        mprev_s = A.alloc('mprev_s', [4, NS], F32)
        Ml_s = A.alloc('Ml_s', [4, NS], F32)
        g_s = A.alloc('g_s', [4, NS], F32)
        dsl = load_small('dsl', [4, NS], I['dsl'])
        mt1 = A.alloc('mt1', [4, NS], F32)
        Gd = A.alloc('Gd_s', [4, NS, 4], F32)
        m_rows = A.mark()
        wg = load_w('wg_s', [128, 8, 16], w_in_v[:, :, O_G:O_G + 16])
        wk_s = A.alloc('wk_s', [128, 8, 512], BF16, at=190 * 1024)
        wv_h = [A.alloc('wv_s0', [128, 8, 512], BF16, at=198 * 1024), None]
        P.dma('pool', wk_s, w_in_v[:, :, O_MK:O_MK + 512], writes=['wk_s'])
        P.dma('pool', wv_h[0], w_in_v[:, :, O_MV:O_MV + 512], writes=['wv_s0'])
        G16s = A.alloc('G16s', [16, NT], F32)
        for grp in range(3):
            bk = 5 + grp % 2
            for kc in range(KC):
                mm(BK[bk][0:16, :], wg[:, kc, :], hTs[:, kc, grp * 512:(grp + 1) * 512], kc == 0, kc == KC - 1,
                   ['wg_s'] + hTs_keys(grp * 4, 4), [bkey(bk)])
            act(G16s[:, grp * 512:(grp + 1) * 512], BK[bk][0:16, :], AF.Identity, [bkey(bk), 'bgate'],
                [('G16s', grp)], bias=bgate)
        G16s_keys = [('G16s', g_) for g_ in range(3)]
        P.filler = kfill
        P.fill_every = 5
        T = []
        for ri in range(4):
            t_ = A.alloc('s_T%d' % ri, [4, NT], F32)
            P.dma('sp', t_, G16s[ri * 4:(ri + 1) * 4, :], reads=G16s_keys, writes=['s_T%d' % ri])
            T.append(t_)
        dm = A.alloc('s_dm', [4, NT], F32)
        P.dma('sp', dm, I['dmask'], writes=['s_dm'])

        def blend_rows(x, xk, y, yk):
            tt('dve', x, x, y, ALU.subtract, [xk, yk], [xk])
            tt('dve', x, x, dm, ALU.mult, [xk, 's_dm'], [xk])
            tt('dve', x, x, y, ALU.add, [xk, yk], [xk])
        blend_rows(T[0], 's_T0', T[1], 's_T1')
        blend_rows(T[2], 's_T2', T[3], 's_T3')
        act(T[2], T[2], AF.Exp, ['s_T2'], ['s_T2'], scale=-1.0)
        act(T[2], T[2], AF.Ln, ['s_T2'], ['s_T2'], bias=1.0)
        scan(T[1], keepF[:, 0:NT], T[2], 0.0, ALU.mult, ALU.add, ['keepF', 's_T2'], ['s_T1'])
        scan(T[3][:, ::-1], keepB[:, 0:NT][:, ::-1], T[2][:, ::-1], 0.0, ALU.mult, ALU.add, ['keepB', 's_T2'], ['s_T3'])
        bF = T[1].rearrange('p (c s) -> p c s', s=128)
        bB = T[3].rearrange('p (c s) -> p c s', s=128)
        cp('dve', nbt_s, bB[:, :, 0], ['s_T3'], ['nbt_s'])
        tt('dve', mt1, bF[:, :, 127], nbt_s, ALU.subtract, ['s_T1', 'nbt_s'], ['mt1'])
        tt('dve', mt1, mt1, dsl, ALU.mult, ['mt1', 'dsl'], ['mt1'])
        tt('dve', nbt_s, nbt_s, mt1, ALU.add, ['nbt_s', 'mt1'], ['nbt_s'])
        blend_rows(T[1], 's_T1', T[3], 's_T3')
        tt('dve', T[0], T[0], T[1], ALU.add, ['s_T0', 's_T1'], ['s_T0'])
        P.dve(lambda e: e.tensor_reduce(out=amax_s, in_=T[0].rearrange('p (c s) -> p c s', s=128), axis=AX.X,
                                        op=ALU.max), ['s_T0'], ['amax_s'])
        sel4 = sel[0:4, :]
        cp('dve', Rm[0], stm4[:, 0:1], ['stm4'], ['Rm0'])
        for k in range(4):
            rk_, rn_ = 'Rm%d' % k, 'Rm%d' % (k + 1)
            if k == 0:
                ts('dve', minit_f, Rm[0], sel4[:, 0:1], None, ALU.mult, None, ['Rm0', 'sel'], ['minit_f'])
            else:
                stt(minit_f, Rm[k], sel4[:, k:k + 1], minit_f, ALU.mult, ALU.add, [rk_, 'sel', 'minit_f'], ['minit_f'])
            tt('dve', mt1[:, 0:1], stm4[:, 1:2], Rm[k], ALU.subtract, ['stm4', rk_], ['mt1'])
            stt(Rm[k], mt1[:, 0:1], sel4[:, k:k + 1], Rm[k], ALU.mult, ALU.add, ['mt1', 'sel', rk_], [rk_])
            if k < 3:
                sl_ = slice(4 * k, 4 * k + 4)
                scan(mout_s[:, sl_], amax_s[:, sl_], nbt_s[:, sl_], Rm[k], ALU.max, ALU.subtract,
                     ['amax_s', 'nbt_s', rk_], [('mout_s', k)])
                cp('dve', mprev_s[:, 4 * k:4 * k + 1], Rm[k], [rk_], [('mprev_s', k, 0)])
                cp('dve', mprev_s[:, 4 * k + 1:4 * k + 4], mout_s[:, 4 * k:4 * k + 3], [('mout_s', k)], [('mprev_s', k, 1)])
                cp('dve', Rm[k + 1], mout_s[:, 4 * k + 3:4 * k + 4], [('mout_s', k)], [rn_])
        cp('dve', minit[:, 0:1], minit_f, ['minit_f'], [('minit', 0)])
        cp('dve', minit[:, 1:2], Rm[3], ['Rm3'], [('minit', 1)])
        mpk_s = [('mprev_s', k, i_) for k in range(3) for i_ in range(2)]
        tt('dve', Ml_s, mprev_s, amax_s, ALU.max, mpk_s + ['amax_s'], ['Ml_s'])
        tt('dve', g_s, mprev_s, Ml_s, ALU.subtract, mpk_s + ['Ml_s'], ['g_s'])
        act(g_s, g_s, AF.Exp, ['g_s'], ['g_s'])
        a3 = T[0].rearrange('p (c s) -> p c s', s=128)
        tt('dve', a3, a3, Ml_s.unsqueeze(2).to_broadcast([4, NS, 128]), ALU.subtract, ['s_T0', 'Ml_s'], ['s_T0'])
        act(T[0], T[0], AF.Exp, ['s_T0'], ['s_T0'])
        for c in range(NS):
            tr(BK[7][:, c * 4:c * 4 + 4], T[0][:, c * 128:(c + 1) * 128], I4, ['s_T0', 'ident_f'], [bkey(7)])
        cp('dve', wtok_s.rearrange('p a b -> p (a b)'), BK[7][:, 0:NS * 4], [bkey(7)], ['wtok_s'])
        tt('dve', Gd, g_s.unsqueeze(2).to_broadcast([4, NS, 4]), I4.unsqueeze(1).to_broadcast([4, NS, 4]), ALU.mult,
           ['g_s', 'ident_f'], ['Gd_s'])
        mm(BK[6][:, 0:NS * 4], ones_f[0:4, :], Gd.rearrange('p a b -> p (a b)'), True, True, ['ones_f', 'Gd_s'], [bkey(6)])
        cp('dve', gbc_s.rearrange('p a b -> p (a b)'), BK[6][:, 0:NS * 4], [bkey(6)], ['gbc_s'])
        dbg('minit', minit, [('minit', 0), ('minit', 1)])
        P.filler = None
        while kfill:
            kfill.pop(0)()
        A.release(m_rows)

        P.phase = 'P2c_scan'
        A.report('P2c_scan')
        m_scan = A.mark()
        Rst = A.alloc('Rst', [128, 4, 258], F32)
        C0b = A.alloc('C0b', [128, 4, 258], F32)
        ctmp = A.alloc('ctmp', [128, 4, 258], F32)
        RK = [('Rst', h_) for h_ in range(4)]
        m_stc = A.mark()
        stc_sb = A.alloc('stc_sb', [128, 8, 2, 128], F32)
        P.dma('sp', stc_sb, I['stC'].rearrange('a (vh p) d -> p a vh d', p=128), writes=['stc_sb'])
        for d, (dst_, dkeys) in enumerate(((Rst, RK), (C0b, ['C0b']))):
            memset('dve', dst_, 0.0, dkeys)
            for h in range(4):
                bk = 5 + (d * 4 + h) % 2
                for vh in range(2):
                    tr(BK[bk][:, vh * 128:(vh + 1) * 128], stc_sb[:, d * 4 + h, vh, :], ident_f, ['stc_sb', 'ident_f'],
                       [bkey(bk)])
                cp('act', dst_[:, h, 0:256], BK[bk][:, 0:256], [bkey(bk)], [dkeys[h]] if d == 0 else dkeys)
            cp('dve', dst_[:, :, 256:257], stnT[:, d * 4:(d + 1) * 4].unsqueeze(2), ['stnT'], dkeys)
        A.release(m_stc)
        wv_h[1] = load_w('wv_s1', [128, 8, 512], w_in_v[:, :, O_MV + 512:O_MV + 1024])
        NPB = 3
        mks = [A.alloc('mks%d' % i, [128, 512], BF16) for i in range(NPB)]
        mvs = [A.alloc('mvs%d' % i, [128, 4, 258], BF16) for i in range(NPB)]
        for i in range(NPB):
            memset('pool', mvs[i][:, :, 256:258], 0.0, [('mvs%d' % i, 'x')])
            memset('pool', mvs[i][:, :, 256:257], 1.0, [('mvs%d' % i, 'x')])
        pj_rr = [0]

        def proj_kv_seq(t):
            i = pj_rr[0] % NPB
            pj_rr[0] += 1
            bk = 4 + i % 2
            for kc in range(KC):
                mm(BK[bk], hTs[:, kc, t * 128:(t + 1) * 128], wk_s[:, kc, :], kc == 0, kc == KC - 1,
                   ['wk_s', ('hTs', t)], [bkey(bk)])
            act(mks[i], BK[bk], AF.Identity, [bkey(bk)], ['mks%d' % i], scale=128 ** -0.5)
            for half in range(2):
                bk2 = 6 + half
                for kc in range(KC):
                    mm(BK[bk2], hTs[:, kc, t * 128:(t + 1) * 128], wv_h[half][:, kc, :], kc == 0,
                       kc == KC - 1, ['wv_s%d' % half, ('hTs', t)], [bkey(bk2)])
                cp('dve' if half == 0 else 'act', mvs[i][:, half * 2:half * 2 + 2, 0:256],
                   BK[bk2].rearrange('p (h v) -> p h v', v=256), [bkey(bk2)], [('mvs%d' % i, half)])
            return i

        vw_s = [A.alloc('vw_s%d' % i, [128, 4, 258], BF16) for i in range(3)]
        sc_rr = [0]
        Rf = Rst.rearrange('p a b -> p (a b)')

        def boundary(k):
            cf = Cinit[0].rearrange('p a b -> p (a b)')
            if k == 0:
                ts('dve', cf, Rf, sel[:, 0:1], None, ALU.mult, None, RK + ['sel'], ['Cinit0'])
            else:
                stt(cf, Rf, sel[:, k:k + 1], cf, ALU.mult, ALU.add, RK + ['sel', 'Cinit0'], ['Cinit0'])
            ct = ctmp.rearrange('p a b -> p (a b)')
            tt('pool', ct, C0b.rearrange('p a b -> p (a b)'), Rf, ALU.subtract, ['C0b'] + RK, ['ctmp'])
            stt(Rf, ct, sel[:, k:k + 1], Rf, ALU.mult, ALU.add, ['ctmp', 'sel'] + RK, RK)

        def scan_prep(c, pb):
            i = sc_rr[0] % 3
            sc_rr[0] += 1
            tt('pool', vw_s[i], mvs[pb], wtok_s[:, c, :].unsqueeze(2).to_broadcast([128, 4, 258]), ALU.mult,
               [('mvs%d' % pb, 'x'), ('mvs%d' % pb, 0), ('mvs%d' % pb, 1), 'wtok_s'], ['vw_s%d' % i])
            return i

        def scan_step(i, c, pb):
            vw, vk = vw_s[i], 'vw_s%d' % i
            for h in range(4):
                bk = (i % 2) * 2 + h % 2
                mm(BK[bk][:, 0:257], mks[pb][:, h * 128:(h + 1) * 128], vw[:, h, 0:257], True, True,
                   ['mks%d' % pb, vk], [bkey(bk)])
                stt(Rst[:, h, 0:257], Rst[:, h, 0:257], gbc_s[:, c, h:h + 1], BK[bk][:, 0:257], ALU.mult, ALU.add,
                    [('Rst', h), 'gbc_s', bkey(bk)], [('Rst', h)])

        pend = None
        for c in range(NS):
            pb_ = proj_kv_seq(c)
            vi_ = scan_prep(c, pb_)
            if pend is not None:
                if pend[1] % 4 == 0:
                    boundary(pend[1] // 4)
                scan_step(*pend)
            pend = (vi_, c, pb_)
        if pend[1] % 4 == 0:
            boundary(pend[1] // 4)
        scan_step(*pend)
        boundary(3)
        cp('dve', Cinit[1].rearrange('p a b -> p (a b)'), Rf, RK, ['Cinit1'])
        dbg('Cinit0', Cinit[0].rearrange('p a b -> p (a b)'), ['Cinit0'])
        dbg('Cinit1', Cinit[1].rearrange('p a b -> p (a b)'), ['Cinit1'])
        A.release(m_scan)
        A.free_fixed('wk_s')
        A.free_fixed('wv_s0')

        A.release(m_seq)

        P.phase = 'P3a_qk'
        A.report('P3a_qk')
        m_mla = A.mark()
        qj = [A.alloc('qj%d' % i, [128, 384], BF16) for i in range(2)]
        rp_o = [A.alloc('rp_o%d' % i, [96, 512], F32) for i in range(2)]
        wq_s = load_w('wq_s', [128, 8, 384], w_in_v[:, :, O_Q:O_Q + 384])
        wuq = load_w('wuq', [128, 3, 768], I['w_uq'].rearrange('(kc p) n -> p kc n', p=128))
        for kc in range(3):
            ts('dve', wuq[:, kc, :], wuq[:, kc, :], gqT[:, kc:kc + 1], None, ALU.mult, None, ['wuq', 'gqT'], ['wuq'])
        wuqp = A.alloc('wuqp', [128, 3, 768], BF16)
        memset('pool', wuqp, 0.0, ['wuqp'])
        for kc in range(3):
            sv = wuq[:, kc, :].rearrange('p (h x) -> p h x', x=96)
            dv = wuqp[:, kc, :].rearrange('p (h x) -> p h x', x=96)
            for a in range(2):
                o = 64 + a * 16
                P.act(lambda e, d_=dv[:, :, o:o + 8], s_=sv[:, :, o + 8:o + 16]: e.mul(out=d_, in_=s_, mul=-1.0),
                      ['wuq', 'wuqp'], ['wuqp'])
                cp('dve', dv[:, :, o + 8:o + 16], sv[:, :, o:o + 8], ['wuq', 'wuqp'], ['wuqp'])
        cs_o = [A.alloc('cs_o%d' % i, [96, 512], F32) for i in range(2)]
        P.dma('sp', cs_o[0][64:96, :], I['cos_o'], writes=['cs_o0'])
        P.dma('sp', cs_o[1][64:96, :], I['sin_o'], writes=['cs_o1'])
        qnT = A.alloc('qnT', [128, 3, 1024], BF16)
        q_ss = [A.alloc('q_ss%d' % i, [128, 4], F32) for i in range(2)]
        def q_p1(t):
            i = t % 2
            bk = 5 + i
            ss, sk = q_ss[i], 'q_ss%d' % i
            for kc in range(KC):
                mm(BK[bk][:, 0:384], hTo[:, kc, t * 128:(t + 1) * 128], wq_s[:, kc, :], kc == 0, kc == KC - 1,
                   ['wq_s', ('hTo', t)], [bkey(bk)])
            act(qj[i], BK[bk][:, 0:384], AF.Square, [bkey(bk)], ['qj%d' % i, (sk, 0)], accum=ss[:, 0:1])
            act(ss[:, 1:2], ss[:, 0:1], AF.Sqrt, [(sk, 0), 'eps_t'], [(sk, 1)], bias=eps_t, scale=1.0 / 384)
            recip(ss[:, 2:3], ss[:, 1:2], [(sk, 1)], [(sk, 2)])
            ts('dve', qj[i], BK[bk][:, 0:384], ss[:, 2:3], None, ALU.mult, None, [bkey(bk), (sk, 2)], ['qj%d' % i])

        def q_p2(t):
            i = t % 2
            bk2 = 3 + i
            pv = BKb[bk2].rearrange('p (a b) -> p a b', b=128)
            for rc in range(3):
                tr(pv[:, rc, :], qj[i][:, rc * 128:(rc + 1) * 128], ident_b, ['qj%d' % i, 'ident_b'], [bkey(bk2)])
            cp('act', qnT[:, :, t * 128:(t + 1) * 128], pv[:, 0:3, :], [bkey(bk2)], [('qnT', t)])

        for t in range(8):
            q_p1(t)
            if t >= 1:
                q_p2(t - 1)
        q_p2(7)
        for w_, bk in ((wkpe, 1), (wkpe2, 2)):
            for kc in range(KC):
                mm(BK[bk][0:96, :], w_[:, kc, :], hTo[:, kc, 512:1024], kc == 0, kc == KC - 1,
                   ['wkpe', 'wkpe2'] + [('hTo', t) for t in range(4, 8)], [bkey(bk)])
        tt('dve', rp_t[0][64:96, :], BK[1][64:96, :], cs_o[0][64:96, :], ALU.mult, [bkey(1), 'cs_o0'], ['rp_t0'])
        tt('dve', rp_t[1][64:96, :], BK[2][64:96, :], cs_o[1][64:96, :], ALU.mult, [bkey(2), 'cs_o1'], ['rp_t1'])
        tt('dve', kpeR[64:96, 1536:2048], rp_t[0][64:96, :], rp_t[1][64:96, :], ALU.add, ['rp_t0', 'rp_t1'], [('kpeR', 3)])

        P.phase = 'P3b_attn'
        A.report('P3b_attn')
        w_mq = A.alloc('w_mq', [128, 8, 512], BF16, at=190 * 1024)
        w_mk = A.alloc('w_mk', [128, 8, 512], BF16, at=198 * 1024)
        P.dma('pool', w_mq, w_in_v[:, :, O_MQ:O_MQ + 512], writes=['w_mq'])
        P.dma('pool', w_mk, w_in_v[:, :, O_MK:O_MK + 512], writes=['w_mk'])
        KT = [A.alloc('KT%d' % i, [96, NKEY], BF16) for i in range(2)]
        QT = [A.alloc('QT%d' % i, [96, 1024], BF16) for i in range(2)]
        pT = [A.alloc('pT%d' % i, [128, 512], BF16) for i in range(3)]
        rdn = [A.alloc('rdn%d' % i, [128, 512], F32) for i in range(2)]
        pt_rr = [0]
        sb_rr = [0]
        od_rr = [0]
        units = [(0, 256, [18, 19]), (256, 256, [20, 21]), (512, 512, list(range(18)))]
        kpeR_keys = [('kpeR', i) for i in range(6)]
        def head_prep(h):
            KTh, ktk = KT[h % 2], 'KT%d' % (h % 2)
            QTh, qtk = QT[h % 2], 'QT%d' % (h % 2)
            th = []

            def kgrp(grp):
                k0 = grp * 512
                n = min(512, NKEY - k0)
                bk = 6 + grp % 2
                for rc in range(2):
                    mm(BK[bk][:, 0:n], wukv[:, rc, h * 128:(h + 1) * 128], ckvT[:, rc, k0:k0 + n], rc == 0, rc == 1,
                       ['wukv'] + [('ckvT', k0 // 128 + i) for i in range(n // 128)], [bkey(bk)])
                cp('dve', KTh[0:64, k0:k0 + n], BK[bk][0:64, 0:n], [bkey(bk)], [(ktk, grp)])
            for grp in range(6):
                th.append(lambda grp=grp: kgrp(grp))
            th.append(lambda: cp('dve', KTh[64:96, :], kpeR[64:96, :], kpeR_keys, [(ktk, 'r')]))

            def qhalf(half):
                bk = 6 + half
                tk = [('qnT', half * 4 + i) for i in range(4)]
                for kc in range(3):
                    mm(BK[bk][0:96, :], wuq[:, kc, h * 96:(h + 1) * 96], qnT[:, kc, half * 512:(half + 1) * 512], kc == 0,
                       kc == 2, ['wuq'] + tk, [bkey(bk)])
                if half == 0:
                    cp('dve', QTh[:, 0:512], BK[bk][0:96, :], [bkey(bk)], [(qtk, 0)])
                else:
                    for kc in range(3):
                        mm(BK[6][0:96, :], wuqp[:, kc, h * 96:(h + 1) * 96], qnT[:, kc, 512:1024], kc == 0, kc == 2,
                           ['wuqp'] + tk, [bkey(6)])
                    cp('dve', QTh[0:64, 512:1024], BK[bk][0:64, :], [bkey(bk)], [(qtk, 1)])
                    tt('dve', rp_o[0][64:96, :], BK[bk][64:96, :], cs_o[0][64:96, :], ALU.mult, [bkey(bk), 'cs_o0'], ['rp_o0'])
                    tt('dve', rp_o[1][64:96, :], BK[6][64:96, :], cs_o[1][64:96, :], ALU.mult, [bkey(6), 'cs_o1'], ['rp_o1'])
                    tt('dve', QTh[64:96, 512:1024], rp_o[0][64:96, :], rp_o[1][64:96, :], ALU.add, ['rp_o0', 'rp_o1'],
                       [(qtk, 2)])
            th.append(lambda: qhalf(0))
            th.append(lambda: qhalf(1))
            return th

        for t_ in head_prep(0):
            t_()
        for h in range(8):
            KTh, ktk = KT[h % 2], 'KT%d' % (h % 2)
            QTh, qtk = QT[h % 2], 'QT%d' % (h % 2)
            side = head_prep(h + 1) if h + 1 < 8 else []
            ktkeys = [(ktk, g_) for g_ in range(6)] + [(ktk, 'r')]
            qtkeys = [(qtk, 0), (qtk, 1), (qtk, 2)]
            prow = slice((h % 2) * 64, (h % 2) * 64 + 64)
            vcol = (h - h % 2) * 64
            for (q0, nq, blocks) in units:
                ob = 2 + (od_rr[0] % 2) * 2
                db = ob + 1
                od_rr[0] += 1
                nb = len(blocks)
                sbank = {}

                def score(bi):
                    sbk = sb_rr[0] % 2
                    sb_rr[0] += 1
                    sbank[bi] = sbk
                    kb = blocks[bi]
                    mm(BK[sbk][:, 0:nq], KTh[0:96, kb * 128:(kb + 1) * 128], QTh[0:96, q0:q0 + nq], True, True,
                       ktkeys + qtkeys, [bkey(sbk)])
                score(0)
                if nb > 1:
                    score(1)
                for bi, kb in enumerate(blocks):
                    sbk = sbank[bi]
                    pi = pt_rr[0] % 3
                    pt_rr[0] += 1
                    act(pT[pi][:, 0:nq], BK[sbk][:, 0:nq], AF.Exp, [bkey(sbk)], ['pT%d' % pi], scale=MLA_SCALE)
                    first, last = bi == 0, bi == nb - 1
                    mm(BK[ob][:, 0:nq], Vt[:, kb, vcol:vcol + 128], pT[pi][:, 0:nq], first, last,
                       [('Vt', kb), 'pT%d' % pi], [bkey(ob)])
                    mm(BK[db][:, 0:nq], ones_b, pT[pi][:, 0:nq], first, last, ['ones_b', 'pT%d' % pi], [bkey(db)])
                    if bi + 2 < nb:
                        score(bi + 2)
                    if nb > 4 and side and bi % 2 == 1:
                        side.pop(0)()
                ri = od_rr[0] % 2
                act(rdn[ri][prow, 0:nq], BK[db][prow, 0:nq], AF.Ln, [bkey(db)], ['rdn%d' % ri])
                act(rdn[ri][prow, 0:nq], rdn[ri][prow, 0:nq], AF.Exp, ['rdn%d' % ri], ['rdn%d' % ri], scale=-1.0)
                tt('dve', attT[prow, h // 2, q0:q0 + nq], BK[ob][prow, 0:nq], rdn[ri][prow, 0:nq], ALU.mult,
                   [bkey(ob), 'rdn%d' % ri], [('attT', h, q0)])
            while side:
                side.pop(0)()
        attT_keys = [('attT', h, q0) for h in range(8) for q0 in (0, 256, 512)]
        if 'attT' in DBG:
            adbg = A.alloc('adbg', [128, 4, 1024], F32)
            cp('dve', adbg, attT, attT_keys, ['adbg'])
            dbg('attT', adbg.rearrange('p a b -> p (a b)'), ['adbg'])
        A.release(m_att)
        if stop_after == 'mla':
            return finish()

        P.phase = 'P4a_proj'
        A.report('P4a_proj')
        hmT = A.alloc('hmT', [128, 8, 1024], BF16)
        m_ml = A.mark()
        mqT = A.alloc('mqT', [128, 4, 1024], BF16)
        mk_tok = A.alloc('mk_tok', [128, 8, 512], BF16)
        Vx = A.alloc('Vx', [128, 8, 4, 258], BF16)
        memset('pool', Vx[:, :, :, 256:258], 0.0, [('Vx', 'x')])
        memset('pool', Vx[:, :, :, 256:257], 1.0, [('Vx', 'x')])
        Sm = [A.alloc('Sm%d' % d, [128, 8, 4, 128], BF16) for d in range(2)]
        sgT = A.alloc('sgT', [128, 8, 1024], BF16)
        wtok_o = A.alloc('wtok_o', [128, 8, 2, 4], F32)
        thr_o = A.alloc('thr_o', [128, 8, 2, 4], F32)
        gbc_o = A.alloc('gbc_o', [128, 2, 8, 4], F32)
        om_sb = A.alloc('om_sb', [4, 4], F32)
        G16o = A.alloc('G16o', [16, 1024], F32)
        m_mlp = A.mark()
        hTo_half = lambda half: [('hTo', half * 4 + i) for i in range(4)]
        w_mv = load_w('w_mv', [128, 8, 1024], w_in_v[:, :, O_MV:O_MV + 1024])
        w_mo = load_w('w_mo', [128, 8, 1024], w_in_v[:, :, O_MO:O_MO + 1024])
        wg_o = load_w('wg_o', [128, 8, 16], w_in_v[:, :, O_G:O_G + 16])
        m_mkT = A.mark()
        mkT = A.alloc('mkT', [128, 4, 1024], BF16)
        n_ev = 0
        for hh in range(4):
            for half in range(2):
                for (w_, wn, dst, dn, scl) in ((w_mq, 'w_mq', mqT, 'mqT', None), (w_mk, 'w_mk', mkT, 'mkT', 128 ** -0.5)):
                    bk = 6 + n_ev % 2
                    n_ev += 1
                    for kc in range(KC):
                        mm(BK[bk], w_[:, kc, hh * 128:(hh + 1) * 128], hTo[:, kc, half * 512:(half + 1) * 512], kc == 0,
                           kc == KC - 1, [wn] + hTo_half(half), [bkey(bk)])
                    if scl is None:
                        cp('dve', dst[:, hh, half * 512:(half + 1) * 512], BK[bk], [bkey(bk)], [(dn, hh, half)])
                    else:
                        act(dst[:, hh, half * 512:(half + 1) * 512], BK[bk], AF.Identity, [bkey(bk)], [(dn, hh, half)], scale=scl)
        for t in range(8):
            bk = 4 + t % 2
            pv = BKb[bk].rearrange('p (a b) -> p a b', b=128)
            for hh in range(4):
                tr(pv[:, hh, :], mkT[:, hh, t * 128:(t + 1) * 128], ident_b, [('mkT', hh, t // 4), 'ident_b'], [bkey(bk)])
            cp('act' if t % 2 == 0 else 'dve', mk_tok[:, t, :].rearrange('p (a b) -> p a b', b=128), pv[:, 0:4, :],
               [bkey(bk)], [('mk_tok', t)])
        for c in range(8):
            bk = 2 + c % 2
            for hh in range(4):
                mm(BK[bk][:, hh * 128:(hh + 1) * 128], mkT[:, hh, c * 128:(c + 1) * 128], mqT[:, hh, c * 128:(c + 1) * 128],
                   True, True, [('mkT', hh, c // 4), ('mqT', hh, c // 4)], [bkey(bk)])
            sv = BK[bk].rearrange('p (h j) -> p h j', j=128)
            tt('dve', Sm[0][:, c], sv, maskF, ALU.mult, [bkey(bk), 'maskF'], [('Sm0', c)])
            tt('dve', Sm[1][:, c], sv, maskB, ALU.mult, [bkey(bk), 'maskB'], [('Sm1', c)])
        A.release(m_mkT)
        A.free_fixed('w_mq')
        A.free_fixed('w_mk')
        def v_unit(t, half):
            bk = 4 + half
            for kc in range(KC):
                mm(BK[bk], hTo[:, kc, t * 128:(t + 1) * 128], w_mv[:, kc, half * 512:(half + 1) * 512], kc == 0,
                   kc == KC - 1, ['w_mv', ('hTo', t)], [bkey(bk)])
            cp('dve' if half == 0 else 'act', Vx[:, t, half * 2:half * 2 + 2, 0:256],
               BK[bk].rearrange('p (h v) -> p h v', v=256), [bkey(bk)], [('Vx', t, half)])

        def sg_unit(fc, half):
            bk = 2 + half
            for kc in range(KC):
                mm(BK[bk], w_mo[:, kc, fc * 128:(fc + 1) * 128], hTo[:, kc, half * 512:(half + 1) * 512], kc == 0,
                   kc == KC - 1, ['w_mo'] + hTo_half(half), [bkey(bk)])
            act(sgT[:, fc, half * 512:(half + 1) * 512], BK[bk], AF.Sigmoid, [bkey(bk)], [('sgT', fc, half)])
        fill_units = [(lambda t=t, half=half: v_unit(t, half)) for t in range(8) for half in range(2)]
        fill_units += [(lambda fc=fc, half=half: sg_unit(fc, half)) for fc in range(8) for half in range(2)]
        for grp in range(2):
            bk = 6 + grp
            for kc in range(KC):
                mm(BK[bk][0:16, :], wg_o[:, kc, :], hTo[:, kc, grp * 512:(grp + 1) * 512], kc == 0, kc == KC - 1,
                   ['wg_o'] + hTo_half(grp), [bkey(bk)])
            act(G16o[:, grp * 512:(grp + 1) * 512], BK[bk][0:16, :], AF.Identity, [bkey(bk), 'bgate'], [('G16o', grp)],
                bias=bgate)
        ch_o = {}
        Gd_o = A.alloc('Gd_o', [4, 2, 8, 4], F32)
        P.filler = fill_units
        P.fill_every = 2
        for di, d in enumerate(('f', 'b')):
            chains = [('A', 0, 2, 0.0, []), ('B', 2, 2, 0.0, []), ('S', 4, 4, minit[:, di:di + 1], [('minit', di)])]
            r_ = gate_dir('o_', di, 1024, G16o, [('G16o', 0), ('G16o', 1)], chains, 0, 1)
            ch_o[d] = r_
            for nm in ('A', 'B', 'S'):
                c_ = r_[nm]
                tt('dve', Gd_o[:, di, c_['c0']:c_['c0'] + c_['n'], :], c_['g'].unsqueeze(2).to_broadcast([4, c_['n'], 4]),
                   I4.unsqueeze(1).to_broadcast([4, c_['n'], 4]), ALU.mult, [c_['pref'] + 'g', 'ident_f'], [('Gd_o', di, nm)])
            for ui, nm in enumerate(('A', 'B')):
                c_ = r_[nm]
                src = c_['mout'][:, 1:2] if d == 'f' else c_['mout'][:, 0:1]
                cp('dve', om_sb[:, ui * 2 + di:ui * 2 + di + 1], src, [c_['pref'] + 'mout'], [('om_sb', ui, di)])
        P.dma('sp', O['om'], om_sb, reads=[('om_sb', u_, d_) for u_ in range(2) for d_ in range(2)], writes=['om'])
        final_keys.append('om')
        P.filler = None
        while fill_units:
            fill_units.pop(0)()
        cp('dve', wtok_o.rearrange('p a b c -> p (a b c)'), BK[0][:, 0:64], [bkey(0)], ['wtok_o'])
        cp('dve', thr_o.rearrange('p a b c -> p (a b c)'), BK[1][:, 0:64], [bkey(1)], ['thr_o'])
        mm(BK[6][:, 0:64], ones_f[0:4, :], Gd_o.rearrange('p a b c -> p (a b c)'), True, True,
           ['ones_f'] + [('Gd_o', d_, n_) for d_ in range(2) for n_ in ('A', 'B', 'S')], [bkey(6)])
        cp('dve', gbc_o.rearrange('p a b c -> p (a b c)'), BK[6][:, 0:64], [bkey(6)], ['gbc_o'])
        A.release(m_mlp)

        P.phase = 'P4b_chunks'
        A.report('P4b_chunks')
        vw_o = [A.alloc('vw_o%d' % i, [128, 4, 258], BF16) for i in range(3)]
        qg_o = [A.alloc('qg_o%d' % i, [128, 4, 128], BF16) for i in range(3)]
        ddt = [A.alloc('ddt%d' % i, [128, 8], F32) for i in range(3)]
        hn_b = [A.alloc('hn_b%d' % i, [128, 4, 256], BF16) for i in range(2)]
        hsq = A.alloc('hsq', [128, 256], BF16)
        h_ss = [A.alloc('h_ss%d' % i, [128, 12], F32) for i in range(2)]
        stg = [A.alloc('stg%d' % i, [128, 2, 128], F32) for i in range(2)]
        stg_all = [(stg[0], 'stg0'), (stg[1], 'stg1')]
        for i_ in range(2):
            flat = hn_b[i_].rearrange('p a b -> p (a b)')
            for half_ in range(2):
                v_ = flat[:, half_ * 512:(half_ + 1) * 512].bitcast(F32).rearrange('p (a b) -> p a b', b=128)
                stg_all.append((v_, 'hn_b%d' % i_))
        step_rr = [0]
        nb_rr = [0]
        cb_rr = [0]
        stg_rr = [0]

        def chunk_prep(di, c):
            i = step_rr[0] % 3
            step_rr[0] += 1
            vw, vk = vw_o[i], 'vw_o%d' % i
            qg, qk = qg_o[i], 'qg_o%d' % i
            tt('pool', vw, Vx[:, c], wtok_o[:, c, di, :].unsqueeze(2).to_broadcast([128, 4, 258]), ALU.mult,
               [('Vx', 'x'), ('Vx', c, 0), ('Vx', c, 1), 'wtok_o'], [vk])
            tt('pool', qg, mqT[:, :, c * 128:(c + 1) * 128], gbc_o[:, di, c, :].unsqueeze(2).to_broadcast([128, 4, 128]),
               ALU.mult, [('mqT', h_, c // 4) for h_ in range(4)] + ['gbc_o'], [qk])
            return i

        def chunk_step(i, Cst_, Cb_, sk, di, c, hacc, hk_, tl, first):
            vw, vk = vw_o[i], 'vw_o%d' % i
            qg, qk = qg_o[i], 'qg_o%d' % i
            dd, dk = ddt[i], 'ddt%d' % i
            db = 6 + (i % 2)
            for hh in range(4):
                mm(BK[db][:, hh:hh + 1], Sm[di][:, c, hh, :], vw[:, hh, 256:257], True, False, [('Sm%d' % di, c), vk], [bkey(db)])
                mm(BK[db][:, hh:hh + 1], qg[:, hh, :], Cb_[:, hh, 256:257], False, True, [qk, (sk + 'b', hh)], [bkey(db)])
            act(dd[:, 0:4], BK[db][:, 0:4], AF.Abs, [bkey(db)], [(dk, 0)])
            tt('dve', dd[:, 0:4], dd[:, 0:4], thr_o[:, c, di, :], ALU.max, [(dk, 0), 'thr_o'], [(dk, 0)])
            recip(dd[:, 4:8], dd[:, 0:4], [(dk, 0)], [(dk, 1)])
            nbanks = []
            npar = nb_rr[0] % 2
            nb_rr[0] += 1
            for hh in range(4):
                bk = npar * 2 + hh // 2
                co = (hh % 2) * 256
                nbanks.append((bk, co))
                mm(BK[bk][:, co:co + 256], Sm[di][:, c, hh, :], vw[:, hh, 0:256], True, False, [('Sm%d' % di, c), vk],
                   [bkey(bk)])
                mm(BK[bk][:, co:co + 256], qg[:, hh, :], Cb_[:, hh, 0:256], False, True, [qk, (sk + 'b', hh)], [bkey(bk)])
            for hh in range(4):
                bk, co = nbanks[hh]
                if first:
                    act(hacc[:, tl, hh, :], BK[bk][:, co:co + 256], AF.Identity, [bkey(bk), (dk, 1)], [(hk_, tl, hh)],
                        scale=dd[:, 4 + hh:5 + hh])
                else:
                    stt(hacc[:, tl, hh, :], BK[bk][:, co:co + 256], dd[:, 4 + hh:5 + hh], hacc[:, tl, hh, :], ALU.mult,
                        ALU.add, [bkey(bk), (dk, 1), (hk_, tl, hh)], [(hk_, tl, hh)])
            for hh in range(4):
                bk = 4 + cb_rr[0] % 2
                cb_rr[0] += 1
                mm(BK[bk][:, 0:257], mk_tok[:, c, hh * 128:(hh + 1) * 128], vw[:, hh, 0:257], True, True,
                   [('mk_tok', c), vk], [bkey(bk)])
                stt(Cst_[:, hh, 0:257], Cst_[:, hh, 0:257], gbc_o[:, di, c, hh:hh + 1], BK[bk][:, 0:257], ALU.mult, ALU.add,
                    [(sk, hh), 'gbc_o', bkey(bk)], [(sk, hh)])
                cp('act', Cb_[:, hh, 0:257], Cst_[:, hh, 0:257], [(sk, hh)], [(sk + 'b', hh)])

        big_stg = [(xin[i_].rearrange('p (h a b) -> p h a b', h=4, a=2), ['xin%d' % i_]) for i_ in range(2)]

        def out_state(Cst_, sk, u, di):
            idx = u * 2 + di
            sg_, sgk = big_stg[stg_rr[0] % 2]
            stg_rr[0] += 1
            for hh in range(4):
                bk = 6 + hh % 2
                for vh in range(2):
                    tr(BK[bk][:, vh * 128:(vh + 1) * 128], Cst_[:, hh, vh * 128:(vh + 1) * 128], ident_f,
                       [(sk, hh), 'ident_f'], [bkey(bk)])
                cp('act' if hh % 2 == 0 else 'dve', sg_[:, hh], BK[bk][:, 0:256].rearrange('p (a b) -> p a b', b=128),
                   [bkey(bk)] + sgk, sgk)
            P.dma('sp', O['oC'][idx * 4:(idx + 1) * 4].rearrange('h (vh p) d -> p h vh d', p=128), sg_, reads=sgk,
                  writes=[('oC', idx)])
            final_keys.append(('oC', idx))
            P.dma('sp', O['on'][idx * 4:(idx + 1) * 4, :].rearrange('h d -> d h'), Cst_[:, :, 256],
                  reads=[(sk, h_) for h_ in range(4)], writes=[('on', idx)], allow_slow_non_contiguous=True)
            final_keys.append(('on', idx))

        def post_tile(hacc, hk_, tl, t):
            i = t % 2
            ss, sk_ = h_ss[i], 'h_ss%d' % i
            for hh in range(4):
                act(hsq, hacc[:, tl, hh, :], AF.Square, [(hk_, tl, hh)], ['hsq', (sk_, hh)], accum=ss[:, hh:hh + 1])
            act(ss[:, 4:8], ss[:, 0:4], AF.Sqrt, [(sk_, h_) for h_ in range(4)] + ['eps_t'], [(sk_, 'sd')], bias=eps_t,
                scale=1.0 / 256)
            recip(ss[:, 8:12], ss[:, 4:8], [(sk_, 'sd')], [(sk_, 'r')])
            hn, hk = hn_b[i], 'hn_b%d' % i
            tt('dve', hn, hacc[:, tl], ss[:, 8:12].unsqueeze(2).to_broadcast([128, 4, 256]), ALU.mult,
               [(hk_, tl, h_) for h_ in range(4)] + [(sk_, 'r')], [hk])
            bk = i
            pv = BKb[bk].rearrange('p (a b) -> p a b', b=128)
            hn2 = hn.rearrange('p a b -> p (a b)')
            for ch in range(KC):
                tr(pv[:, ch, :], hn2[:, ch * 128:(ch + 1) * 128], ident_b, [hk, 'ident_b'], [bkey(bk)])
            tt('dve', hmT[:, :, t * 128:(t + 1) * 128], pv, sgT[:, :, t * 128:(t + 1) * 128], ALU.mult,
               [bkey(bk)] + [('sgT', fc, t // 4) for fc in range(8)], [('hmT', t)])

        for grp in range(2):
            m_grp = A.mark()
            hacc = A.alloc('hacc%d' % grp, [128, 4, 4, 256], F32)
            if grp == 0:
                chains = [('A', 0, [0, 1], 0), ('A', 0, [1, 0], 1), ('B', 1, [2, 3], 0), ('B', 1, [3, 2], 1)]
            else:
                chains = [('S', 2, [4, 5, 6, 7], 0), ('S', 2, [7, 6, 5, 4], 1)]
            sts = []
            for ci, (nm, u, order, di) in enumerate(chains):
                sk = 'Cst_o%d_%d' % (grp, ci)
                Cst_ = A.alloc(sk, [128, 4, 258], F32)
                Cb_ = A.alloc(sk + 'b', [128, 4, 258], BF16)
                if nm == 'S':
                    cp('dve', Cst_.rearrange('p a b -> p (a b)'), Cinit[di].rearrange('p a b -> p (a b)'), ['Cinit%d' % di],
                       [(sk, h_) for h_ in range(4)])
                    cp('act', Cb_.rearrange('p a b -> p (a b)'), Cinit[di].rearrange('p a b -> p (a b)'), ['Cinit%d' % di],
                       [(sk + 'b', h_) for h_ in range(4)])
                else:
                    memset('dve', Cst_, 0.0, [(sk, h_) for h_ in range(4)])
                    memset('pool', Cb_, 0.0, [(sk + 'b', h_) for h_ in range(4)])
                sts.append((Cst_, Cb_, sk))
            seen = set()
            nstep = len(chains[0][2])
            steps = []
            for s_ in range(nstep):
                for ci, (nm, u, order, di) in enumerate(chains):
                    c = order[s_]
                    first = c not in seen
                    seen.add(c)
                    steps.append((ci, di, c, c - grp * 4, first))
            P.phase = 'P4b_g%d_steps' % grp
            nxt = chunk_prep(steps[0][1], steps[0][2])
            for k_, (ci, di, c, tl, first) in enumerate(steps):
                cur = nxt
                if k_ + 1 < len(steps):
                    nxt = chunk_prep(steps[k_ + 1][1], steps[k_ + 1][2])
                chunk_step(cur, sts[ci][0], sts[ci][1], sts[ci][2], di, c, hacc, 'hacc%d' % grp, tl, first)
            P.phase = 'P4b_g%d_post' % grp
            for tl in range(4):
                if grp == 0:
                    nm, u, order, di = chains[tl]
                    out_state(sts[tl][0], sts[tl][2], u, di)
                post_tile(hacc, 'hacc%d' % grp, tl, grp * 4 + tl)
            A.release(m_grp)
        hmT_keys = [('hmT', t) for t in range(8)]
        if 'hmT' in DBG:
            A.release(m_ml)
            hdbg2 = A.alloc('hdbg2', [128, 8, 1024], F32)
            cp('dve', hdbg2, hmT, hmT_keys, ['hdbg2'])
            dbg('hmT', hdbg2.rearrange('p a b -> p (a b)'), ['hdbg2'])
        else:
            A.release(m_ml)
        if stop_after == 'mlstm':
            return finish()

        P.phase = 'P5_merge'
        A.report('P5_merge')
        hole0 = A.off_of('Cinit0')
        x1 = A.alloc('x1', [128, 8, 1024], F32)
        m_mg = A.mark()
        mixT = A.alloc('mixT', [128, 8, 1024], BF16)
        wout = A.alloc('wout', [128, 8, 1024], BF16)
        m_mg2 = A.mark()
        womla = load_w('womla', [128, 4, 1024], I['w_o_mla'].rearrange('(kc p) n -> p kc n', p=128))
        womls = load_w('womls', [128, 8, 1024], I['w_o_mlstm'].rearrange('(kc p) n -> p kc n', p=128))
        for kc in range(KC):
            ts('dve', womls[:, kc, :], womls[:, kc, :], gmlT[:, kc:kc + 1], None, ALU.mult, None, ['womls', 'gmlT'], ['womls'])
        wbr = [A.alloc('wbr%d' % i, [128, 8, 2, 512], BF16) for i in range(2)]

        def load_wbr(sl):
            for ab in range(2):
                P.dma('pool', wbr[sl % 2][:, :, ab, :], w_in_v[:, :, O_BR + ab * 1024 + sl * 512:O_BR + ab * 1024 + (sl + 1) * 512],
                      writes=[('wbr%d' % (sl % 2), ab)])
        load_wbr(0)
        load_wbr(1)
        P.dma('pool', wout, I['w_out'].rearrange('(kc p) n -> p kc n', p=128), writes=['wout'])
        sgt = [A.alloc('sgt%d' % i, [128, 512], F32) for i in range(4)]
        n_u = 0
        for fc in range(8):
            sl = fc // 4
            wb, wbk = wbr[sl % 2], 'wbr%d' % (sl % 2)
            for half in range(2):
                pbase = (n_u % 2) * 4
                n_u += 1
                ts_ = slice(half * 512, (half + 1) * 512)
                for kc in range(4):
                    mm(BK[pbase], womla[:, kc, fc * 128:(fc + 1) * 128], attT[:, kc, ts_], kc == 0, kc == 3,
                       ['womla'] + attT_keys, [bkey(pbase)])
                for kc in range(KC):
                    mm(BK[pbase + 1], womls[:, kc, fc * 128:(fc + 1) * 128], hmT[:, kc, ts_], kc == 0, kc == KC - 1,
                       ['womls'] + hmT_keys, [bkey(pbase + 1)])
                for ab in range(2):
                    for kc in range(KC):
                        mm(BK[pbase + 2 + ab], wb[:, kc, ab, (fc % 4) * 128:(fc % 4 + 1) * 128], hTo[:, kc, ts_], kc == 0,
                           kc == KC - 1, [(wbk, ab)] + hTo_half(half), [bkey(pbase + 2 + ab)])
                sa, sak = sgt[(n_u % 2) * 2], 'sgt%d' % ((n_u % 2) * 2)
                sb_, sbk = sgt[(n_u % 2) * 2 + 1], 'sgt%d' % ((n_u % 2) * 2 + 1)
                act(sa, BK[pbase + 2], AF.Sigmoid, [bkey(pbase + 2)], [sak])
                act(sb_, BK[pbase + 3], AF.Sigmoid, [bkey(pbase + 3)], [sbk])
                tt('dve', sa, sa, BK[pbase], ALU.mult, [sak, bkey(pbase)], [sak])
                tt('dve', sb_, sb_, BK[pbase + 1], ALU.mult, [sbk, bkey(pbase + 1)], [sbk])
                tt('dve', mixT[:, fc, ts_], sa, sb_, ALU.add, [sak, sbk], [('mixT', fc, half)])
        A.release(m_mg2)
        gt1 = build_gt(1, 2, [6, 7])
        mixT_keys = [('mixT', fc, half) for fc in range(8) for half in range(2)]
        rtmp = [A.alloc('rtmp%d' % i, [128, 512], F32) for i in range(2)]
        n_r = 0
        for t in range(8):
            c = 0 if t < 4 else 1
            xt, xk = load_x(I['xo'][t * 128:(t + 1) * 128, :])
            for half in range(2):
                bk = n_r % 4
                cs_ = slice(half * 512, (half + 1) * 512)
                for kc in range(KC):
                    mm(BK[bk], mixT[:, kc, t * 128:(t + 1) * 128], wout[:, kc, cs_], kc == 0, kc == KC - 1,
                       ['wout', ('mixT', kc, t // 4)], [bkey(bk)])
                rt, rk = rtmp[n_r % 2], 'rtmp%d' % (n_r % 2)
                n_r += 1
                tt('dve', rt, BK[bk], gt1[c][:, cs_], ALU.mult, [bkey(bk), ('gt1_%d' % c, half)], [rk])
                tt('dve', x1[:, t, cs_], rt, xt[:, cs_], ALU.add, [rk, xk], [('x1', t, half)])
        A.release(m_mg)
        if 'x1' in DBG:
            dbg('x1', x1.rearrange('p a b -> p (a b)'), [('x1', t, h_) for t in range(8) for h_ in range(2)])
        if stop_after == 'merge':
            return finish()

        P.phase = 'P6_ffn'
        A.report('P6_ffn')
        A.open_hole(hole0, A.off_of('x1'))
        wfi = [A.alloc('wfi%d' % i, [128, 8, 2, 256], BF16, hole=True) for i in range(2)]
        gt2 = build_gt(2, 5, [6, 7], hole=True)
        ftmp = [A.alloc('ftmp%d' % i, [128, 512], BF16, hole=True) for i in range(2)]
        rtmp2 = [A.alloc('rtmp2_%d' % i, [128, 512], F32, hole=True) for i in range(2)]
        gT = A.alloc('gT', [128, 22, 1024], BF16)
        wfo = A.alloc('wfo', [128, 22, 1024], BF16)
        w_fi_v = I['w_ffn_in'].rearrange('(kc p) n -> p kc n', p=128)

        def load_wfi(sl):
            for au in range(2):
                P.dma('pool', wfi[sl % 2][:, :, au, :], w_fi_v[:, :, au * FFN + sl * 256:au * FFN + (sl + 1) * 256],
                      writes=[('wfi%d' % (sl % 2), au)])
        load_wfi(0)
        load_wfi(1)
        wfo_v = I['w_ffn_out'].rearrange('(kc p) n -> p kc n', p=128)
        for q_ in range(2):
            P.dma('pool', wfo[:, q_ * 11:(q_ + 1) * 11, :], wfo_v[:, q_ * 11:(q_ + 1) * 11, :], writes=[('wfo', q_)])
        norm_pipeline([((lambda t=t: (x1[:, t, :], [('x1', t, 0), ('x1', t, 1)])),
                        (A2, B2, 0 if t < 4 else 1, hTo[:, :, t * 128:(t + 1) * 128], ('hTo', t), 'A2', ('modT', 1), [(4, 6), (5, 7)]))
                       for t in range(8)])
        n_f = 0
        for sl in range(11):
            wf, wfk = wfi[sl % 2], 'wfi%d' % (sl % 2)
            for f2 in range(2):
                fc = sl * 2 + f2
                for half in range(2):
                    ba, bu = (n_f % 2) * 2, (n_f % 2) * 2 + 1
                    ts_ = slice(half * 512, (half + 1) * 512)
                    for au, bk in ((0, ba), (1, bu)):
                        for kc in range(KC):
                            mm(BK[bk], wf[:, kc, au, f2 * 128:(f2 + 1) * 128], hTo[:, kc, ts_], kc == 0, kc == KC - 1,
                               [(wfk, au)] + hTo_half(half), [bkey(bk)])
                    ft, fk = ftmp[n_f % 2], 'ftmp%d' % (n_f % 2)
                    n_f += 1
                    act(ft, BK[ba], AF.Silu, [bkey(ba)], [fk])
                    tt('dve', gT[:, fc, ts_], ft, BK[bu], ALU.mult, [fk, bkey(bu)], [('gT', fc, half)])
            if sl + 2 < 11:
                load_wfi(sl + 2)
        gfin_bc = A.alloc('gfin_bc', [128, 1024], F32)
        P.dma('sp', gfin_bc, I['gfin'].partition_broadcast(128), writes=['gfin_bc'])
        ybuf = xin
        f_ss = [A.alloc('f_ss%d' % i, [128, 4], F32, hole=True) for i in range(2)]
        fjunk = nrm_xn[0]
        n_r = 0
        for t in range(8):
            c = 0 if t < 4 else 1
            for half in range(2):
                bk = 4 + n_r % 4
                cs_ = slice(half * 512, (half + 1) * 512)
                for fc in range(22):
                    mm(BK[bk], gT[:, fc, t * 128:(t + 1) * 128], wfo[:, fc, cs_], fc == 0, fc == 21,
                       [('wfo', fc // 11), ('gT', fc, t // 4)], [bkey(bk)])
                rt, rk = rtmp2[n_r % 2], 'rtmp2_%d' % (n_r % 2)
                n_r += 1
                tt('dve', rt, BK[bk], gt2[c][:, cs_], ALU.mult, [bkey(bk), ('gt2_%d' % c, half)], [rk])
                tt('dve', x1[:, t, cs_], rt, x1[:, t, cs_], ALU.add, [rk, ('x1', t, half)], [('x1', t, half)])
            i = t % 2
            ss, sk_ = f_ss[i], 'f_ss%d' % i
            xk2 = [('x1', t, 0), ('x1', t, 1)]
            act(fjunk, x1[:, t, :], AF.Square, xk2, ['nrm_xn0', (sk_, 0)], accum=ss[:, 0:1])
            act(ss[:, 1:2], ss[:, 0:1], AF.Sqrt, [(sk_, 0), 'eps_t'], [(sk_, 1)], bias=eps_t, scale=1.0 / D)
            recip(ss[:, 2:3], ss[:, 1:2], [(sk_, 1)], [(sk_, 2)])
            yb, yk = ybuf[i], 'xin%d' % i
            stt(yb, x1[:, t, :], ss[:, 2:3], gfin_bc, ALU.mult, ALU.mult, xk2 + [(sk_, 2), 'gfin_bc'], [yk])
            P.dma('sp', O['y'][t * 128:(t + 1) * 128, :], yb, reads=[yk], writes=[('y', t)])
            final_keys.append(('y', t))


        return finish()


def _rope_tables():
    half = 16
    inv = 10000.0 ** (-np.arange(0, half, 2, dtype=np.float64) / half)
    t = np.arange(2048)
    r = (t // 64).astype(np.float64)
    col = (t % 64).astype(np.float64)
    ang_r = r[None, :] * inv[:, None]
    ang_c = col[None, :] * inv[:, None]
    ang = np.concatenate([ang_r, ang_r, ang_c, ang_c], 0)
    return np.cos(ang).astype(np.float32), np.sin(ang).astype(np.float32)


def make_in_maps(inp):
    f = lambda a: np.ascontiguousarray(a, dtype=np.float32)
    cosT, sinT = _rope_tables()
    shared = {
        'b_modT': f(inp['b_mod'][0].reshape(48, 128).T),
        'gmixT': f(inp['g_norm_mix'][0].reshape(8, 128).T),
        'gffnT': f(inp['g_norm_ffn'][0].reshape(8, 128).T),
        'gqT': f(inp['g_q_norm'][0].reshape(3, 128).T),
        'gmlT': f(inp['g_mlstm_norm'][0].reshape(8, 128).T),
        'gkv': f(inp['g_kv_norm'][0].reshape(1, 256)),
        'gfin': f(inp['g_final'].reshape(1, 1024)),
        'bgate': f(inp['b_gates'][0].reshape(16, 1)),
        'w_mod': f(inp['w_mod'][0]), 'w_in': f(inp['w_in'][0]), 'w_uq': f(inp['w_uq'][0]),
        'w_ukv': f(inp['w_ukv'][0]), 'w_o_mla': f(inp['w_o_mla'][0]), 'w_o_mlstm': f(inp['w_o_mlstm'][0]),
        'w_out': f(inp['w_out'][0]), 'w_ffn_in': f(inp['w_ffn_in'][0]), 'w_ffn_out': f(inp['w_ffn_out'][0]),
    }
    maps = []
    for core in range(8):
        b, j = core // 4, core % 4
        m = dict(shared)
        xo = np.concatenate([inp['x_prompt'][2 * core], inp['x_prompt'][2 * core + 1],
                             inp['x_sample'][b, j * 512:(j + 1) * 512]], 0)
        m['xo'] = f(xo)
        chunks = list(range(0, 4 * j)) + list(range(15, 4 * j + 3, -1))
        assert len(chunks) == 12
        tok = np.concatenate([np.arange(c * 128, (c + 1) * 128) for c in chunks])
        m['xs'] = f(inp['x_sample'][b][tok])
        m['cos_s'] = f(cosT[:, tok])
        m['sin_s'] = f(sinT[:, tok])
        dsl = np.zeros((4, 12), np.float32)
        dsl[:, :4 * j] = 1.0
        m['dsl'] = dsl
        m['dmask'] = f(np.repeat(dsl, 128, axis=1))
        cond = np.stack([inp['c_ctx'], inp['c'][b]], 0)
        m['condT'] = f(cond.reshape(2, 8, 128).transpose(2, 1, 0).reshape(128, 16))
        m['cckv'] = f(inp['cache_ckv'][b, 0])
        m['ckro'] = f(inp['cache_krope'][b, 0])
        m['stC'] = f(inp['state_C'][b, 0].reshape(8, 256, 128))
        m['stnT'] = f(inp['state_n'][b, 0].reshape(8, 128).T)
        m['stm4'] = f(inp['state_m'][b, 0].T)
        s = np.zeros((128, 4), np.float32)
        s[:, j] = 1.0
        m['sel'] = s
        m['cos_o'] = f(cosT[:, j * 512:(j + 1) * 512])
        m['sin_o'] = f(sinT[:, j * 512:(j + 1) * 512])
        maps.append(m)
    return maps


_NC_CACHE = {}


def kernel(**inp):
    inp = {k: np.asarray(v) for k, v in inp.items()}
    if 'nc' not in _NC_CACHE:
        _NC_CACHE['nc'] = build_program()
    nc = _NC_CACHE['nc']
    maps = make_in_maps(inp)
    res = run_bass_kernel_spmd(nc, maps, core_ids=list(range(8)))
    R = res.results
    y_prompt = np.zeros((16, 256, 1024), np.float32)
    y_sample = np.zeros((2, 2048, 1024), np.float32)
    new_ckv = np.zeros((16, 1, 256, 256), np.float32)
    new_krope = np.zeros((16, 1, 256, 32), np.float32)
    new_C = np.zeros((16, 1, 2, 4, 256, 128), np.float32)
    new_n = np.zeros((16, 1, 2, 4, 128), np.float32)
    new_m = np.zeros((16, 1, 2, 4), np.float32)
    for core in range(8):
        r = R[core]
        b, j = core // 4, core % 4
        y = r['y']
        y_prompt[2 * core] = y[0:256]
        y_prompt[2 * core + 1] = y[256:512]
        y_sample[b, j * 512:(j + 1) * 512] = y[512:1024]
        new_ckv[2 * core, 0] = r['ockv'][0:256]
        new_ckv[2 * core + 1, 0] = r['ockv'][256:512]
        new_krope[2 * core, 0] = r['okr'][0:256]
        new_krope[2 * core + 1, 0] = r['okr'][256:512]
        oC = r['oC'].reshape(2, 2, 4, 256, 128)
        on = r['on'].reshape(2, 2, 4, 128)
        om = r['om'].T.reshape(2, 2, 4)
        for u in range(2):
            new_C[2 * core + u, 0] = oC[u]
            new_n[2 * core + u, 0] = on[u]
            new_m[2 * core + u, 0] = om[u]
    return (y_prompt, y_sample, new_ckv, new_krope, new_C, new_n, new_m)
```
